# Optimizing a Trainium2 kernel written in Bass

```python
import math
import jax, jax.numpy as jnp
from jax import lax
import numpy as np

D_MODEL = 1024
BATCH = 8
SEQ = 2048
DEPTH = 2
DEC_BATCH = 128
DEC_SEQ = 8
PAST_LEN = 16384
PAGE_SIZE = 128

N_MIXERS = 2
N_CONV_LAYERS = (DEPTH + 1) // 2
N_SSM_LAYERS = DEPTH // 2
CONV_WIDTH = 31
GROUP_SIZE = 16
N_GROUPS = D_MODEL // GROUP_SIZE
STATE_DIM = 64
D_FF = int(math.ceil(8 * D_MODEL / 3 / 256)) * 256
EPS = 1e-6
DT_MIN = 1e-3
DT_MAX = 1e-1

kernel_name = "hybrid_conformer_conv_s5_decoder_step"


def rms_norm(x, g):
    xf = x.astype(jnp.float32)
    y = xf * lax.rsqrt(jnp.mean(xf * xf, axis=-1, keepdims=True) + EPS)
    return (y * g.astype(jnp.float32)).astype(x.dtype)


def layer_norm(x, g, b):
    xf = x.astype(jnp.float32)
    mu = jnp.mean(xf, axis=-1, keepdims=True)
    xc = xf - mu
    var = jnp.mean(xc * xc, axis=-1, keepdims=True)
    y = xc * lax.rsqrt(var + EPS) * g.astype(jnp.float32) + b.astype(jnp.float32)
    return y.astype(x.dtype)


def swiglu(h, w_gate, w_up, w_down):
    return (jax.nn.silu(h @ w_gate) * (h @ w_up)) @ w_down


def conv_module(h, buf, w_pw1, b_pw1, w_dw, b_dw, ln_g, ln_b, w_pw2):
    a = h @ w_pw1 + b_pw1
    v = a[..., :D_MODEL] * jax.nn.sigmoid(a[..., D_MODEL:])
    full = jnp.concatenate([buf.astype(v.dtype), v], axis=1)
    y = lax.conv_general_dilated(
        full, w_dw[:, None, :].astype(v.dtype), window_strides=(1,), padding='VALID',
        dimension_numbers=('NWC', 'WIO', 'NWC'), feature_group_count=D_MODEL) + b_dw
    y = jax.nn.silu(layer_norm(y, ln_g, ln_b))
    return y @ w_pw2, full[:, -(CONV_WIDTH - 1):]


def _scan_combine(e1, e2):
    a1r, a1i, b1r, b1i = e1
    a2r, a2i, b2r, b2i = e2
    return (a1r * a2r - a1i * a2i,
            a1r * a2i + a1i * a2r,
            a2r * b1r - a2i * b1i + b2r,
            a2r * b1i + a2i * b1r + b2i)


def ssm_module(h, s_re, s_im, lam_re, lam_im, log_dt, b_re, b_im, c_re, c_im, d_skip, w_glu, b_glu):
    f32 = jnp.float32
    n, l, _ = h.shape
    hf = h.astype(f32)
    u = hf.reshape(n, l, N_GROUPS, GROUP_SIZE)
    dt = jnp.exp(log_dt.astype(f32))[:, None]
    lr = lam_re.astype(f32)
    li = lam_im.astype(f32)
    mag = jnp.exp(lr * dt)
    ang = li * dt
    ab_re = mag * jnp.cos(ang)
    ab_im = mag * jnp.sin(ang)
    nr = ab_re - 1.0
    den = lr * lr + li * li
    cf_re = (nr * lr + ab_im * li) / den
    cf_im = (ab_im * lr - nr * li) / den
    bu_re = jnp.einsum('nlgc,gpc->nlgp', u, b_re.astype(f32))
    bu_im = jnp.einsum('nlgc,gpc->nlgp', u, b_im.astype(f32))
    x_re = cf_re * bu_re - cf_im * bu_im
    x_im = cf_re * bu_im + cf_im * bu_re
    h0_re = s_re.astype(f32)
    h0_im = s_im.astype(f32)
    x_re = x_re.at[:, 0].add(ab_re * h0_re - ab_im * h0_im)
    x_im = x_im.at[:, 0].add(ab_re * h0_im + ab_im * h0_re)
    a_re = jnp.broadcast_to(ab_re, (1, l, N_GROUPS, STATE_DIM))
    a_im = jnp.broadcast_to(ab_im, (1, l, N_GROUPS, STATE_DIM))
    _, _, st_re, st_im = lax.associative_scan(_scan_combine, (a_re, a_im, x_re, x_im), axis=1)
    y = (jnp.einsum('nlgp,gcp->nlgc', st_re, c_re.astype(f32))
         - jnp.einsum('nlgp,gcp->nlgc', st_im, c_im.astype(f32)))
    y = y.reshape(n, l, D_MODEL) + d_skip.astype(f32) * hf
    y = jax.nn.gelu(y).astype(h.dtype)
    z = y @ w_glu + b_glu
    out = z[..., :D_MODEL] * jax.nn.sigmoid(z[..., D_MODEL:])
    return out, st_re[:, -1], st_im[:, -1]


def setup_inputs(seed: int = 0) -> dict:
    key = jax.random.key(seed)
    ks = jax.random.split(key, 32)
    f32 = jnp.float32
    nrm = lambda k, s, sc: jax.random.normal(k, s, f32) * sc
    inv = lambda n: 1.0 / math.sqrt(n)
    n_idx = jnp.arange(STATE_DIM, dtype=f32)
    lam_re = -0.5 + nrm(ks[15], (N_SSM_LAYERS, N_GROUPS, STATE_DIM), 0.01)
    lam_im = math.pi * n_idx + nrm(ks[16], (N_SSM_LAYERS, N_GROUPS, STATE_DIM), 0.01)
    log_dt = jax.random.uniform(ks[17], (N_SSM_LAYERS, N_GROUPS), f32,
                                math.log(DT_MIN), math.log(DT_MAX))
    return {
        "x_prompt": nrm(ks[0], (BATCH, SEQ, D_MODEL), 1.0),
        "x_sample": nrm(ks[1], (DEC_BATCH, DEC_SEQ, D_MODEL), 1.0),
        "cache_conv": nrm(ks[2], (N_CONV_LAYERS, DEC_BATCH, CONV_WIDTH - 1, D_MODEL), 0.5),
        "state_ssm_re": nrm(ks[3], (N_SSM_LAYERS, DEC_BATCH, N_GROUPS, STATE_DIM), 0.1),
        "state_ssm_im": nrm(ks[4], (N_SSM_LAYERS, DEC_BATCH, N_GROUPS, STATE_DIM), 0.1),
        "norm_mix": 1.0 + nrm(ks[5], (DEPTH, D_MODEL), 0.02),
        "norm_ffn": 1.0 + nrm(ks[6], (DEPTH, D_MODEL), 0.02),
        "norm_final": 1.0 + nrm(ks[7], (D_MODEL,), 0.02),
        "conv_w_pw1": nrm(ks[8], (N_CONV_LAYERS, D_MODEL, 2 * D_MODEL), inv(D_MODEL)),
        "conv_b_pw1": nrm(ks[9], (N_CONV_LAYERS, 2 * D_MODEL), 0.01),
        "conv_w_dw": nrm(ks[10], (N_CONV_LAYERS, CONV_WIDTH, D_MODEL), inv(CONV_WIDTH)),
        "conv_b_dw": nrm(ks[11], (N_CONV_LAYERS, D_MODEL), 0.01),
        "conv_ln_g": 1.0 + nrm(ks[12], (N_CONV_LAYERS, D_MODEL), 0.02),
        "conv_ln_b": nrm(ks[13], (N_CONV_LAYERS, D_MODEL), 0.01),
        "conv_w_pw2": nrm(ks[14], (N_CONV_LAYERS, D_MODEL, D_MODEL), inv(D_MODEL)),
        "ssm_lam_re": lam_re,
        "ssm_lam_im": lam_im,
        "ssm_log_dt": log_dt,
        "ssm_b_re": nrm(ks[18], (N_SSM_LAYERS, N_GROUPS, STATE_DIM, GROUP_SIZE), inv(2 * GROUP_SIZE)),
        "ssm_b_im": nrm(ks[19], (N_SSM_LAYERS, N_GROUPS, STATE_DIM, GROUP_SIZE), inv(2 * GROUP_SIZE)),
        "ssm_c_re": nrm(ks[20], (N_SSM_LAYERS, N_GROUPS, GROUP_SIZE, STATE_DIM), inv(STATE_DIM)),
        "ssm_c_im": nrm(ks[21], (N_SSM_LAYERS, N_GROUPS, GROUP_SIZE, STATE_DIM), inv(STATE_DIM)),
        "ssm_d": 1.0 + nrm(ks[22], (N_SSM_LAYERS, D_MODEL), 0.1),
        "ssm_w_glu": nrm(ks[23], (N_SSM_LAYERS, D_MODEL, 2 * D_MODEL), inv(D_MODEL)),
        "ssm_b_glu": nrm(ks[24], (N_SSM_LAYERS, 2 * D_MODEL), 0.01),
        "ffn_w_gate": nrm(ks[25], (DEPTH, D_MODEL, D_FF), inv(D_MODEL)),
        "ffn_w_up": nrm(ks[26], (DEPTH, D_MODEL, D_FF), inv(D_MODEL)),
        "ffn_w_down": nrm(ks[27], (DEPTH, D_FF, D_MODEL), inv(D_FF)),
    }


def reference(x_prompt, x_sample, cache_conv, state_ssm_re, state_ssm_im,
              norm_mix, norm_ffn, norm_final,
              conv_w_pw1, conv_b_pw1, conv_w_dw, conv_b_dw, conv_ln_g, conv_ln_b, conv_w_pw2,
              ssm_lam_re, ssm_lam_im, ssm_log_dt, ssm_b_re, ssm_b_im, ssm_c_re, ssm_c_im,
              ssm_d, ssm_w_glu, ssm_b_glu,
              ffn_w_gate, ffn_w_up, ffn_w_down):
    yp, ys = x_prompt, x_sample
    n_p = x_prompt.shape[0]
    zero_buf = jnp.zeros((n_p, CONV_WIDTH - 1, D_MODEL), x_prompt.dtype)
    zero_state = jnp.zeros((n_p, N_GROUPS, STATE_DIM), jnp.float32)
    conv_new_p, conv_new_s = [], []
    ssm_p_re, ssm_p_im, ssm_s_re, ssm_s_im = [], [], [], []
    for i in range(DEPTH):
        j = i // N_MIXERS
        hp = rms_norm(yp, norm_mix[i])
        hs = rms_norm(ys, norm_mix[i])
        if i % N_MIXERS == 0:
            cp = (conv_w_pw1[j], conv_b_pw1[j], conv_w_dw[j], conv_b_dw[j],
                  conv_ln_g[j], conv_ln_b[j], conv_w_pw2[j])
            dp, bp = conv_module(hp, zero_buf, *cp)
            ds, bs = conv_module(hs, cache_conv[j], *cp)
            conv_new_p.append(bp)
            conv_new_s.append(bs)
        else:
            sp = (ssm_lam_re[j], ssm_lam_im[j], ssm_log_dt[j], ssm_b_re[j], ssm_b_im[j],
                  ssm_c_re[j], ssm_c_im[j], ssm_d[j], ssm_w_glu[j], ssm_b_glu[j])
            dp, pr, pi = ssm_module(hp, zero_state, zero_state, *sp)
            ds, sr, si = ssm_module(hs, state_ssm_re[j], state_ssm_im[j], *sp)
            ssm_p_re.append(pr)
            ssm_p_im.append(pi)
            ssm_s_re.append(sr)
            ssm_s_im.append(si)
        yp = yp + dp
        ys = ys + ds
        yp = yp + swiglu(rms_norm(yp, norm_ffn[i]), ffn_w_gate[i], ffn_w_up[i], ffn_w_down[i])
        ys = ys + swiglu(rms_norm(ys, norm_ffn[i]), ffn_w_gate[i], ffn_w_up[i], ffn_w_down[i])
    y_prompt = rms_norm(yp, norm_final)
    y_sample = rms_norm(ys, norm_final)
    return (y_prompt, y_sample,
            jnp.stack(conv_new_p), jnp.stack(conv_new_s),
            jnp.stack(ssm_p_re), jnp.stack(ssm_p_im),
            jnp.stack(ssm_s_re), jnp.stack(ssm_s_im))
```

```python
import math
import numpy as np
import concourse.bass as bass
import concourse.mybir as mybir
from concourse.bass_utils import run_bass_kernel_spmd

F32 = mybir.dt.float32
BF16 = mybir.dt.bfloat16
I32 = mybir.dt.int32
AF = mybir.ActivationFunctionType
ALU = mybir.AluOpType

D = 1024
NCT = 8
DFF = 2816
NFC = 22
SEQ = 2048
NSEQ_S = 16
TS = 8
KW = 31
EPS = 1e-6
T = 8
NG = 64
NPAIR = 32

_ESZ = {F32: 4, BF16: 2, I32: 4}


def _esz(dt):
    return _ESZ[dt]


class Op:
    __slots__ = ("eng", "fn", "reads", "writes", "dma", "deps", "sig", "sigval", "dsem", "dval", "prewait")

    def __init__(self, eng, fn, reads, writes, dma):
        self.eng = eng
        self.fn = fn
        self.reads = reads
        self.writes = writes
        self.dma = dma
        self.deps = set()
        self.sig = False
        self.sigval = 0
        self.dsem = None
        self.dval = 0
        self.prewait = None


def _acc(ap):
    t = ap.tensor
    name = t.name
    pat = ap.ap
    esz = _esz(ap.dtype)
    off = ap.offset
    sp = str(ap.space)
    if sp not in ("SB", "PSUM"):
        lo = off
        hi = off
        for (s, c) in pat:
            if s >= 0:
                hi += (c - 1) * s
            else:
                lo += (c - 1) * s
        return (name, 0, 1, lo * esz, (hi + 1) * esz)
    ps = pat[0][0]
    pc = pat[0][1]
    if ps == 0:
        p0 = 0
        f0 = off
        pc = 128
    else:
        p0 = off // ps
        f0 = off % ps
    lo = f0
    hi = f0
    for (s, c) in pat[1:]:
        if s >= 0:
            hi += (c - 1) * s
        else:
            lo += (c - 1) * s
    if sp == "PSUM":
        return (name, (p0 // 32) * 32, ((p0 + pc + 31) // 32) * 32, 0, 1 << 20)
    return (name, p0, p0 + pc, lo * esz, (hi + 1) * esz)


def _acc_multi(ap):
    base = _acc(ap)
    sp = str(ap.space)
    if sp != "SB":
        return [base]
    pat = ap.ap
    if len(pat) < 3 or pat[0][0] == 0:
        return [base]
    esz = _esz(ap.dtype)
    f0 = ap.offset % pat[0][0]
    dims = sorted([(s, c) for (s, c) in pat[1:] if c > 1], key=lambda d: -abs(d[0]))
    name, p0, p1, _, _ = base

    def extent(ds):
        lo = hi = 0
        for (s, c) in ds:
            if s >= 0:
                hi += (c - 1) * s
            else:
                lo += (c - 1) * s
        return lo, hi

    out = []

    def rec(ds, off, budget):
        if ds:
            (s, c) = ds[0]
            lo, hi = extent(ds[1:])
            inner = hi - lo + 1
            if c <= budget and abs(s) > inner:
                for i in range(c):
                    rec(ds[1:], off + i * s, budget // c)
                return
        lo, hi = extent(ds)
        out.append((name, p0, p1, (off + lo) * esz, (off + hi + 1) * esz))

    rec(dims, f0, 64)
    return out


class Prog:
    ENGS = ("pe", "act", "dve", "pool", "sp")

    def __init__(self):
        self.ops = []
        self.recs = {}

    def _track(self, idx, op):
        for (aps, is_w0) in ((op.reads, False), (op.writes, True)):
            for (ap, name, p0, p1, b0, b1) in [(ap_,) + iv for ap_ in aps for iv in _acc_multi(ap_)]:
                ap_space = str(ap.space)
                is_w = is_w0 or (ap_space == "PSUM")
                lst = self.recs.setdefault(name, [])
                keep = []
                for r in lst:
                    ov = not (r[1] <= p0 or p1 <= r[0] or r[3] <= b0 or b1 <= r[2])
                    if ov and (is_w or r[5]):
                        if r[4] != idx:
                            op.deps.add(r[4])
                        if is_w and r[0] >= p0 and r[1] <= p1 and r[2] >= b0 and r[3] <= b1 and r[4] != idx:
                            continue
                    keep.append(r)
                if not is_w and not op.dma:
                    keep = [r for r in keep if not ((not r[5]) and r[0] == p0 and r[1] == p1 and r[2] == b0
                                                    and r[3] == b1 and (not self.ops[r[4]].dma)
                                                    and self.ops[r[4]].eng == op.eng)]
                keep.append([p0, p1, b0, b1, idx, is_w])
                self.recs[name] = keep

    def add(self, eng, fn, reads, writes, dma=False):
        op = Op(eng, fn, list(reads), list(writes), dma)
        idx = len(self.ops)
        self.ops.append(op)
        self._track(idx, op)
        return op

    def emit(self, nc, nsem_dma=12):
        ops = self.ops
        for op in ops:
            nd = set()
            for d in op.deps:
                p = ops[d]
                if (not p.dma) and p.eng == "pe" and op.eng == "pe" and not op.dma:
                    continue
                nd.add(d)
            latest = {}
            nd2 = set()
            for d in nd:
                p = ops[d]
                if p.dma:
                    nd2.add(d)
                else:
                    if p.eng not in latest or d > latest[p.eng]:
                        latest[p.eng] = d
            nd2.update(latest.values())
            op.deps = nd2
            for d in nd2:
                if not ops[d].dma:
                    ops[d].sig = True
        cnt = {e: 0 for e in self.ENGS}
        ndma = 0
        dma_last = {}
        npool_dma = 0
        for op in ops:
            if op.dma and op.eng == "pool":
                op.dsem = ("p", npool_dma)
                op.dval = 16
                npool_dma += 1
            elif op.dma:
                s = ndma % nsem_dma
                op.dsem = s
                op.dval = 16 * (ndma // nsem_dma + 1)
                if s in dma_last:
                    op.prewait = dma_last[s]
                dma_last[s] = (s, op.dval)
                ndma += 1
            elif op.sig:
                cnt[op.eng] += 1
                op.sigval = cnt[op.eng]
        SEMCAP = 1000
        nsem_e = {e: max(1, (cnt[e] + SEMCAP - 1) // SEMCAP) for e in self.ENGS}
        print("sem counts", cnt, "ndma", ndma, "npool_dma", npool_dma, "nops", len(ops))
        import contextlib
        with contextlib.ExitStack() as st:
            esem = {e: [st.enter_context(nc.semaphore("s_%s%d" % (e, i))) for i in range(nsem_e[e])] for e in self.ENGS}
            dsem = {i: st.enter_context(nc.semaphore("d%d" % i)) for i in range(nsem_dma)}
            for i in range(npool_dma):
                dsem[("p", i)] = st.enter_context(nc.semaphore("q%d" % i))
            block = st.enter_context(nc.Block())
            final_d = dict(dma_last)

            def body(ename):
                def run(e):
                    waited = {}

                    def wait(sem, key, val):
                        if waited.get(key, 0) >= val:
                            return
                        waited[key] = val
                        e.wait_ge(sem, val)

                    for op in ops:
                        if op.eng != ename:
                            continue
                        for d in sorted(op.deps):
                            p = ops[d]
                            if p.dma:
                                wait(dsem[p.dsem], ("d", p.dsem), p.dval)
                            else:
                                si_ = (p.sigval - 1) // SEMCAP
                                wait(esem[p.eng][si_], ("e", p.eng, si_), (p.sigval - 1) % SEMCAP + 1)
                        if op.dma and op.prewait is not None:
                            wait(dsem[op.prewait[0]], ("d", op.prewait[0]), op.prewait[1])
                        ins = op.fn(e)
                        if op.dma:
                            ins.then_inc(dsem[op.dsem], 16)
                        elif op.sig:
                            ins.then_inc(esem[op.eng][(op.sigval - 1) // SEMCAP], 1)
                    if ename == "sp":
                        for s, (si, v) in final_d.items():
                            wait(dsem[si], ("d", si), v)
                        for i in range(npool_dma):
                            wait(dsem[("p", i)], ("d", ("p", i)), 16)
                        for en in self.ENGS:
                            if en != "sp" and cnt[en] > 0:
                                si_ = (cnt[en] - 1) // SEMCAP
                                wait(esem[en][si_], ("e", en, si_), (cnt[en] - 1) % SEMCAP + 1)
                return run

            block.tensor(body("pe"))
            block.scalar(body("act"))
            block.vector(body("dve"))
            block.gpsimd(body("pool"))
            block.sync(body("sp"))


class Builder:
    def __init__(self, nc, stages):
        self.nc = nc
        self.P = Prog()
        self.stages = stages
        self.psi = 0

    def mm(self, out, lhsT, rhs, start=True, stop=True, tp=None):
        kw = {}
        if tp is not None:
            kw["tile_position"] = tp
        self.P.add("pe", lambda e: e.matmul(out, lhsT, rhs, start=start, stop=stop, **kw), [lhsT, rhs], [out])

    def tr(self, out, in_, ident, tp=None):
        kw = {}
        if tp is not None:
            kw["tile_position"] = tp
        self.P.add("pe", lambda e: e.transpose(out, in_, ident, **kw), [in_, ident], [out])

    def act(self, out, in_, func, bias=None, scale=None, eng="act"):
        kw = {}
        rd = [in_]
        if bias is not None:
            kw["bias"] = bias
            if not isinstance(bias, (int, float)):
                rd.append(bias)
        if scale is not None:
            kw["scale"] = scale
            if not isinstance(scale, (int, float)):
                rd.append(scale)
        self.P.add("act", lambda e: e.activation(out, in_, func, **kw), rd, [out])

    def tt(self, out, a, b, op, eng="dve"):
        self.P.add(eng, lambda e: e.tensor_tensor(out, a, b, op), [a, b], [out])

    def ts(self, out, a, s1, s2, op0, op1=None, eng="dve"):
        rd = [a]
        for s in (s1, s2):
            if s is not None and not isinstance(s, (int, float)):
                rd.append(s)
        if op1 is None:
            self.P.add(eng, lambda e: e.tensor_scalar(out, a, s1, None, op0), rd, [out])
        else:
            self.P.add(eng, lambda e: e.tensor_scalar(out, a, s1, s2, op0, op1), rd, [out])

    def stt(self, out, a, s, b, op0, op1, eng="dve"):
        rd = [a, b]
        if not isinstance(s, (int, float)):
            rd.append(s)
        self.P.add(eng, lambda e: e.scalar_tensor_tensor(out, a, s, b, op0, op1), rd, [out])

    def cp(self, out, in_, eng="dve"):
        if eng == "act":
            self.P.add("act", lambda e: e.activation(out, in_, AF.Copy), [in_], [out])
        else:
            self.P.add(eng, lambda e: e.tensor_copy(out, in_), [in_], [out])

    def memset(self, ap, v, eng="pool"):
        self.P.add(eng, lambda e: e.memset(ap, v), [], [ap])

    def recip(self, out, in_):
        self.P.add("dve", lambda e: e.reciprocal(out, in_), [in_], [out])

    def dma(self, out, in_, eng="sp"):
        self.P.add(eng, lambda e: e.dma_start(out=out, in_=in_), [in_], [out], dma=True)

    def ps(self):
        p = self.psum[self.psi % 8]
        self.psi += 1
        return p


class Arena:
    def __init__(self, handle, base, limit):
        self.h = handle
        self.off = base
        self.limit = limit

    def alloc(self, nbytes):
        nbytes = (nbytes + 63) // 64 * 64
        o = self.off
        self.off += nbytes
        assert self.off <= self.limit, (self.off, self.limit)
        return o


ARENA_BYTES = 206848
NTMAX = 1152
NV = 44


def build(stages=("conv", "ffn0", "ssm", "ffn1"), final_norm=True):
    nc = bass.Bass("TRN2", target_bir_lowering=False)
    B = Builder(nc, stages)

    def din(name, shape, dt=F32):
        return nc.dram_tensor(name, shape, dt, kind="ExternalInput").ap()

    def dout(name, shape, dt=F32):
        return nc.dram_tensor(name, shape, dt, kind="ExternalOutput").ap()

    xp = din("xp", [SEQ, D]); xs = din("xs", [128, D]); cc = din("cc", [480, D])
    sre = din("sre", [NSEQ_S, 4096]); sim = din("sim", [NSEQ_S, 4096])
    ident_d = din("ident", [128, 128])
    norm_mix = din("norm_mix", [2, D]); norm_ffn = din("norm_ffn", [2, D]); norm_final = din("norm_final", [1, D])
    w_pw1 = din("w_pw1", [D, 2 * D]); b_pw1 = din("b_pw1", [2, D]); w_dw = din("w_dw", [KW, D])
    b_dw = din("b_dw", [1, D]); ln_g = din("ln_g", [1, D]); ln_b = din("ln_b", [1, D]); w_pw2 = din("w_pw2", [D, D])
    lam_re = din("lam_re", [NG, 64]); lam_im = din("lam_im", [NG, 64]); log_dt = din("log_dt", [NG, 1])
    b_re = din("b_re", [NG * 64, 16]); b_im = din("b_im", [NG * 64, 16])
    c_re = din("c_re", [NG * 16, 64]); c_im = din("c_im", [NG * 16, 64])
    ssm_d = din("ssm_d", [1, D]); w_glu = din("w_glu", [D, 2 * D]); b_glu = din("b_glu", [2, D])
    wg = din("wg", [2, D, DFF]); wu = din("wu", [2, D, DFF]); wd = din("wd", [2, DFF, D])

    yp = dout("yp", [SEQ, D]); ys = dout("ys", [128, D])
    convp = dout("convp", [KW - 1, D]); convs = dout("convs", [480, D])
    sspr = dout("sspr", [NG, 64]); sspi = dout("sspi", [NG, 64])
    sssr = dout("sssr", [NSEQ_S, 4096]); sssi = dout("sssi", [NSEQ_S, 4096])

    import contextlib
    st = contextlib.ExitStack()
    arena_h = st.enter_context(nc.sbuf_tensor("arena", [128, ARENA_BYTES // 4], F32))
    B.psum = [st.enter_context(nc.psum_tensor("ps%d" % i, [128, 512], F32)) for i in range(8)]

    def V(off, shape, dt=F32):
        n = 1
        for s in shape:
            n *= s
        nb = n * _esz(dt)
        assert off % 4 == 0 and nb % 4 == 0
        ap = arena_h[:, off // 4: off // 4 + nb // 4]
        if dt != F32:
            ap = ap.bitcast(dt)
        if len(shape) == 2:
            ap = ap.rearrange("p (a b) -> p a b", b=shape[1])
        elif len(shape) == 3:
            ap = ap.rearrange("p (a b c) -> p a b c", b=shape[1], c=shape[2])
        elif len(shape) == 4:
            ap = ap.rearrange("p (a b c d) -> p a b c d", b=shape[1], c=shape[2], d=shape[3])
        return ap

    A = Arena(arena_h, 0, ARENA_BYTES)
    o_x = A.alloc(NCT * NTMAX * 4)
    o_h = A.alloc(NCT * NTMAX * 2)
    o_rs = A.alloc(NTMAX * 4)
    o_idf = A.alloc(128 * 4)
    o_idb = A.alloc(128 * 2)
    o_one = A.alloc(128 * 2)
    o_vec = A.alloc(NCT * NV * 4)
    o_small = A.alloc(2048)
    GEN0 = A.off
    X = V(o_x, [NCT, NTMAX]); H = V(o_h, [NCT, NTMAX], BF16); RS = V(o_rs, [NTMAX])
    IDF = V(o_idf, [128]); IDB = V(o_idb, [128], BF16); ONE = V(o_one, [128], BF16)
    VEC = V(o_vec, [NCT, NV])

    def vec(v, ct):
        return VEC[:, ct, v:v + 1]

    B.dma(IDF, ident_d)
    B.cp(IDB, IDF, eng="dve")
    B.memset(ONE, 1.0, eng="dve")
    G = Arena(arena_h, GEN0, ARENA_BYTES)
    o_vt = G.alloc(D * 4)
    VT = V(o_vt, [D])
    rows = [(norm_mix, 0, 2), (norm_ffn, 2, 2), (norm_final, 4, 1), (b_pw1, 5, 2), (b_dw, 7, 1), (ln_g, 8, 1),
            (ln_b, 9, 1), (ssm_d, 10, 1), (b_glu, 11, 2), (w_dw, 13, KW)]
    for (src, r0, n) in rows:
        B.dma(VT[r0:r0 + n, :], src)
    for ct in range(NCT):
        p = B.ps()
        B.tr(p[:, 0:NV], VT[0:NV, ct * 128:(ct + 1) * 128], IDF[0:NV, 0:NV])
        B.cp(VEC[:, ct, :], p[:, 0:NV], eng="dve")
    B.ts(VEC[:, :, 0:5], VEC[:, :, 0:5], math.sqrt(D), None, ALU.mult)

    def subs_of(nt):
        out = []
        c = 0
        while c < nt:
            n = min(512, nt - c)
            out.append((c, n))
            c += n
        return out

    def subs_tok(nt):
        if nt == 1152:
            return [(0, 384), (384, 384), (768, 384)]
        return subs_of(nt)

    def load_x(src_rows, c0, ntok, stage_off):
        ntile = ntok // 128
        for ti in range(ntile):
            so = stage_off[ti % 2]
            XIN = V(so, [D])
            B.dma(XIN, src_rows[ti * 128:(ti + 1) * 128, :])
            for half in range(2):
                p = B.ps()
                for j in range(4):
                    ct = half * 4 + j
                    B.tr(p[:, j * 128:(j + 1) * 128], XIN[:, ct * 128:(ct + 1) * 128], IDF)
                B.cp(X[:, half * 4:half * 4 + 4, c0 + ti * 128: c0 + (ti + 1) * 128],
                     p.rearrange("p (a b) -> p a b", b=128), eng="act")

    prenormed = [False]
    STAGE_ORDER = [s_ for s_ in ("conv", "ffn0", "ssm", "ffn1") if s_ in stages]
    NEXT_GIDX = {"ffn0": 2, "ssm": 1, "ffn1": 3}

    def rmsnorm_sub(c0, n, gidx, sq_off):
        SQ = V(sq_off, [NCT, 512], BF16)
        for ct in range(NCT):
            B.act(SQ[:, ct, 0:n], X[:, ct, c0:c0 + n], AF.Square)
        p = B.ps()
        for ct in range(NCT):
            B.mm(p[:, 0:n], ONE, SQ[:, ct, 0:n], start=(ct == 0), stop=(ct == NCT - 1))
        B.act(RS[:, c0:c0 + n], p[:, 0:n], AF.Sqrt, bias=float(D * EPS), scale=1.0)
        B.recip(RS[:, c0:c0 + n], RS[:, c0:c0 + n])
        for ct in range(NCT):
            B.stt(H[:, ct, c0:c0 + n], X[:, ct, c0:c0 + n], vec(gidx, ct), RS[:, c0:c0 + n], ALU.mult, ALU.mult)

    def rmsnorm(nt, gidx, sq_off, dst=None, f32dst=None):
        if prenormed[0]:
            prenormed[0] = False
            return
        for (c0, n) in subs_tok(nt):
            rmsnorm_sub(c0, n, gidx, sq_off)

    def tail_norm_fn(stage_name, sq_off):
        i_ = STAGE_ORDER.index(stage_name)
        if i_ + 1 >= len(STAGE_ORDER):
            return None
        gidx = NEXT_GIDX[STAGE_ORDER[i_ + 1]]
        prenormed[0] = True
        return lambda c0, n: rmsnorm_sub(c0, n, gidx, sq_off)

    def store_tokens(src_fm, c0, ntok, dst_rows, stage_off):
        ntile = (ntok + 127) // 128
        for ti in range(ntile):
            n = min(128, ntok - ti * 128)
            so = stage_off[ti % 2]
            XO = V(so, [D])
            for half in range(2):
                p = B.ps()
                for j in range(4):
                    ct = half * 4 + j
                    B.tr(p[0:n, j * 128:(j + 1) * 128], src_fm[:, ct, c0 + ti * 128: c0 + ti * 128 + n], IDF)
                B.cp(XO[0:n, half * 512:(half + 1) * 512], p[0:n, :], eng="act")
            B.dma(dst_rows[ti * 128: ti * 128 + n, :], XO[0:n, :])

    tables_done = []

    def build_tables():
        if tables_done or "ssm" not in stages:
            return
        tables_done.append(1)
        PCs = V(o_h, [16, 2, 32]); PSs = V(o_h + 4096, [16, 2, 32])
        Lt1 = V(o_h + 8192, [2, 32]); Lt2 = V(o_h + 8192 + 256, [2, 32])
        B.cp(PCs[:, 0, :, :], ACt, eng="dve")
        B.cp(PSs[:, 0, :, :], ASt, eng="dve")
        for k in range(1, 16):
            U = PCs[:, k - 1, :, :]; W = PSs[:, k - 1, :, :]
            B.tt(Lt1, U, ACt, ALU.mult)
            B.tt(Lt2, W, ASt, ALU.mult)
            B.tt(PCs[:, k, :, :], Lt1, Lt2, ALU.subtract)
            B.tt(Lt1, U, ASt, ALU.mult)
            B.tt(Lt2, W, ACt, ALU.mult)
            B.tt(PSs[:, k, :, :], Lt1, Lt2, ALU.add)
        B.dma(scrP[:, 0:1024], PCs.rearrange("p a b c -> p (a b c)"))
        B.dma(scrP[:, 1024:2048], PSs.rearrange("p a b c -> p (a b c)"))

    def ffn(layer, nt, G):
        subs = subs_tok(nt)
        o_act = G.alloc(NFC * nt * 2)
        o_wd = G.alloc(NFC * D * 2)
        o_slab = [G.alloc(2 * NCT * 512 * 2) for _ in range(2)]
        o_sg = [G.alloc(2048) for _ in range(2)]
        ACTB = V(o_act, [NFC, nt], BF16)
        WD = V(o_wd, [NFC, D], BF16)
        rmsnorm(nt, 2 + layer, o_wd)
        wgv = wg[layer].rearrange("(k p) n -> p k n", p=128)
        wuv = wu[layer].rearrange("(k p) n -> p k n", p=128)
        wdv = wd[layer].rearrange("(f p) n -> p f n", p=128)
        slabs = []
        c = 0
        while c < DFF:
            w = min(512, DFF - c)
            slabs.append((c, w))
            c += w
        cnt = 0
        for si, (col0, w) in enumerate(slabs):
            SL = V(o_slab[si % 2], [2, NCT, 512], BF16)
            B.dma(SL[:, 0, :, 0:w], wgv[:, :, col0:col0 + w], eng="pool")
            B.dma(SL[:, 1, :, 0:w], wuv[:, :, col0:col0 + w], eng="pool")
            if si == 1:
                B.dma(WD[:, 0:11, :], wdv[:, 0:11, :], eng="pool")
            if si == 2:
                B.dma(WD[:, 11:22, :], wdv[:, 11:22, :], eng="pool")
            for j in range(w // 128):
                fc = col0 // 128 + j
                for (c0, n) in subs:
                    pg = B.ps()
                    pu = B.ps()
                    for k in range(NCT):
                        B.mm(pg[:, 0:n], SL[:, 0, k, j * 128:(j + 1) * 128], H[:, k, c0:c0 + n], start=(k == 0), stop=(k == NCT - 1))
                    for k in range(NCT):
                        B.mm(pu[:, 0:n], SL[:, 1, k, j * 128:(j + 1) * 128], H[:, k, c0:c0 + n], start=(k == 0), stop=(k == NCT - 1))
                    SG = V(o_sg[cnt % 2], [512])
                    cnt += 1
                    B.act(SG[:, 0:n], pg[:, 0:n], AF.Silu)
                    B.tt(ACTB[:, fc, c0:c0 + n], SG[:, 0:n], pu[:, 0:n], ALU.mult)
        if layer == 0:
            build_tables()
        tn = tail_norm_fn("ffn%d" % layer, o_slab[0])
        for (c0, n) in subs:
            for ct in range(NCT):
                p = B.ps()
                for fc in range(NFC):
                    B.mm(p[:, 0:n], WD[:, fc, ct * 128:(ct + 1) * 128], ACTB[:, fc, c0:c0 + n], start=(fc == 0), stop=(fc == NFC - 1))
                B.tt(X[:, ct, c0:c0 + n], X[:, ct, c0:c0 + n], p[:, 0:n], ALU.add)
            if tn is not None:
                tn(c0, n)

    VCARRY = V(o_small, [NCT, 30], BF16)
    WDB = V(o_small + 1536, [NCT, KW + 1], BF16)[:, :, 0:KW]
    B.cp(WDB, VEC[:, :, 13:13 + KW], eng="dve")
    w1v = w_pw1.rearrange("(k p) n -> p k n", p=128)
    w2v = w_pw2.rearrange("(k p) n -> p k n", p=128)
    VWMAX = 30 + 1024 + NSEQ_S * 38
    SB0 = 30 + 1024

    def conv_layer(bi, blk, nt, G):
        subs = subs_of(nt)
        o_vb = G.alloc(NCT * VWMAX * 2)
        o_w1 = ARENA_BYTES - 49152
        o_w2 = ARENA_BYTES - 16384
        G.limit = o_w1
        o_yf = G.alloc(NCT * 512 * 4)
        o_ybf = G.alloc(NCT * 512 * 2)
        o_ysq = G.alloc(NCT * 512 * 2)
        o_dg = [G.alloc(KW * 128 * 2) for _ in range(2)]
        o_sig = [G.alloc(2048) for _ in range(2)]
        o_stat = [G.alloc(2048) for _ in range(4)]
        o_vo = G.alloc(NCT * 160 * 4)
        VB = V(o_vb, [NCT, VWMAX], BF16)
        W1 = V(o_w1, [NCT, 2048], BF16)
        W2 = V(o_w2, [NCT, 1024], BF16)
        YF = V(o_yf, [NCT, 512])
        YBF = V(o_ybf, [NCT, 512], BF16)
        YSQ = V(o_ysq, [NCT, 512], BF16)
        VO = V(o_vo, [NCT, 160])
        MEAN = V(o_stat[0], [512]); TMP = V(o_stat[1], [512]); RSTD = V(o_stat[2], [512]); MR = V(o_stat[3], [512])
        stg = [o_ybf, o_ybf + 4096]
        for s in (0, 2, 1, 3):
            B.dma(W1[:, :, s * 512:(s + 1) * 512], w1v[:, :, s * 512:(s + 1) * 512], eng="pool")
        for s in range(2):
            B.dma(W2[:, :, s * 512:(s + 1) * 512], w2v[:, :, s * 512:(s + 1) * 512], eng="pool")
        rmsnorm(nt, 0, o_yf)
        if bi == 0:
            B.memset(VB[:, :, 0:30], 0.0, eng="pool")
            for i in range(4):
                XIN = V(stg[i % 2], [D])
                B.dma(XIN[0:120, :], cc[120 * i:120 * i + 120, :])
                for half in range(2):
                    p = B.ps()
                    for j in range(4):
                        ct = half * 4 + j
                        B.tr(p[:, j * 128:j * 128 + 120], XIN[0:120, ct * 128:(ct + 1) * 128], IDF[0:120, 0:120])
                    for j in range(4):
                        ct = half * 4 + j
                        dst = VB[:, ct, SB0 + 152 * i: SB0 + 152 * i + 152].rearrange("p (s k) -> p s k", k=38)[:, :, 0:30]
                        B.cp(dst, p[:, j * 128:j * 128 + 120].rearrange("p (s k) -> p s k", k=30), eng="dve")
        else:
            B.cp(VB[:, :, 0:30], VCARRY, eng="dve")
        cnt = 0
        for ct in range(NCT):
            for (c0, n) in subs:
                pa = B.ps()
                pb = B.ps()
                for k in range(NCT):
                    B.mm(pa[:, 0:n], W1[:, k, ct * 128:(ct + 1) * 128], H[:, k, c0:c0 + n], start=(k == 0), stop=(k == NCT - 1))
                for k in range(NCT):
                    B.mm(pb[:, 0:n], W1[:, k, 1024 + ct * 128:1024 + (ct + 1) * 128], H[:, k, c0:c0 + n], start=(k == 0), stop=(k == NCT - 1))
                SIG = V(o_sig[cnt % 2], [512])
                cnt += 1
                B.act(SIG[:, 0:n], pb[:, 0:n], AF.Sigmoid, bias=vec(6, ct), scale=1.0)
                if c0 < 1024:
                    B.stt(VB[:, ct, 30 + c0:30 + c0 + n], pa[:, 0:n], vec(5, ct), SIG[:, 0:n], ALU.add, ALU.mult)
                    if bi == 1 and c0 + n == 1024:
                        B.stt(VO[:, ct, 0:30], pa[:, n - 30:n], vec(5, ct), SIG[:, n - 30:n], ALU.add, ALU.mult)
                else:
                    dst = VB[:, ct, SB0:SB0 + 608].rearrange("p (s k) -> p s k", k=38)[:, :, 30:38]
                    B.stt(dst, pa[:, 0:n].rearrange("p (s t) -> p s t", t=8), vec(5, ct),
                          SIG[:, 0:n].rearrange("p (s t) -> p s t", t=8), ALU.add, ALU.mult)
                    B.stt(VO[:, ct, 32:160], pa[:, 0:n], vec(5, ct), SIG[:, 0:n], ALU.add, ALU.mult)
        o_dgr = list(o_dg) + [o_w1 + 8192 * i_ for i_ in range(4)]
        NDG = len(o_dgr)
        NAHEAD = NDG - 1

        def gen_dg(i):
            ct_ = i % NCT
            DG_ = V(o_dgr[i % NDG], [KW, 128], BF16)
            idb_b = bass.AP(IDB.tensor, IDB.offset, [list(IDB.ap[0]), [0, KW], [1, 128]])
            wsl = WDB[:, ct_, :]
            w_b = bass.AP(wsl.tensor, wsl.offset, [list(wsl.ap[0]), [1, KW], [0, 128]])
            B.tt(DG_, idb_b, w_b, ALU.mult)

        dgc = 0
        ndg = len(subs) * NCT
        for i_ in range(min(NAHEAD, ndg)):
            gen_dg(i_)
        for (c0, n) in subs:
            for ct in range(NCT):
                DG = V(o_dgr[dgc % NDG], [KW, 128], BF16)
                if dgc + NAHEAD < ndg:
                    gen_dg(dgc + NAHEAD)
                dgc += 1
                p = B.ps()
                for k in range(KW):
                    if c0 < 1024:
                        rhs = VB[:, ct, c0 + k:c0 + k + n]
                        out = p[:, 0:n]
                    else:
                        rhs = VB[:, ct, SB0:SB0 + 608].rearrange("p (s k) -> p s k", k=38)[:, :, k:k + 8]
                        out = p[:, 0:n].rearrange("p (s t) -> p s t", t=8)
                    B.mm(out, DG[:, k, :], rhs, start=(k == 0), stop=(k == KW - 1))
                B.act(YF[:, ct, 0:n], p[:, 0:n], AF.Identity, bias=vec(7, ct), scale=1.0)
                B.act(YSQ[:, ct, 0:n], p[:, 0:n], AF.Square, bias=vec(7, ct), scale=1.0)
                B.act(YBF[:, ct, 0:n], p[:, 0:n], AF.Identity, bias=vec(7, ct), scale=1.0)
            p1 = B.ps()
            p2 = B.ps()
            for ct in range(NCT):
                B.mm(p1[:, 0:n], ONE, YBF[:, ct, 0:n], start=(ct == 0), stop=(ct == NCT - 1))
            for ct in range(NCT):
                B.mm(p2[:, 0:n], ONE, YSQ[:, ct, 0:n], start=(ct == 0), stop=(ct == NCT - 1))
            B.ts(MEAN[:, 0:n], p1[:, 0:n], 1.0 / D, None, ALU.mult)
            B.tt(TMP[:, 0:n], MEAN[:, 0:n], MEAN[:, 0:n], ALU.mult)
            B.stt(TMP[:, 0:n], p2[:, 0:n], 1.0 / D, TMP[:, 0:n], ALU.mult, ALU.subtract)
            B.act(RSTD[:, 0:n], TMP[:, 0:n], AF.Sqrt, bias=float(EPS), scale=1.0)
            B.recip(RSTD[:, 0:n], RSTD[:, 0:n])
            B.tt(MR[:, 0:n], MEAN[:, 0:n], RSTD[:, 0:n], ALU.mult)
            for ct in range(NCT):
                B.tt(YF[:, ct, 0:n], YF[:, ct, 0:n], RSTD[:, 0:n], ALU.mult)
                B.tt(YF[:, ct, 0:n], YF[:, ct, 0:n], MR[:, 0:n], ALU.subtract)
                B.act(H[:, ct, c0:c0 + n], YF[:, ct, 0:n], AF.Silu, bias=vec(9, ct), scale=vec(8, ct))
        tn = tail_norm_fn("conv", o_ysq)
        for (c0, n) in subs_tok(nt):
            for ct in range(NCT):
                p = B.ps()
                for k in range(NCT):
                    B.mm(p[:, 0:n], W2[:, k, ct * 128:(ct + 1) * 128], H[:, k, c0:c0 + n], start=(k == 0), stop=(k == NCT - 1))
                B.tt(X[:, ct, c0:c0 + n], X[:, ct, c0:c0 + n], p[:, 0:n], ALU.add)
            if tn is not None:
                tn(c0, n)
        if bi == 0:
            B.cp(VCARRY, VB[:, :, 1024:1054], eng="dve")
            XO = V(stg[0], [D])
            for half in range(2):
                p = B.ps()
                for j in range(4):
                    ct = half * 4 + j
                    B.tr(p[:, j * 128:(j + 1) * 128], VO[:, ct, 32:160], IDF)
                B.cp(XO[:, half * 512:(half + 1) * 512], p[:, :], eng="act")
            cs3 = convs.rearrange("(s k) d -> s k d", k=30)
            cc3 = cc.rearrange("(s k) d -> s k d", k=30)
            for s in range(NSEQ_S):
                B.dma(cs3[s, 22:30, :], XO[8 * s:8 * s + 8, :])
            B.dma(cs3[:, 0:22, :], cc3[:, 8:30, :])
        else:
            XO = V(stg[0], [D])
            for half in range(2):
                p = B.ps()
                for j in range(4):
                    ct = half * 4 + j
                    B.tr(p[0:30, j * 128:(j + 1) * 128], VO[:, ct, 0:30], IDF)
                B.cp(XO[0:30, half * 512:(half + 1) * 512], p[0:30, :], eng="act")
            B.dma(convp, XO[0:30, :])

    scrE = nc.dram_tensor("scrE", [128, 16384], BF16, kind="Internal").ap()
    scrD = nc.dram_tensor("scrD", [128, 16384], BF16, kind="Internal").ap()
    scrK = nc.dram_tensor("scrK", [128, 8192], BF16, kind="Internal").ap()
    scrP = nc.dram_tensor("scrP", [128, 2048], F32, kind="Internal").ap()
    o_ac = o_small + 512
    ACt = V(o_ac, [2, 32]); ASt = V(o_ac + 256, [2, 32]); SCARRY = V(o_ac + 512, [2, 32])
    MASK = V(o_ac + 768, [4])
    PSTR = ARENA_BYTES // 4
    NSLOT = 161

    def bc_last(ap2, n):
        return bass.AP(ap2.tensor, ap2.offset, [list(ap2.ap[0]), list(ap2.ap[1]), [0, n]])

    def ssm_setup_loads():
        G = Arena(arena_h, GEN0, ARENA_BYTES - 49152 - 8192)
        f = lambda n: G.alloc(n)
        ld = dict(G=G)
        ld["T1a"] = V(f(512), [128]); ld["T1b"] = V(f(512), [128]); ld["T2"] = V(f(512), [128])
        ld["BZ"] = [V(f(2048), [32, 16]) for _ in range(2)]
        ld["CST"] = V(f(8192), [2, 8, 128])
        for T1x, src_ in ((ld["T1a"], lam_re), (ld["T1b"], lam_im)):
            B.dma(T1x[0:64, 0:64], src_)
            B.dma(T1x[0:64, 64:128], src_)
        B.dma(ld["T2"][0:64, 0:1], log_dt)
        for part, src_ in enumerate((b_re, b_im)):
            sv = src_.rearrange("(q t p) c -> t p q c", t=2, p=64)
            for g2 in range(2):
                for q8 in range(4):
                    B.dma(ld["BZ"][part][64 * g2:64 * g2 + 64, 8 * q8:8 * q8 + 8, :], sv[g2][:, 8 * q8:8 * q8 + 8, :])
        for part, src_ in enumerate((c_re, c_im)):
            sv3 = src_.rearrange("(rt p) x -> p rt x", p=128)
            B.dma(ld["CST"][:, part, :, 0:64], sv3)
            B.dma(ld["CST"][:, part, :, 64:128], sv3)
        return ld

    def ssm_setup(ld):
        G = ld["G"]
        f = lambda n: G.alloc(n)
        sm = {}
        for nm in ("LR", "LI", "DT", "MAG", "ANG", "KF", "R", "M", "SIN", "COS", "AR", "AI", "NR", "DEN", "CFR", "CFI", "t1", "t2", "PR", "PI"):
            sm[nm] = V(f(128), [32])
        KI = V(f(128), [32], I32)
        T1 = V(f(512), [128])
        T2 = ld["T2"]
        BZ = ld["BZ"]
        CZ = [V(f(2048), [32, 16]) for _ in range(2)]
        TA = V(f(2048), [32, 16]); TB = V(f(2048), [32, 16])
        TA2 = V(f(2048), [32, 16]); TB2 = V(f(2048), [32, 16])
        GG = [[V(f(2048), [32, 16]) for _ in range(2)] for _ in range(2)]
        FF = [[V(f(2048), [32, 16]) for _ in range(2)] for _ in range(2)]
        GB = [V(f(2048), [32, 32], BF16) for _ in range(2)]
        FBm = [V(f(4096), [2, 32, 32], BF16) for _ in range(2)]
        BTB = [V(f(2048), [32, 32], BF16), V(f(2048), [32, 32], BF16)]
        ETS = [V(f(2048), [T, 128], BF16), V(f(2048), [T, 128], BF16)]
        DTS = [V(f(4096), [32, 2, 32], BF16) for _ in range(2)]
        KT = V(o_h, [NCT, T, 128], BF16)

        for z_ in (FBm[0], FBm[1], DTS[0], DTS[1], KT):
            B.P.add("act", (lambda z_: (lambda e: e.memzero(z_)))(z_), [], [z_])

        def tposed(dst, staged, is_col=False):
            if is_col:
                B.act(T2[0:64, 0:1], T2[0:64, 0:1], AF.Exp)
                B.cp(T1[0:64, :], T2[0:64, 0:1].to_broadcast([64, 128]), eng="dve")
                staged = T1
            p = B.ps()
            B.tr(p[:, 0:64], staged[0:64, :], IDF[0:64, 0:64])
            for g2 in range(2):
                B.cp(dst[64 * g2:64 * g2 + 64, :], p[64 * g2:64 * g2 + 64, g2:64:2], eng="dve")

        tposed(sm["LR"], ld["T1a"])
        tposed(sm["LI"], ld["T1b"])
        tposed(sm["DT"], None, is_col=True)
        S = sm
        B.tt(S["MAG"], S["LR"], S["DT"], ALU.mult)
        B.act(S["MAG"], S["MAG"], AF.Exp)
        B.tt(S["ANG"], S["LI"], S["DT"], ALU.mult)
        TWO_PI = 2.0 * math.pi

        def reduce_sin(dst, src, shift):
            B.ts(S["R"], src, shift, None, ALU.add)
            B.ts(S["KF"], S["R"], 1.0 / TWO_PI, None, ALU.mult)
            B.cp(KI, S["KF"], eng="dve")
            B.cp(S["KF"], KI, eng="dve")
            B.stt(S["R"], S["KF"], -TWO_PI, S["R"], ALU.mult, ALU.add)
            B.ts(S["M"], S["R"], -math.pi, TWO_PI, ALU.is_lt, ALU.mult)
            B.tt(S["R"], S["R"], S["M"], ALU.add)
            B.ts(S["M"], S["R"], math.pi, -TWO_PI, ALU.is_gt, ALU.mult)
            B.tt(S["R"], S["R"], S["M"], ALU.add)
            B.ts(S["R"], S["R"], math.pi, -math.pi, ALU.min, ALU.max)
            B.act(dst, S["R"], AF.Sin)

        reduce_sin(S["SIN"], S["ANG"], 0.0)
        reduce_sin(S["COS"], S["ANG"], math.pi / 2)
        B.tt(S["AR"], S["MAG"], S["COS"], ALU.mult)
        B.tt(S["AI"], S["MAG"], S["SIN"], ALU.mult)
        B.ts(S["NR"], S["AR"], -1.0, None, ALU.add)
        B.tt(S["DEN"], S["LR"], S["LR"], ALU.mult)
        B.tt(S["t1"], S["LI"], S["LI"], ALU.mult)
        B.tt(S["DEN"], S["DEN"], S["t1"], ALU.add)
        B.recip(S["DEN"], S["DEN"])
        B.tt(S["t1"], S["NR"], S["LR"], ALU.mult)
        B.tt(S["t2"], S["AI"], S["LI"], ALU.mult)
        B.tt(S["t1"], S["t1"], S["t2"], ALU.add)
        B.tt(S["CFR"], S["t1"], S["DEN"], ALU.mult)
        B.tt(S["t1"], S["AI"], S["LR"], ALU.mult)
        B.tt(S["t2"], S["NR"], S["LI"], ALU.mult)
        B.tt(S["t1"], S["t1"], S["t2"], ALU.subtract)
        B.tt(S["CFI"], S["t1"], S["DEN"], ALU.mult)
        B.cp(S["PR"], S["AR"], eng="dve")
        B.cp(S["PI"], S["AI"], eng="dve")
        for _ in range(3):
            B.tt(S["t1"], S["PR"], S["PR"], ALU.mult)
            B.tt(S["t2"], S["PI"], S["PI"], ALU.mult)
            B.tt(S["M"], S["PR"], S["PI"], ALU.mult)
            B.tt(S["PR"], S["t1"], S["t2"], ALU.subtract)
            B.ts(S["PI"], S["M"], 2.0, None, ALU.mult)
        B.cp(ACt[:, 0, :], S["PR"], eng="dve")
        B.cp(ACt[:, 1, :], S["PR"], eng="dve")
        B.ts(ASt[:, 0, :], S["PI"], -1.0, None, ALU.mult)
        B.cp(ASt[:, 1, :], S["PI"], eng="dve")
        CST = ld["CST"]
        for part, src in enumerate((c_re, c_im)):
            for rt in range(8):
                p = B.ps()
                B.tr(p[:, 0:128], CST[:, part, rt, :], IDF)
                for g2 in range(2):
                    srcv = p[64 * g2:64 * g2 + 64, 0:128].rearrange("p (a t c) -> p a t c", t=2, c=16)[:, :, g2, :]
                    B.cp(CZ[part][64 * g2:64 * g2 + 64, 4 * rt:4 * rt + 4, :], srcv, eng="dve")
        ARb = bc_last(S["AR"], 16); AIb = bc_last(S["AI"], 16)
        CRb = bc_last(S["CFR"], 16); CIb = bc_last(S["CFI"], 16)

        def cmul(dre, dim, sre_, sim_, br, bi_):
            B.tt(TA, sre_, br, ALU.mult)
            B.tt(TB, sim_, bi_, ALU.mult)
            B.tt(TA2, sre_, bi_, ALU.mult)
            B.tt(TB2, sim_, br, ALU.mult)
            B.tt(dre, TA, TB, ALU.subtract)
            B.tt(dim, TA2, TB2, ALU.add)

        def zcast(dst_zb, src_c, scale=None):
            for g2 in range(2):
                d_ = dst_zb[64 * g2:64 * g2 + 64, :, 16 * g2:16 * g2 + 16]
                s_ = src_c[64 * g2:64 * g2 + 64, :, :]
                if scale is None:
                    B.act(d_, s_, AF.Copy)
                else:
                    B.act(d_, s_, AF.Copy, scale=scale)

        for z_ in (GB[0], GB[1], BTB[0], BTB[1]):
            B.memset(z_, 0.0, eng="dve")
        cmul(FF[0][0], FF[0][1], BZ[0], BZ[1], CRb, CIb)
        zcast(BTB[0], FF[0][0])
        zcast(BTB[1], FF[0][1], scale=-1.0)
        BTP = [V(f(4096), [32, 64], BF16) for _ in range(2)]
        for i_ in range(2):
            B.memset(BTP[i_], 0.0, eng="dve")
            B.cp(BTP[i_][:, :, 32:64], BTB[i_], eng="dve")
        B.cp(GG[0][0], CZ[0], eng="act")
        B.cp(GG[0][1], CZ[1], eng="act")
        IDBq = IDB
        for m in range(T + 1):
            cur = m % 2
            nxt = (m + 1) % 2
            Gc = GG[cur]
            if m < T:
                Fc = FF[cur]
                FBc = FBm[m % 2]
                zcast(FBc[:, 0, :, :], Fc[0])
                zcast(FBc[:, 1, :, :], Fc[1])
                for part in range(2):
                    p = B.ps()
                    pb = p.bitcast(BF16)
                    for qq in range(8):
                        for r in range(4):
                            B.tr(pb[32 * r:32 * r + 32, qq * 128:(qq + 1) * 128], FBc[:, part, 4 * qq + r, :], IDB, tp=(0, 32 * r))
                    ets = ETS[(2 * m + part) % 2]
                    B.cp(ets, pb.rearrange("p (k c) -> p k c", c=128), eng="act")
                    B.dma(scrE.rearrange("p (a b c) -> p a b c", a=T, b=2)[:, T - 1 - m, part, :], ets.rearrange("p a b -> p (a b)"))
                zcast(GB[0], Gc[0])
                zcast(GB[1], Gc[1])
                for ct in range(NCT):
                    if m % 4 == 0:
                        pass
            if m >= 1:
                DTc = DTS[m % 2]
                zcast(DTc[:, :, 0, :], Gc[0])
                zcast(DTc[:, :, 1, :], Gc[1], scale=-1.0)
                B.dma(scrD.rearrange("p (a b) -> p a b", a=T)[:, m - 1, :], DTc.rearrange("p a b c -> p (a b c)"))
            if m < T:
                kcopies = []
                for cg in range(2):
                    p = B.ps()
                    for c4 in range(4):
                        ct = 4 * cg + c4
                        for r in range(4):
                            q = 4 * ct + r
                            if r < 3:
                                o = p[32 * r:32 * r + 32, 128 * c4 + 32 * r:128 * c4 + 32 * r + 32]
                                B.mm(o, BTB[0][:, q, :], GB[0][:, q, :], start=True, stop=False, tp=(0, 32 * r))
                                B.mm(o, BTB[1][:, q, :], GB[1][:, q, :], start=False, stop=True, tp=(0, 32 * r))
                            else:
                                o = p[64:128, 128 * c4 + 96:128 * c4 + 128]
                                B.mm(o, BTP[0][:, q, :], GB[0][:, q, :], start=True, stop=False, tp=(0, 64))
                                B.mm(o, BTP[1][:, q, :], GB[1][:, q, :], start=False, stop=True, tp=(0, 64))
                    kcopies.append((cg, p))
                cmul(FF[nxt][0], FF[nxt][1], FF[cur][0], FF[cur][1], ARb, AIb)
            if m < T:
                cmul(GG[nxt][0], GG[nxt][1], Gc[0], Gc[1], ARb, AIb)
                for (cg, p) in kcopies:
                    for r in range(4):
                        pr0 = 32 * r if r < 3 else 64
                        B.cp(KT[pr0:128 if r == 3 else pr0 + 32, 4 * cg:4 * cg + 4, m, 32 * r:32 * r + 32],
                             p[pr0:128 if r == 3 else pr0 + 32, :].rearrange("p (c x) -> p c x", x=128)[:, :, 32 * r:32 * r + 32], eng="dve")
        B.dma(scrK, KT.rearrange("p a b c -> p (a b c)"))

    wglv = w_glu.rearrange("(k p) n -> p k n", p=128)

    def ssm_layer(bi, blk, nt, G):
        import os
        SK = os.environ.get('SSM_SKIP', '')
        subs = subs_of(nt)
        o_ed = G.alloc(32768)
        o_k = G.alloc(16384)
        o_xs = G.alloc(NSLOT * 64 * 4)
        o_xsb = G.alloc(145 * 64 * 2)
        o_wgl = G.alloc(NCT * 2048 * 2)
        ET = V(o_ed, [T, 2, 8, 128], BF16)
        DTl = V(o_ed, [T, 32, 2, 32], BF16)
        KT = V(o_k, [NCT, T, 128], BF16)
        XS = V(o_xs, [NSLOT, 2, 32])
        XSB = V(o_xsb, [2, 32, 145], BF16)
        WGL = V(o_wgl, [NCT, 2048], BF16)
        build_tables()
        if 'S' not in SK:
            B.dma(ET.rearrange("p a b c d -> p (a b c d)"), scrE)
            B.dma(KT.rearrange("p a b c -> p (a b c)"), scrK)
        rmsnorm(nt, 1, o_xs)
        nchp = 128
        cpi = 0
        if 'E' in SK:
            B.memset(XS[:, 1:129, :, :], 0.0, eng="dve")
            B.memset(XS[:, 129:145, :, :], 0.0, eng="dve")
        NCHM = NTMAX // T
        nchk = nt // T
        HMs = [V(o_xsb, [4, T, NCHM], BF16), V(o_xsb + 4 * NTMAX * 2, [4, T, NCHM], BF16)]
        for q in (range(NPAIR) if 'E' not in SK else []):
            r = q % 4
            ct = q // 4
            HM = HMs[ct % 2]
            if r == 0:
                for r_ in range(4):
                    hsrc = H[:, ct, 0:nt].rearrange("p (n k) -> p k n", k=T)
                    if r_ != 3:
                        B.ts(HM[:, r_, :, 0:nchk], hsrc, MASK[:, r_:r_ + 1], None, ALU.mult)
                    else:
                        B.act(HM[:, r_, :, 0:nchk], hsrc, AF.Copy, scale=MASK[:, r_:r_ + 1])
            for part in range(2):
                p = B.ps()
                for kap in range(T):
                    B.mm(p[:, 0:nchk], ET[:, kap, part, ct, :], HM[:, r, kap, 0:nchk],
                         start=(kap == 0), stop=(kap == T - 1))
                eng = "act" if cpi % 2 == 0 else "dve"
                cpi += 1
                B.cp(XS[:, 1:1 + nchk, part, q], p[:, 0:nchk], eng=eng)
        o_scr = o_wgl
        if bi == 0:
            B.memset(XS[:, 0, :, :], 0.0, eng="dve")
            S16 = V(o_scr, [4096])
            if 'I' in SK:
                B.memset(XS[:, 145:161, :, :], 0.0, eng="dve")
            for part, src in (enumerate((sre, sim)) if 'I' not in SK else []):
                B.dma(S16[0:16, :], src)
                p = B.ps()
                for q in range(NPAIR):
                    B.tr(p[:, 16 * q:16 * q + 16], S16[0:16, 128 * q:128 * q + 128], IDF[0:16, 0:16])
                B.cp(XS[:, 145:161, part, :], p[:, :].rearrange("p (q s) -> p s q", s=16), eng="dve")
        else:
            B.cp(XS[:, 0, :, :], SCARRY, eng="dve")
        L1 = V(o_scr + 16384, [16, 2, 32]); L2 = V(o_scr + 16384 + 4096, [16, 2, 32])

        def swp(ap3):
            return bass.AP(ap3.tensor, ap3.offset + 32, [list(ap3.ap[0]), [-32, 2], [1, 32]])

        def bcm(t3, m):
            return bass.AP(t3.tensor, t3.offset, [list(t3.ap[0]), [0, m], [32, 2], [1, 32]])

        def bch(t3, h, m):
            return bass.AP(t3.tensor, t3.offset + 32 * h, [list(t3.ap[0]), [0, m], [1, 32]])

        import os as _os
        USE_SWAP = _os.environ.get('NO_SWAP', '') == ''

        def cmac(dst, s_full, s_h0, s_h1, c_full, s0, s1, m, ts_full=None):
            L1v = L1[:, 0:m, :, :]
            L2v = L2[:, 0:m, :, :]
            B.tt(L1v, c_full, s_full, ALU.mult)
            if USE_SWAP:
                pat = [list(x) for x in s_full.ap]
                assert pat[2] == [32, 2] and pat[3] == [1, 32], pat
                s_sw = bass.AP(s_full.tensor, s_full.offset + 32, [pat[0], pat[1], [-32, 2], [1, 32]])
                B.tt(L2v, ts_full, s_sw, ALU.mult)
            else:
                B.tt(L2v[:, :, 0, :], s0, s_h1, ALU.mult)
                B.tt(L2v[:, :, 1, :], s1, s_h0, ALU.mult)
            B.tt(L1v, L1v, L2v, ALU.add)
            B.tt(dst, dst, L1v, ALU.add)

        if 'L' not in SK:
            PC = V(o_scr, [16, 2, 32]); PS = V(o_scr + 4096, [16, 2, 32])
            B.dma(PC.rearrange("p a b c -> p (a b c)"), scrP[:, 0:1024])
            B.dma(PS.rearrange("p a b c -> p (a b c)"), scrP[:, 1024:2048])
            for t in range(1, 16):
                s = XS[:, t:t + 113:16, :, :]
                d = XS[:, t + 1:t + 114:16, :, :]
                cmac(d, s, s[:, :, 0, :], s[:, :, 1, :], bcm(ACt, 8), bch(ASt, 0, 8), bch(ASt, 1, 8), 8, ts_full=bcm(ASt, 8))
            def stepB(j):
                s = XS[:, 16 * j:16 * j + 1, :, :]
                d = XS[:, 16 * j + 16:16 * j + 17, :, :]
                cmac(d, s, s[:, :, 0, :], s[:, :, 1, :], PC[:, 15:16, :, :], PS[:, 15:16, 0, :], PS[:, 15:16, 1, :], 1, ts_full=PS[:, 15:16, :, :])

            def stepC(j):
                c3 = XS[:, 16 * j, :, :]
                d = XS[:, 16 * j + 1:16 * j + 16, :, :]
                cmac(d, bcm(c3, 15), bch(c3, 0, 15), bch(c3, 1, 15), PC[:, 0:15, :, :], PS[:, 0:15, 0, :], PS[:, 0:15, 1, :], 15, ts_full=PS[:, 0:15, :, :])
                B.cp(XSB[:, :, :, 16 * j:16 * j + 16], XS[:, 16 * j:16 * j + 16, :, :].rearrange("p n a q -> p a q n"), eng="act")

            for j in range(3):
                stepB(j)
            for j in range(4):
                stepC(j)
            for j in range(3, 8):
                stepB(j)
            for j in range(4, 8):
                stepC(j)
        if bi == 0 and 'L' not in SK:
            cur = XS[:, 145:161, :, :]
            acb = bass.AP(ACt.tensor, ACt.offset, [list(ACt.ap[0]), [0, 16], [32, 2], [1, 32]])
            asb = bass.AP(ASt.tensor, ASt.offset, [list(ASt.ap[0]), [0, 16], [32, 2], [1, 32]])
            B.tt(L1, acb, cur, ALU.mult)
            as0 = bass.AP(ASt.tensor, ASt.offset, [list(ASt.ap[0]), [0, 16], [1, 32]])
            as1 = bass.AP(ASt.tensor, ASt.offset + 32, [list(ASt.ap[0]), [0, 16], [1, 32]])
            B.tt(L2[:, :, 0, :], as0, cur[:, :, 1, :], ALU.mult)
            B.tt(L2[:, :, 1, :], as1, cur[:, :, 0, :], ALU.mult)
            B.tt(L1, L1, L2, ALU.add)
            B.tt(XS[:, 129:145, :, :], XS[:, 129:145, :, :], L1, ALU.add)
        if 'L' not in SK:
            if bi == 0:
                B.cp(XSB[:, :, :, 129:145], XS[:, 145:161, :, :].rearrange("p n a q -> p a q n"), eng="act")
        else:
            B.cp(XSB[:, :, :, 0:145], XS[:, 0:145, :, :].rearrange("p n a q -> p a q n"), eng="act")
        if 'S' not in SK:
            B.dma(DTl.rearrange("p a b c d -> p (a b c d)"), scrD)
        for s in range(4):
            B.dma(WGL[:, :, s * 512:(s + 1) * 512], wglv[:, :, s * 512:(s + 1) * 512], eng="pool")
        TA_ = V(o_xs, [512]); TB_ = V(o_xs + 2048, [512]); SIGs = [V(o_xs + 4096, [512]), V(o_xs + 6144, [512])]
        for (c0, n) in subs:
            nch = n // T
            slot0 = (c0 // T) if c0 < 1024 else 129
            for ct in range(NCT):
                p = B.ps()
                pv = p[:, 0:n].rearrange("p (c t) -> p c t", t=T)
                hv = H[:, ct, c0:c0 + n].rearrange("p (c t) -> p c t", t=T)
                for j in (range(T) if 'K' not in SK else [0]):
                    B.mm(pv[:, :, j:T], KT[:, ct, j, :], hv[:, :, 0:T - j], start=(j == 0), stop=False)
                for r in (range(4) if 'D' not in SK else []):
                    q = 4 * ct + r
                    for tau in range(T):
                        for part in range(2):
                            last = (r == 3 and tau == T - 1 and part == 1)
                            B.mm(p[32 * r:32 * r + 32, tau:n:T], DTl[:, tau, q, part, :], XSB[:, part, q, slot0:slot0 + nch],
                                 start=False, stop=last, tp=(0, 32 * r))
                TAc = TA_ if ct % 2 == 0 else TB_
                B.stt(TAc[:, 0:n], X[:, ct, c0:c0 + n], vec(1, ct), RS[:, c0:c0 + n], ALU.mult, ALU.mult)
                B.stt(TAc[:, 0:n], TAc[:, 0:n], vec(10, ct), p[:, 0:n], ALU.mult, ALU.add)
                B.act(H[:, ct, c0:c0 + n], TAc[:, 0:n], AF.Gelu_apprx_tanh)
        OUTS = V(o_xs + 8192, [16, 128])
        if bi == 0:
            B.cp(SCARRY, XS[:, 128, :, :], eng="dve")
            for part, dst in (enumerate((sssr, sssi)) if 'O' not in SK else []):
                for sg in range(4):
                    p = B.ps()
                    for s4 in range(4):
                        s = 4 * sg + s4
                        B.tr(p[0:32, 128 * s4:128 * s4 + 128], XS[:, 129 + s, part, :], IDF)
                    B.cp(OUTS[0:32, 4 * sg:4 * sg + 4, :], p[0:32, :].rearrange("p (s c) -> p s c", c=128), eng="dve")
                for s in range(NSEQ_S):
                    B.dma(dst[s:s + 1, :].rearrange("o (q c) -> (o q) c", c=128), OUTS[0:32, s, :])
        else:
            for part, dst in (enumerate((sspr, sspi)) if 'O' not in SK else []):
                p = B.ps()
                B.tr(p[0:32, 0:128], XS[:, 128, part, :], IDF)
                B.cp(OUTS[0:32, part, :], p[0:32, 0:128], eng="dve")
                B.dma(dst.rearrange("(q t) p -> q (t p)", t=2), OUTS[0:32, part, :])
        cnt = 0
        tn = tail_norm_fn("ssm", o_xs + 16384)
        for (c0, n) in subs_tok(nt):
          for ct in range(NCT):
            if True:
                pa = B.ps()
                pb = B.ps()
                for k in range(NCT):
                    B.mm(pa[:, 0:n], WGL[:, k, ct * 128:(ct + 1) * 128], H[:, k, c0:c0 + n], start=(k == 0), stop=(k == NCT - 1))
                for k in range(NCT):
                    B.mm(pb[:, 0:n], WGL[:, k, 1024 + ct * 128:1024 + (ct + 1) * 128], H[:, k, c0:c0 + n], start=(k == 0), stop=(k == NCT - 1))
                SIG = SIGs[cnt % 2]
                cnt += 1
                B.act(SIG[:, 0:n], pb[:, 0:n], AF.Sigmoid, bias=vec(12, ct), scale=1.0)
                B.stt(SIG[:, 0:n], pa[:, 0:n], vec(11, ct), SIG[:, 0:n], ALU.add, ALU.mult)
                B.tt(X[:, ct, c0:c0 + n], X[:, ct, c0:c0 + n], SIG[:, 0:n], ALU.add)
          if tn is not None:
              tn(c0, n)

    stg0 = [ARENA_BYTES - 49152 - 8192, ARENA_BYTES - 49152 - 4096]
    ld_ssm = ssm_setup_loads() if ("ssm" in stages or "ssmsetup" in stages) else None
    load_x(xp[0:1024, :], 0, 1024, stg0)
    load_x(xs, 1024, 128, stg0)
    if "ssm" in stages or "ssmsetup" in stages:
        for r_ in range(4):
            B.P.add("dve", (lambda r_: (lambda e: e.reduce_sum(MASK[:, r_:r_ + 1], IDF[:, 32 * r_:32 * r_ + 32], mybir.AxisListType.X)))(r_),
                    [IDF[:, 32 * r_:32 * r_ + 32]], [MASK[:, r_:r_ + 1]])
        ssm_setup(ld_ssm)

    B.V = V
    blocks = [dict(p_off=0, npr=1024, samples=True, nt=1152), dict(p_off=1024, npr=1024, samples=False, nt=1024)]
    for bi, blk in enumerate(blocks):
        nt = blk["nt"]
        G = Arena(arena_h, GEN0, ARENA_BYTES)
        stg = [G.alloc(D * 4), G.alloc(D * 4)]
        if bi > 0:
            load_x(xp[blk["p_off"]:blk["p_off"] + blk["npr"], :], 0, blk["npr"], stg)
        for li, stg_name in enumerate(("conv", "ffn0", "ssm", "ffn1")):
            if stg_name not in stages:
                continue
            G = Arena(arena_h, GEN0, ARENA_BYTES)
            if stg_name.startswith("ffn"):
                ffn(int(stg_name[3]), nt, G)
            elif stg_name == "conv":
                conv_layer(bi, blk, nt, G)
            else:
                ssm_layer(bi, blk, nt, G)
        G = Arena(arena_h, GEN0, ARENA_BYTES)
        stg = [G.alloc(D * 4), G.alloc(D * 4)]
        o_sq = G.alloc(NCT * 512 * 2)
        o_yo = G.alloc(NCT * 512 * 4)
        if final_norm:
            SQ = V(o_sq, [NCT, 512], BF16)
            YO = V(o_yo, [NCT, 512])
            for (c0, n) in subs_of(nt):
                for ct in range(NCT):
                    B.act(SQ[:, ct, 0:n], X[:, ct, c0:c0 + n], AF.Square)
                p = B.ps()
                for ct in range(NCT):
                    B.mm(p[:, 0:n], ONE, SQ[:, ct, 0:n], start=(ct == 0), stop=(ct == NCT - 1))
                B.act(RS[:, c0:c0 + n], p[:, 0:n], AF.Sqrt, bias=float(D * EPS), scale=1.0)
                B.recip(RS[:, c0:c0 + n], RS[:, c0:c0 + n])
                for ct in range(NCT):
                    B.stt(YO[:, ct, 0:n], X[:, ct, c0:c0 + n], vec(4, ct), RS[:, c0:c0 + n], ALU.mult, ALU.mult)
                if c0 < 1024:
                    store_tokens(YO, 0, n, yp[blk["p_off"] + c0: blk["p_off"] + c0 + n, :], stg)
                else:
                    store_tokens(YO, 0, n, ys, stg)
    B.P.emit(nc)
    st.close()
    return nc


_NC_CACHE = {}
_BUILD_KW = {}


def kernel(**inp):
    f32 = np.float32
    n = 8
    if "nc" not in _NC_CACHE:
        _NC_CACHE["nc"] = build(**_BUILD_KW)
    nc = _NC_CACHE["nc"]
    c = lambda a: np.ascontiguousarray(np.asarray(a, dtype=f32))
    shared = {
        "ident": np.eye(128, dtype=f32),
        "norm_mix": c(inp["norm_mix"]), "norm_ffn": c(inp["norm_ffn"]), "norm_final": c(inp["norm_final"]).reshape(1, D),
        "w_pw1": c(inp["conv_w_pw1"][0]), "b_pw1": c(inp["conv_b_pw1"]).reshape(2, D), "w_dw": c(inp["conv_w_dw"][0]),
        "b_dw": c(inp["conv_b_dw"]).reshape(1, D), "ln_g": c(inp["conv_ln_g"]).reshape(1, D),
        "ln_b": c(inp["conv_ln_b"]).reshape(1, D), "w_pw2": c(inp["conv_w_pw2"][0]),
        "lam_re": c(inp["ssm_lam_re"][0]), "lam_im": c(inp["ssm_lam_im"][0]), "log_dt": c(inp["ssm_log_dt"]).reshape(NG, 1),
        "b_re": c(inp["ssm_b_re"]).reshape(NG * 64, 16), "b_im": c(inp["ssm_b_im"]).reshape(NG * 64, 16),
        "c_re": c(inp["ssm_c_re"]).reshape(NG * 16, 64), "c_im": c(inp["ssm_c_im"]).reshape(NG * 16, 64),
        "ssm_d": c(inp["ssm_d"]).reshape(1, D), "w_glu": c(inp["ssm_w_glu"][0]), "b_glu": c(inp["ssm_b_glu"]).reshape(2, D),
        "wg": c(inp["ffn_w_gate"]), "wu": c(inp["ffn_w_up"]), "wd": c(inp["ffn_w_down"]),
    }
    xpr = c(inp["x_prompt"]); xsa = c(inp["x_sample"]); ccv = c(inp["cache_conv"])
    s_re = c(inp["state_ssm_re"]); s_im = c(inp["state_ssm_im"])
    in_maps = []
    for i in range(n):
        m = dict(shared)
        m["xp"] = xpr[i]
        m["xs"] = xsa[16 * i:16 * i + 16].reshape(128, D)
        m["cc"] = ccv[0, 16 * i:16 * i + 16].reshape(480, D)
        m["sre"] = s_re[0, 16 * i:16 * i + 16].reshape(16, 4096)
        m["sim"] = s_im[0, 16 * i:16 * i + 16].reshape(16, 4096)
        in_maps.append(m)
    res = run_bass_kernel_spmd(nc, in_maps, core_ids=list(range(n)))
    R = res.results
    y_prompt = np.stack([R[i]["yp"] for i in range(n)]).astype(f32)
    y_sample = np.concatenate([R[i]["ys"].reshape(16, 8, D) for i in range(n)]).astype(f32)
    conv_p = np.stack([R[i]["convp"] for i in range(n)])[None].astype(f32)
    conv_s = np.concatenate([R[i]["convs"].reshape(16, 30, D) for i in range(n)])[None].astype(f32)
    sp_re = np.stack([R[i]["sspr"] for i in range(n)])[None].astype(f32)
    sp_im = np.stack([R[i]["sspi"] for i in range(n)])[None].astype(f32)
    ss_re = np.concatenate([R[i]["sssr"].reshape(16, 64, 64) for i in range(n)])[None].astype(f32)
    ss_im = np.concatenate([R[i]["sssi"].reshape(16, 64, 64) for i in range(n)])[None].astype(f32)
    return (y_prompt, y_sample, conv_p, conv_s, sp_re, sp_im, ss_re, ss_im)
```

```python
import math
import numpy as np
import concourse.bass as bass
import concourse.mybir as mybir
from concourse.bass_utils import run_bass_kernel_spmd

F32 = mybir.dt.float32
BF16 = mybir.dt.bfloat16
I32 = mybir.dt.int32
AF = mybir.ActivationFunctionType
ALU = mybir.AluOpType

D = 1024
NCT = 8
DFF = 2816
NFC = 22
SEQ = 2048
NSEQ_S = 16
TS = 8
KW = 31
EPS = 1e-6
T = 8
NG = 64
NPAIR = 32

_ESZ = {F32: 4, BF16: 2, I32: 4}


def _esz(dt):
    return _ESZ[dt]


class Op:
    __slots__ = ("eng", "fn", "reads", "writes", "dma", "deps", "sig", "sigval", "dsem", "dval", "prewait")

    def __init__(self, eng, fn, reads, writes, dma):
        self.eng = eng
        self.fn = fn
        self.reads = reads
        self.writes = writes
        self.dma = dma
        self.deps = set()
        self.sig = False
        self.sigval = 0
        self.dsem = None
        self.dval = 0
        self.prewait = None


def _acc(ap):
    t = ap.tensor
    name = t.name
    pat = ap.ap
    esz = _esz(ap.dtype)
    off = ap.offset
    sp = str(ap.space)
    if sp not in ("SB", "PSUM"):
        lo = off
        hi = off
        for (s, c) in pat:
            if s >= 0:
                hi += (c - 1) * s
            else:
                lo += (c - 1) * s
        return (name, 0, 1, lo * esz, (hi + 1) * esz)
    ps = pat[0][0]
    pc = pat[0][1]
    if ps == 0:
        p0 = 0
        f0 = off
        pc = 128
    else:
        p0 = off // ps
        f0 = off % ps
    lo = f0
    hi = f0
    for (s, c) in pat[1:]:
        if s >= 0:
            hi += (c - 1) * s
        else:
            lo += (c - 1) * s
    if sp == "PSUM":
        return (name, (p0 // 32) * 32, ((p0 + pc + 31) // 32) * 32, 0, 1 << 20)
    return (name, p0, p0 + pc, lo * esz, (hi + 1) * esz)


def _acc_multi(ap):
    base = _acc(ap)
    sp = str(ap.space)
    if sp != "SB":
        return [base]
    pat = ap.ap
    if len(pat) < 3 or pat[0][0] == 0:
        return [base]
    esz = _esz(ap.dtype)
    f0 = ap.offset % pat[0][0]
    dims = sorted([(s, c) for (s, c) in pat[1:] if c > 1], key=lambda d: -abs(d[0]))
    name, p0, p1, _, _ = base

    def extent(ds):
        lo = hi = 0
        for (s, c) in ds:
            if s >= 0:
                hi += (c - 1) * s
            else:
                lo += (c - 1) * s
        return lo, hi

    out = []

    def rec(ds, off, budget):
        if ds:
            (s, c) = ds[0]
            lo, hi = extent(ds[1:])
            inner = hi - lo + 1
            if c <= budget and abs(s) > inner:
                for i in range(c):
                    rec(ds[1:], off + i * s, budget // c)
                return
        lo, hi = extent(ds)
        out.append((name, p0, p1, (off + lo) * esz, (off + hi + 1) * esz))

    rec(dims, f0, 64)
    return out


class Prog:
    ENGS = ("pe", "act", "dve", "pool", "sp")

    def __init__(self):
        self.ops = []
        self.recs = {}

    def _track(self, idx, op):
        for (aps, is_w0) in ((op.reads, False), (op.writes, True)):
            for (ap, name, p0, p1, b0, b1) in [(ap_,) + iv for ap_ in aps for iv in _acc_multi(ap_)]:
                ap_space = str(ap.space)
                is_w = is_w0 or (ap_space == "PSUM")
                lst = self.recs.setdefault(name, [])
                keep = []
                for r in lst:
                    ov = not (r[1] <= p0 or p1 <= r[0] or r[3] <= b0 or b1 <= r[2])
                    if ov and (is_w or r[5]):
                        if r[4] != idx:
                            op.deps.add(r[4])
                        if is_w and r[0] >= p0 and r[1] <= p1 and r[2] >= b0 and r[3] <= b1 and r[4] != idx:
                            continue
                    keep.append(r)
                if not is_w and not op.dma:
                    keep = [r for r in keep if not ((not r[5]) and r[0] == p0 and r[1] == p1 and r[2] == b0
                                                    and r[3] == b1 and (not self.ops[r[4]].dma)
                                                    and self.ops[r[4]].eng == op.eng)]
                keep.append([p0, p1, b0, b1, idx, is_w])
                self.recs[name] = keep

    def add(self, eng, fn, reads, writes, dma=False):
        op = Op(eng, fn, list(reads), list(writes), dma)
        idx = len(self.ops)
        self.ops.append(op)
        self._track(idx, op)
        return op

    def emit(self, nc, nsem_dma=12):
        ops = self.ops
        for op in ops:
            nd = set()
            for d in op.deps:
                p = ops[d]
                if (not p.dma) and p.eng == "pe" and op.eng == "pe" and not op.dma:
                    continue
                nd.add(d)
            latest = {}
            nd2 = set()
            for d in nd:
                p = ops[d]
                if p.dma:
                    nd2.add(d)
                else:
                    if p.eng not in latest or d > latest[p.eng]:
                        latest[p.eng] = d
            nd2.update(latest.values())
            op.deps = nd2
            for d in nd2:
                if not ops[d].dma:
                    ops[d].sig = True
        cnt = {e: 0 for e in self.ENGS}
        ndma = 0
        dma_last = {}
        npool_dma = 0
        for op in ops:
            if op.dma and op.eng == "pool":
                op.dsem = ("p", npool_dma)
                op.dval = 16
                npool_dma += 1
            elif op.dma:
                s = ndma % nsem_dma
                op.dsem = s
                op.dval = 16 * (ndma // nsem_dma + 1)
                if s in dma_last:
                    op.prewait = dma_last[s]
                dma_last[s] = (s, op.dval)
                ndma += 1
            elif op.sig:
                cnt[op.eng] += 1
                op.sigval = cnt[op.eng]
        SEMCAP = 1000
        nsem_e = {e: max(1, (cnt[e] + SEMCAP - 1) // SEMCAP) for e in self.ENGS}
        print("sem counts", cnt, "ndma", ndma, "npool_dma", npool_dma, "nops", len(ops))
        import contextlib
        with contextlib.ExitStack() as st:
            esem = {e: [st.enter_context(nc.semaphore("s_%s%d" % (e, i))) for i in range(nsem_e[e])] for e in self.ENGS}
            dsem = {i: st.enter_context(nc.semaphore("d%d" % i)) for i in range(nsem_dma)}
            for i in range(npool_dma):
                dsem[("p", i)] = st.enter_context(nc.semaphore("q%d" % i))
            block = st.enter_context(nc.Block())
            final_d = dict(dma_last)

            def body(ename):
                def run(e):
                    waited = {}

                    def wait(sem, key, val):
                        if waited.get(key, 0) >= val:
                            return
                        waited[key] = val
                        e.wait_ge(sem, val)

                    for op in ops:
                        if op.eng != ename:
                            continue
                        for d in sorted(op.deps):
                            p = ops[d]
                            if p.dma:
                                wait(dsem[p.dsem], ("d", p.dsem), p.dval)
                            else:
                                si_ = (p.sigval - 1) // SEMCAP
                                wait(esem[p.eng][si_], ("e", p.eng, si_), (p.sigval - 1) % SEMCAP + 1)
                        if op.dma and op.prewait is not None:
                            wait(dsem[op.prewait[0]], ("d", op.prewait[0]), op.prewait[1])
                        ins = op.fn(e)
                        if op.dma:
                            ins.then_inc(dsem[op.dsem], 16)
                        elif op.sig:
                            ins.then_inc(esem[op.eng][(op.sigval - 1) // SEMCAP], 1)
                    if ename == "sp":
                        for s, (si, v) in final_d.items():
                            wait(dsem[si], ("d", si), v)
                        for i in range(npool_dma):
                            wait(dsem[("p", i)], ("d", ("p", i)), 16)
                        for en in self.ENGS:
                            if en != "sp" and cnt[en] > 0:
                                si_ = (cnt[en] - 1) // SEMCAP
                                wait(esem[en][si_], ("e", en, si_), (cnt[en] - 1) % SEMCAP + 1)
                return run

            block.tensor(body("pe"))
            block.scalar(body("act"))
            block.vector(body("dve"))
            block.gpsimd(body("pool"))
            block.sync(body("sp"))


class Builder:
    def __init__(self, nc, stages):
        self.nc = nc
        self.P = Prog()
        self.stages = stages
        self.psi = 0

    def mm(self, out, lhsT, rhs, start=True, stop=True, tp=None):
        kw = {}
        if tp is not None:
            kw["tile_position"] = tp
        self.P.add("pe", lambda e: e.matmul(out, lhsT, rhs, start=start, stop=stop, **kw), [lhsT, rhs], [out])

    def tr(self, out, in_, ident, tp=None):
        kw = {}
        if tp is not None:
            kw["tile_position"] = tp
        self.P.add("pe", lambda e: e.transpose(out, in_, ident, **kw), [in_, ident], [out])

    def act(self, out, in_, func, bias=None, scale=None, eng="act"):
        kw = {}
        rd = [in_]
        if bias is not None:
            kw["bias"] = bias
            if not isinstance(bias, (int, float)):
                rd.append(bias)
        if scale is not None:
            kw["scale"] = scale
            if not isinstance(scale, (int, float)):
                rd.append(scale)
        self.P.add("act", lambda e: e.activation(out, in_, func, **kw), rd, [out])

    def tt(self, out, a, b, op, eng="dve"):
        self.P.add(eng, lambda e: e.tensor_tensor(out, a, b, op), [a, b], [out])

    def ts(self, out, a, s1, s2, op0, op1=None, eng="dve"):
        rd = [a]
        for s in (s1, s2):
            if s is not None and not isinstance(s, (int, float)):
                rd.append(s)
        if op1 is None:
            self.P.add(eng, lambda e: e.tensor_scalar(out, a, s1, None, op0), rd, [out])
        else:
            self.P.add(eng, lambda e: e.tensor_scalar(out, a, s1, s2, op0, op1), rd, [out])

    def stt(self, out, a, s, b, op0, op1, eng="dve"):
        rd = [a, b]
        if not isinstance(s, (int, float)):
            rd.append(s)
        self.P.add(eng, lambda e: e.scalar_tensor_tensor(out, a, s, b, op0, op1), rd, [out])

    def cp(self, out, in_, eng="dve"):
        if eng == "act":
            self.P.add("act", lambda e: e.activation(out, in_, AF.Copy), [in_], [out])
        else:
            self.P.add(eng, lambda e: e.tensor_copy(out, in_), [in_], [out])

    def memset(self, ap, v, eng="pool"):
        self.P.add(eng, lambda e: e.memset(ap, v), [], [ap])

    def recip(self, out, in_):
        self.P.add("dve", lambda e: e.reciprocal(out, in_), [in_], [out])

    def dma(self, out, in_, eng="sp"):
        self.P.add(eng, lambda e: e.dma_start(out=out, in_=in_), [in_], [out], dma=True)

    def ps(self):
        p = self.psum[self.psi % 8]
        self.psi += 1
        return p


class Arena:
    def __init__(self, handle, base, limit):
        self.h = handle
        self.off = base
        self.limit = limit

    def alloc(self, nbytes):
        nbytes = (nbytes + 63) // 64 * 64
        o = self.off
        self.off += nbytes
        assert self.off <= self.limit, (self.off, self.limit)
        return o


ARENA_BYTES = 206848
NTMAX = 1152
NV = 44


def build(stages=("conv", "ffn0", "ssm", "ffn1"), final_norm=True):
    nc = bass.Bass("TRN2", target_bir_lowering=False)
    B = Builder(nc, stages)

    def din(name, shape, dt=F32):
        return nc.dram_tensor(name, shape, dt, kind="ExternalInput").ap()

    def dout(name, shape, dt=F32):
        return nc.dram_tensor(name, shape, dt, kind="ExternalOutput").ap()

    xp = din("xp", [SEQ, D]); xs = din("xs", [128, D]); cc = din("cc", [480, D])
    sre = din("sre", [NSEQ_S, 4096]); sim = din("sim", [NSEQ_S, 4096])
    ident_d = din("ident", [128, 128])
    norm_mix = din("norm_mix", [2, D]); norm_ffn = din("norm_ffn", [2, D]); norm_final = din("norm_final", [1, D])
    w_pw1 = din("w_pw1", [D, 2 * D]); b_pw1 = din("b_pw1", [2, D]); w_dw = din("w_dw", [KW, D])
    b_dw = din("b_dw", [1, D]); ln_g = din("ln_g", [1, D]); ln_b = din("ln_b", [1, D]); w_pw2 = din("w_pw2", [D, D])
    lam_re = din("lam_re", [NG, 64]); lam_im = din("lam_im", [NG, 64]); log_dt = din("log_dt", [NG, 1])
    b_re = din("b_re", [NG * 64, 16]); b_im = din("b_im", [NG * 64, 16])
    c_re = din("c_re", [NG * 16, 64]); c_im = din("c_im", [NG * 16, 64])
    ssm_d = din("ssm_d", [1, D]); w_glu = din("w_glu", [D, 2 * D]); b_glu = din("b_glu", [2, D])
    wg = din("wg", [2, D, DFF]); wu = din("wu", [2, D, DFF]); wd = din("wd", [2, DFF, D])

    yp = dout("yp", [SEQ, D]); ys = dout("ys", [128, D])
    convp = dout("convp", [KW - 1, D]); convs = dout("convs", [480, D])
    sspr = dout("sspr", [NG, 64]); sspi = dout("sspi", [NG, 64])
    sssr = dout("sssr", [NSEQ_S, 4096]); sssi = dout("sssi", [NSEQ_S, 4096])

    import contextlib
    st = contextlib.ExitStack()
    arena_h = st.enter_context(nc.sbuf_tensor("arena", [128, ARENA_BYTES // 4], F32))
    B.psum = [st.enter_context(nc.psum_tensor("ps%d" % i, [128, 512], F32)) for i in range(8)]

    def V(off, shape, dt=F32):
        n = 1
        for s in shape:
            n *= s
        nb = n * _esz(dt)
        assert off % 4 == 0 and nb % 4 == 0
        ap = arena_h[:, off // 4: off // 4 + nb // 4]
        if dt != F32:
            ap = ap.bitcast(dt)
        if len(shape) == 2:
            ap = ap.rearrange("p (a b) -> p a b", b=shape[1])
        elif len(shape) == 3:
            ap = ap.rearrange("p (a b c) -> p a b c", b=shape[1], c=shape[2])
        elif len(shape) == 4:
            ap = ap.rearrange("p (a b c d) -> p a b c d", b=shape[1], c=shape[2], d=shape[3])
        return ap

    A = Arena(arena_h, 0, ARENA_BYTES)
    o_x = A.alloc(NCT * NTMAX * 4)
    o_h = A.alloc(NCT * NTMAX * 2)
    o_rs = A.alloc(NTMAX * 4)
    o_idf = A.alloc(128 * 4)
    o_idb = A.alloc(128 * 2)
    o_one = A.alloc(128 * 2)
    o_vec = A.alloc(NCT * NV * 4)
    o_small = A.alloc(2048)
    GEN0 = A.off
    X = V(o_x, [NCT, NTMAX]); H = V(o_h, [NCT, NTMAX], BF16); RS = V(o_rs, [NTMAX])
    IDF = V(o_idf, [128]); IDB = V(o_idb, [128], BF16); ONE = V(o_one, [128], BF16)
    VEC = V(o_vec, [NCT, NV])

    def vec(v, ct):
        return VEC[:, ct, v:v + 1]

    B.dma(IDF, ident_d)
    B.cp(IDB, IDF, eng="dve")
    B.memset(ONE, 1.0, eng="dve")
    G = Arena(arena_h, GEN0, ARENA_BYTES)
    o_vt = G.alloc(D * 4)
    VT = V(o_vt, [D])
    rows = [(norm_mix, 0, 2), (norm_ffn, 2, 2), (norm_final, 4, 1), (b_pw1, 5, 2), (b_dw, 7, 1), (ln_g, 8, 1),
            (ln_b, 9, 1), (ssm_d, 10, 1), (b_glu, 11, 2), (w_dw, 13, KW)]
    for (src, r0, n) in rows:
        B.dma(VT[r0:r0 + n, :], src)
    for ct in range(NCT):
        p = B.ps()
        B.tr(p[:, 0:NV], VT[0:NV, ct * 128:(ct + 1) * 128], IDF[0:NV, 0:NV])
        B.cp(VEC[:, ct, :], p[:, 0:NV], eng="dve")
    B.ts(VEC[:, :, 0:5], VEC[:, :, 0:5], math.sqrt(D), None, ALU.mult)

    def subs_of(nt):
        out = []
        c = 0
        while c < nt:
            n = min(512, nt - c)
            out.append((c, n))
            c += n
        return out

    def subs_tok(nt):
        if nt == 1152:
            return [(0, 384), (384, 384), (768, 384)]
        return subs_of(nt)

    def load_x(src_rows, c0, ntok, stage_off):
        ntile = ntok // 128
        for ti in range(ntile):
            so = stage_off[ti % 2]
            XIN = V(so, [D])
            B.dma(XIN, src_rows[ti * 128:(ti + 1) * 128, :])
            for half in range(2):
                p = B.ps()
                for j in range(4):
                    ct = half * 4 + j
                    B.tr(p[:, j * 128:(j + 1) * 128], XIN[:, ct * 128:(ct + 1) * 128], IDF)
                B.cp(X[:, half * 4:half * 4 + 4, c0 + ti * 128: c0 + (ti + 1) * 128],
                     p.rearrange("p (a b) -> p a b", b=128), eng="act")

    prenormed = [False]
    STAGE_ORDER = [s_ for s_ in ("conv", "ffn0", "ssm", "ffn1") if s_ in stages]
    NEXT_GIDX = {"ffn0": 2, "ssm": 1, "ffn1": 3}

    def rmsnorm_sub(c0, n, gidx, sq_off):
        SQ = V(sq_off, [NCT, 512], BF16)
        for ct in range(NCT):
            B.act(SQ[:, ct, 0:n], X[:, ct, c0:c0 + n], AF.Square)
        p = B.ps()
        for ct in range(NCT):
            B.mm(p[:, 0:n], ONE, SQ[:, ct, 0:n], start=(ct == 0), stop=(ct == NCT - 1))
        B.act(RS[:, c0:c0 + n], p[:, 0:n], AF.Sqrt, bias=float(D * EPS), scale=1.0)
        B.recip(RS[:, c0:c0 + n], RS[:, c0:c0 + n])
        for ct in range(NCT):
            B.stt(H[:, ct, c0:c0 + n], X[:, ct, c0:c0 + n], vec(gidx, ct), RS[:, c0:c0 + n], ALU.mult, ALU.mult)

    def rmsnorm(nt, gidx, sq_off, dst=None, f32dst=None):
        if prenormed[0]:
            prenormed[0] = False
            return
        for (c0, n) in subs_tok(nt):
            rmsnorm_sub(c0, n, gidx, sq_off)

    def tail_norm_fn(stage_name, sq_off):
        i_ = STAGE_ORDER.index(stage_name)
        if i_ + 1 >= len(STAGE_ORDER):
            return None
        gidx = NEXT_GIDX[STAGE_ORDER[i_ + 1]]
        prenormed[0] = True
        return lambda c0, n: rmsnorm_sub(c0, n, gidx, sq_off)

    def store_tokens(src_fm, c0, ntok, dst_rows, stage_off):
        ntile = (ntok + 127) // 128
        for ti in range(ntile):
            n = min(128, ntok - ti * 128)
            so = stage_off[ti % 2]
            XO = V(so, [D])
            for half in range(2):
                p = B.ps()
                for j in range(4):
                    ct = half * 4 + j
                    B.tr(p[0:n, j * 128:(j + 1) * 128], src_fm[:, ct, c0 + ti * 128: c0 + ti * 128 + n], IDF)
                B.cp(XO[0:n, half * 512:(half + 1) * 512], p[0:n, :], eng="act")
            B.dma(dst_rows[ti * 128: ti * 128 + n, :], XO[0:n, :])

    tables_done = []

    def build_tables():
        if tables_done or "ssm" not in stages:
            return
        tables_done.append(1)
        PCs = V(o_h, [16, 2, 32]); PSs = V(o_h + 4096, [16, 2, 32])
        Lt1 = V(o_h + 8192, [2, 32]); Lt2 = V(o_h + 8192 + 256, [2, 32])
        B.cp(PCs[:, 0, :, :], ACt, eng="dve")
        B.cp(PSs[:, 0, :, :], ASt, eng="dve")
        for k in range(1, 16):
            U = PCs[:, k - 1, :, :]; W = PSs[:, k - 1, :, :]
            B.tt(Lt1, U, ACt, ALU.mult)
            B.tt(Lt2, W, ASt, ALU.mult)
            B.tt(PCs[:, k, :, :], Lt1, Lt2, ALU.subtract)
            B.tt(Lt1, U, ASt, ALU.mult)
            B.tt(Lt2, W, ACt, ALU.mult)
            B.tt(PSs[:, k, :, :], Lt1, Lt2, ALU.add)
        B.dma(scrP[:, 0:1024], PCs.rearrange("p a b c -> p (a b c)"))
        B.dma(scrP[:, 1024:2048], PSs.rearrange("p a b c -> p (a b c)"))

    def ffn(layer, nt, G):
        subs = subs_tok(nt)
        o_act = G.alloc(NFC * nt * 2)
        o_wd = G.alloc(NFC * D * 2)
        o_slab = [G.alloc(2 * NCT * 512 * 2) for _ in range(2)]
        o_sg = [G.alloc(2048) for _ in range(2)]
        ACTB = V(o_act, [NFC, nt], BF16)
        WD = V(o_wd, [NFC, D], BF16)
        rmsnorm(nt, 2 + layer, o_wd)
        wgv = wg[layer].rearrange("(k p) n -> p k n", p=128)
        wuv = wu[layer].rearrange("(k p) n -> p k n", p=128)
        wdv = wd[layer].rearrange("(f p) n -> p f n", p=128)
        slabs = []
        c = 0
        while c < DFF:
            w = min(512, DFF - c)
            slabs.append((c, w))
            c += w
        cnt = 0
        for si, (col0, w) in enumerate(slabs):
            SL = V(o_slab[si % 2], [2, NCT, 512], BF16)
            B.dma(SL[:, 0, :, 0:w], wgv[:, :, col0:col0 + w], eng="pool")
            B.dma(SL[:, 1, :, 0:w], wuv[:, :, col0:col0 + w], eng="pool")
            if si == 1:
                B.dma(WD[:, 0:11, :], wdv[:, 0:11, :], eng="pool")
            if si == 2:
                B.dma(WD[:, 11:22, :], wdv[:, 11:22, :], eng="pool")
            for j in range(w // 128):
                fc = col0 // 128 + j
                for (c0, n) in subs:
                    pg = B.ps()
                    pu = B.ps()
                    for k in range(NCT):
                        B.mm(pg[:, 0:n], SL[:, 0, k, j * 128:(j + 1) * 128], H[:, k, c0:c0 + n], start=(k == 0), stop=(k == NCT - 1))
                    for k in range(NCT):
                        B.mm(pu[:, 0:n], SL[:, 1, k, j * 128:(j + 1) * 128], H[:, k, c0:c0 + n], start=(k == 0), stop=(k == NCT - 1))
                    SG = V(o_sg[cnt % 2], [512])
                    cnt += 1
                    B.act(SG[:, 0:n], pg[:, 0:n], AF.Silu)
                    B.tt(ACTB[:, fc, c0:c0 + n], SG[:, 0:n], pu[:, 0:n], ALU.mult)
        if layer == 0:
            build_tables()
        tn = tail_norm_fn("ffn%d" % layer, o_slab[0])
        for (c0, n) in subs:
            for ct in range(NCT):
                p = B.ps()
                for fc in range(NFC):
                    B.mm(p[:, 0:n], WD[:, fc, ct * 128:(ct + 1) * 128], ACTB[:, fc, c0:c0 + n], start=(fc == 0), stop=(fc == NFC - 1))
                B.tt(X[:, ct, c0:c0 + n], X[:, ct, c0:c0 + n], p[:, 0:n], ALU.add)
            if tn is not None:
                tn(c0, n)

    VCARRY = V(o_small, [NCT, 30], BF16)
    WDB = V(o_small + 1536, [NCT, KW + 1], BF16)[:, :, 0:KW]
    B.cp(WDB, VEC[:, :, 13:13 + KW], eng="dve")
    w1v = w_pw1.rearrange("(k p) n -> p k n", p=128)
    w2v = w_pw2.rearrange("(k p) n -> p k n", p=128)
    VWMAX = 30 + 1024 + NSEQ_S * 38
    SB0 = 30 + 1024

    def conv_layer(bi, blk, nt, G):
        subs = subs_of(nt)
        o_vb = G.alloc(NCT * VWMAX * 2)
        o_w1 = ARENA_BYTES - 49152
        o_w2 = ARENA_BYTES - 16384
        G.limit = o_w1
        o_yf = G.alloc(NCT * 512 * 4)
        o_ybf = G.alloc(NCT * 512 * 2)
        o_ysq = G.alloc(NCT * 512 * 2)
        o_dg = [G.alloc(KW * 128 * 2) for _ in range(2)]
        o_sig = [G.alloc(2048) for _ in range(2)]
        o_stat = [G.alloc(2048) for _ in range(4)]
        o_vo = G.alloc(NCT * 160 * 4)
        VB = V(o_vb, [NCT, VWMAX], BF16)
        W1 = V(o_w1, [NCT, 2048], BF16)
        W2 = V(o_w2, [NCT, 1024], BF16)
        YF = V(o_yf, [NCT, 512])
        YBF = V(o_ybf, [NCT, 512], BF16)
        YSQ = V(o_ysq, [NCT, 512], BF16)
        VO = V(o_vo, [NCT, 160])
        MEAN = V(o_stat[0], [512]); TMP = V(o_stat[1], [512]); RSTD = V(o_stat[2], [512]); MR = V(o_stat[3], [512])
        stg = [o_ybf, o_ybf + 4096]
        for s in (0, 2, 1, 3):
            B.dma(W1[:, :, s * 512:(s + 1) * 512], w1v[:, :, s * 512:(s + 1) * 512], eng="pool")
        for s in range(2):
            B.dma(W2[:, :, s * 512:(s + 1) * 512], w2v[:, :, s * 512:(s + 1) * 512], eng="pool")
        rmsnorm(nt, 0, o_yf)
        if bi == 0:
            B.memset(VB[:, :, 0:30], 0.0, eng="pool")
            for i in range(4):
                XIN = V(stg[i % 2], [D])
                B.dma(XIN[0:120, :], cc[120 * i:120 * i + 120, :])
                for half in range(2):
                    p = B.ps()
                    for j in range(4):
                        ct = half * 4 + j
                        B.tr(p[:, j * 128:j * 128 + 120], XIN[0:120, ct * 128:(ct + 1) * 128], IDF[0:120, 0:120])
                    for j in range(4):
                        ct = half * 4 + j
                        dst = VB[:, ct, SB0 + 152 * i: SB0 + 152 * i + 152].rearrange("p (s k) -> p s k", k=38)[:, :, 0:30]
                        B.cp(dst, p[:, j * 128:j * 128 + 120].rearrange("p (s k) -> p s k", k=30), eng="dve")
        else:
            B.cp(VB[:, :, 0:30], VCARRY, eng="dve")
        cnt = 0
        for ct in range(NCT):
            for (c0, n) in subs:
                pa = B.ps()
                pb = B.ps()
                for k in range(NCT):
                    B.mm(pa[:, 0:n], W1[:, k, ct * 128:(ct + 1) * 128], H[:, k, c0:c0 + n], start=(k == 0), stop=(k == NCT - 1))
                for k in range(NCT):
                    B.mm(pb[:, 0:n], W1[:, k, 1024 + ct * 128:1024 + (ct + 1) * 128], H[:, k, c0:c0 + n], start=(k == 0), stop=(k == NCT - 1))
                SIG = V(o_sig[cnt % 2], [512])
                cnt += 1
                B.act(SIG[:, 0:n], pb[:, 0:n], AF.Sigmoid, bias=vec(6, ct), scale=1.0)
                if c0 < 1024:
                    B.stt(VB[:, ct, 30 + c0:30 + c0 + n], pa[:, 0:n], vec(5, ct), SIG[:, 0:n], ALU.add, ALU.mult)
                    if bi == 1 and c0 + n == 1024:
                        B.stt(VO[:, ct, 0:30], pa[:, n - 30:n], vec(5, ct), SIG[:, n - 30:n], ALU.add, ALU.mult)
                else:
                    dst = VB[:, ct, SB0:SB0 + 608].rearrange("p (s k) -> p s k", k=38)[:, :, 30:38]
                    B.stt(dst, pa[:, 0:n].rearrange("p (s t) -> p s t", t=8), vec(5, ct),
                          SIG[:, 0:n].rearrange("p (s t) -> p s t", t=8), ALU.add, ALU.mult)
                    B.stt(VO[:, ct, 32:160], pa[:, 0:n], vec(5, ct), SIG[:, 0:n], ALU.add, ALU.mult)
        o_dgr = list(o_dg) + [o_w1 + 8192 * i_ for i_ in range(4)]
        NDG = len(o_dgr)
        NAHEAD = NDG - 1

        def gen_dg(i):
            ct_ = i % NCT
            DG_ = V(o_dgr[i % NDG], [KW, 128], BF16)
            idb_b = bass.AP(IDB.tensor, IDB.offset, [list(IDB.ap[0]), [0, KW], [1, 128]])
            wsl = WDB[:, ct_, :]
            w_b = bass.AP(wsl.tensor, wsl.offset, [list(wsl.ap[0]), [1, KW], [0, 128]])
            B.tt(DG_, idb_b, w_b, ALU.mult)

        dgc = 0
        ndg = len(subs) * NCT
        for i_ in range(min(NAHEAD, ndg)):
            gen_dg(i_)
        for (c0, n) in subs:
            for ct in range(NCT):
                DG = V(o_dgr[dgc % NDG], [KW, 128], BF16)
                if dgc + NAHEAD < ndg:
                    gen_dg(dgc + NAHEAD)
                dgc += 1
                p = B.ps()
                for k in range(KW):
                    if c0 < 1024:
                        rhs = VB[:, ct, c0 + k:c0 + k + n]
                        out = p[:, 0:n]
                    else:
                        rhs = VB[:, ct, SB0:SB0 + 608].rearrange("p (s k) -> p s k", k=38)[:, :, k:k + 8]
                        out = p[:, 0:n].rearrange("p (s t) -> p s t", t=8)
                    B.mm(out, DG[:, k, :], rhs, start=(k == 0), stop=(k == KW - 1))
                B.act(YF[:, ct, 0:n], p[:, 0:n], AF.Identity, bias=vec(7, ct), scale=1.0)
                B.act(YSQ[:, ct, 0:n], p[:, 0:n], AF.Square, bias=vec(7, ct), scale=1.0)
                B.act(YBF[:, ct, 0:n], p[:, 0:n], AF.Identity, bias=vec(7, ct), scale=1.0)
            p1 = B.ps()
            p2 = B.ps()
            for ct in range(NCT):
                B.mm(p1[:, 0:n], ONE, YBF[:, ct, 0:n], start=(ct == 0), stop=(ct == NCT - 1))
            for ct in range(NCT):
                B.mm(p2[:, 0:n], ONE, YSQ[:, ct, 0:n], start=(ct == 0), stop=(ct == NCT - 1))
            B.ts(MEAN[:, 0:n], p1[:, 0:n], 1.0 / D, None, ALU.mult)
            B.tt(TMP[:, 0:n], MEAN[:, 0:n], MEAN[:, 0:n], ALU.mult)
            B.stt(TMP[:, 0:n], p2[:, 0:n], 1.0 / D, TMP[:, 0:n], ALU.mult, ALU.subtract)
            B.act(RSTD[:, 0:n], TMP[:, 0:n], AF.Sqrt, bias=float(EPS), scale=1.0)
            B.recip(RSTD[:, 0:n], RSTD[:, 0:n])
            B.tt(MR[:, 0:n], MEAN[:, 0:n], RSTD[:, 0:n], ALU.mult)
            for ct in range(NCT):
                B.tt(YF[:, ct, 0:n], YF[:, ct, 0:n], RSTD[:, 0:n], ALU.mult)
                B.tt(YF[:, ct, 0:n], YF[:, ct, 0:n], MR[:, 0:n], ALU.subtract)
                B.act(H[:, ct, c0:c0 + n], YF[:, ct, 0:n], AF.Silu, bias=vec(9, ct), scale=vec(8, ct))
        tn = tail_norm_fn("conv", o_ysq)
        for (c0, n) in subs_tok(nt):
            for ct in range(NCT):
                p = B.ps()
                for k in range(NCT):
                    B.mm(p[:, 0:n], W2[:, k, ct * 128:(ct + 1) * 128], H[:, k, c0:c0 + n], start=(k == 0), stop=(k == NCT - 1))
                B.tt(X[:, ct, c0:c0 + n], X[:, ct, c0:c0 + n], p[:, 0:n], ALU.add)
            if tn is not None:
                tn(c0, n)
        if bi == 0:
            B.cp(VCARRY, VB[:, :, 1024:1054], eng="dve")
            XO = V(stg[0], [D])
            for half in range(2):
                p = B.ps()
                for j in range(4):
                    ct = half * 4 + j
                    B.tr(p[:, j * 128:(j + 1) * 128], VO[:, ct, 32:160], IDF)
                B.cp(XO[:, half * 512:(half + 1) * 512], p[:, :], eng="act")
            cs3 = convs.rearrange("(s k) d -> s k d", k=30)
            cc3 = cc.rearrange("(s k) d -> s k d", k=30)
            for s in range(NSEQ_S):
                B.dma(cs3[s, 22:30, :], XO[8 * s:8 * s + 8, :])
            B.dma(cs3[:, 0:22, :], cc3[:, 8:30, :])
        else:
            XO = V(stg[0], [D])
            for half in range(2):
                p = B.ps()
                for j in range(4):
                    ct = half * 4 + j
                    B.tr(p[0:30, j * 128:(j + 1) * 128], VO[:, ct, 0:30], IDF)
                B.cp(XO[0:30, half * 512:(half + 1) * 512], p[0:30, :], eng="act")
            B.dma(convp, XO[0:30, :])

    scrE = nc.dram_tensor("scrE", [128, 16384], BF16, kind="Internal").ap()
    scrD = nc.dram_tensor("scrD", [128, 16384], BF16, kind="Internal").ap()
    scrK = nc.dram_tensor("scrK", [128, 8192], BF16, kind="Internal").ap()
    scrP = nc.dram_tensor("scrP", [128, 2048], F32, kind="Internal").ap()
    o_ac = o_small + 512
    ACt = V(o_ac, [2, 32]); ASt = V(o_ac + 256, [2, 32]); SCARRY = V(o_ac + 512, [2, 32])
    MASK = V(o_ac + 768, [4])
    PSTR = ARENA_BYTES // 4
    NSLOT = 161

    def bc_last(ap2, n):
        return bass.AP(ap2.tensor, ap2.offset, [list(ap2.ap[0]), list(ap2.ap[1]), [0, n]])

    def ssm_setup():
        G = Arena(arena_h, GEN0, ARENA_BYTES - 49152 - 8192)
        f = lambda n: G.alloc(n)
        sm = {}
        for nm in ("LR", "LI", "DT", "MAG", "ANG", "KF", "R", "M", "SIN", "COS", "AR", "AI", "NR", "DEN", "CFR", "CFI", "t1", "t2", "PR", "PI"):
            sm[nm] = V(f(128), [32])
        KI = V(f(128), [32], I32)
        T1 = V(f(512), [128])
        T2 = V(f(512), [128])
        BZ = [V(f(2048), [32, 16]) for _ in range(2)]
        CZ = [V(f(2048), [32, 16]) for _ in range(2)]
        TA = V(f(2048), [32, 16]); TB = V(f(2048), [32, 16])
        TA2 = V(f(2048), [32, 16]); TB2 = V(f(2048), [32, 16])
        GG = [[V(f(2048), [32, 16]) for _ in range(2)] for _ in range(2)]
        FF = [[V(f(2048), [32, 16]) for _ in range(2)] for _ in range(2)]
        GB = [V(f(2048), [32, 32], BF16) for _ in range(2)]
        FBm = [V(f(4096), [2, 32, 32], BF16) for _ in range(2)]
        BTB = [V(f(2048), [32, 32], BF16), V(f(2048), [32, 32], BF16)]
        ETS = [V(f(2048), [T, 128], BF16), V(f(2048), [T, 128], BF16)]
        DTS = [V(f(4096), [32, 2, 32], BF16) for _ in range(2)]
        KT = V(o_h, [NCT, T, 128], BF16)

        for z_ in (FBm[0], FBm[1], DTS[0], DTS[1], KT):
            B.P.add("act", (lambda z_: (lambda e: e.memzero(z_)))(z_), [], [z_])

        def tposed(dst, src_ap_rows, is_col=False):
            if is_col:
                B.dma(T2[0:64, 0:1], src_ap_rows)
                B.act(T2[0:64, 0:1], T2[0:64, 0:1], AF.Exp)
                B.cp(T1[0:64, :], T2[0:64, 0:1].to_broadcast([64, 128]), eng="dve")
            else:
                B.dma(T1[0:64, 0:64], src_ap_rows)
                B.dma(T1[0:64, 64:128], src_ap_rows)
            p = B.ps()
            B.tr(p[:, 0:64], T1[0:64, :], IDF[0:64, 0:64])
            for g2 in range(2):
                B.cp(dst[64 * g2:64 * g2 + 64, :], p[64 * g2:64 * g2 + 64, g2:64:2], eng="dve")

        tposed(sm["LR"], lam_re)
        tposed(sm["LI"], lam_im)
        tposed(sm["DT"], log_dt, is_col=True)
        S = sm
        B.tt(S["MAG"], S["LR"], S["DT"], ALU.mult)
        B.act(S["MAG"], S["MAG"], AF.Exp)
        B.tt(S["ANG"], S["LI"], S["DT"], ALU.mult)
        TWO_PI = 2.0 * math.pi

        def reduce_sin(dst, src, shift):
            B.ts(S["R"], src, shift, None, ALU.add)
            B.ts(S["KF"], S["R"], 1.0 / TWO_PI, None, ALU.mult)
            B.cp(KI, S["KF"], eng="dve")
            B.cp(S["KF"], KI, eng="dve")
            B.stt(S["R"], S["KF"], -TWO_PI, S["R"], ALU.mult, ALU.add)
            B.ts(S["M"], S["R"], -math.pi, TWO_PI, ALU.is_lt, ALU.mult)
            B.tt(S["R"], S["R"], S["M"], ALU.add)
            B.ts(S["M"], S["R"], math.pi, -TWO_PI, ALU.is_gt, ALU.mult)
            B.tt(S["R"], S["R"], S["M"], ALU.add)
            B.ts(S["R"], S["R"], math.pi, -math.pi, ALU.min, ALU.max)
            B.act(dst, S["R"], AF.Sin)

        reduce_sin(S["SIN"], S["ANG"], 0.0)
        reduce_sin(S["COS"], S["ANG"], math.pi / 2)
        B.tt(S["AR"], S["MAG"], S["COS"], ALU.mult)
        B.tt(S["AI"], S["MAG"], S["SIN"], ALU.mult)
        B.ts(S["NR"], S["AR"], -1.0, None, ALU.add)
        B.tt(S["DEN"], S["LR"], S["LR"], ALU.mult)
        B.tt(S["t1"], S["LI"], S["LI"], ALU.mult)
        B.tt(S["DEN"], S["DEN"], S["t1"], ALU.add)
        B.recip(S["DEN"], S["DEN"])
        B.tt(S["t1"], S["NR"], S["LR"], ALU.mult)
        B.tt(S["t2"], S["AI"], S["LI"], ALU.mult)
        B.tt(S["t1"], S["t1"], S["t2"], ALU.add)
        B.tt(S["CFR"], S["t1"], S["DEN"], ALU.mult)
        B.tt(S["t1"], S["AI"], S["LR"], ALU.mult)
        B.tt(S["t2"], S["NR"], S["LI"], ALU.mult)
        B.tt(S["t1"], S["t1"], S["t2"], ALU.subtract)
        B.tt(S["CFI"], S["t1"], S["DEN"], ALU.mult)
        B.cp(S["PR"], S["AR"], eng="dve")
        B.cp(S["PI"], S["AI"], eng="dve")
        for _ in range(3):
            B.tt(S["t1"], S["PR"], S["PR"], ALU.mult)
            B.tt(S["t2"], S["PI"], S["PI"], ALU.mult)
            B.tt(S["M"], S["PR"], S["PI"], ALU.mult)
            B.tt(S["PR"], S["t1"], S["t2"], ALU.subtract)
            B.ts(S["PI"], S["M"], 2.0, None, ALU.mult)
        B.cp(ACt[:, 0, :], S["PR"], eng="dve")
        B.cp(ACt[:, 1, :], S["PR"], eng="dve")
        B.ts(ASt[:, 0, :], S["PI"], -1.0, None, ALU.mult)
        B.cp(ASt[:, 1, :], S["PI"], eng="dve")
        for part, src in enumerate((b_re, b_im)):
            sv = src.rearrange("(q t p) c -> t p q c", t=2, p=64)
            for g2 in range(2):
                for q8 in range(4):
                    B.dma(BZ[part][64 * g2:64 * g2 + 64, 8 * q8:8 * q8 + 8, :], sv[g2][:, 8 * q8:8 * q8 + 8, :])
        CST = V(f(8192), [2, 8, 128])
        for part, src in enumerate((c_re, c_im)):
            sv3 = src.rearrange("(rt p) x -> p rt x", p=128)
            B.dma(CST[:, part, :, 0:64], sv3)
            B.dma(CST[:, part, :, 64:128], sv3)
        for part, src in enumerate((c_re, c_im)):
            for rt in range(8):
                p = B.ps()
                B.tr(p[:, 0:128], CST[:, part, rt, :], IDF)
                for g2 in range(2):
                    srcv = p[64 * g2:64 * g2 + 64, 0:128].rearrange("p (a t c) -> p a t c", t=2, c=16)[:, :, g2, :]
                    B.cp(CZ[part][64 * g2:64 * g2 + 64, 4 * rt:4 * rt + 4, :], srcv, eng="dve")
        ARb = bc_last(S["AR"], 16); AIb = bc_last(S["AI"], 16)
        CRb = bc_last(S["CFR"], 16); CIb = bc_last(S["CFI"], 16)

        def cmul(dre, dim, sre_, sim_, br, bi_):
            B.tt(TA, sre_, br, ALU.mult)
            B.tt(TB, sim_, bi_, ALU.mult)
            B.tt(TA2, sre_, bi_, ALU.mult)
            B.tt(TB2, sim_, br, ALU.mult)
            B.tt(dre, TA, TB, ALU.subtract)
            B.tt(dim, TA2, TB2, ALU.add)

        def zcast(dst_zb, src_c, scale=None):
            for g2 in range(2):
                d_ = dst_zb[64 * g2:64 * g2 + 64, :, 16 * g2:16 * g2 + 16]
                s_ = src_c[64 * g2:64 * g2 + 64, :, :]
                if scale is None:
                    B.act(d_, s_, AF.Copy)
                else:
                    B.act(d_, s_, AF.Copy, scale=scale)

        for z_ in (GB[0], GB[1], BTB[0], BTB[1]):
            B.memset(z_, 0.0, eng="dve")
        cmul(FF[0][0], FF[0][1], BZ[0], BZ[1], CRb, CIb)
        zcast(BTB[0], FF[0][0])
        zcast(BTB[1], FF[0][1], scale=-1.0)
        BTP = [V(f(4096), [32, 64], BF16) for _ in range(2)]
        for i_ in range(2):
            B.memset(BTP[i_], 0.0, eng="dve")
            B.cp(BTP[i_][:, :, 32:64], BTB[i_], eng="dve")
        B.cp(GG[0][0], CZ[0], eng="act")
        B.cp(GG[0][1], CZ[1], eng="act")
        IDBq = IDB
        for m in range(T + 1):
            cur = m % 2
            nxt = (m + 1) % 2
            Gc = GG[cur]
            if m < T:
                Fc = FF[cur]
                FBc = FBm[m % 2]
                zcast(FBc[:, 0, :, :], Fc[0])
                zcast(FBc[:, 1, :, :], Fc[1])
                for part in range(2):
                    p = B.ps()
                    pb = p.bitcast(BF16)
                    for qq in range(8):
                        for r in range(4):
                            B.tr(pb[32 * r:32 * r + 32, qq * 128:(qq + 1) * 128], FBc[:, part, 4 * qq + r, :], IDB, tp=(0, 32 * r))
                    ets = ETS[(2 * m + part) % 2]
                    B.cp(ets, pb.rearrange("p (k c) -> p k c", c=128), eng="act")
                    B.dma(scrE.rearrange("p (a b c) -> p a b c", a=T, b=2)[:, T - 1 - m, part, :], ets.rearrange("p a b -> p (a b)"))
                zcast(GB[0], Gc[0])
                zcast(GB[1], Gc[1])
                for ct in range(NCT):
                    if m % 4 == 0:
                        pass
            if m >= 1:
                DTc = DTS[m % 2]
                zcast(DTc[:, :, 0, :], Gc[0])
                zcast(DTc[:, :, 1, :], Gc[1], scale=-1.0)
                B.dma(scrD.rearrange("p (a b) -> p a b", a=T)[:, m - 1, :], DTc.rearrange("p a b c -> p (a b c)"))
            if m < T:
                kcopies = []
                for cg in range(2):
                    p = B.ps()
                    for c4 in range(4):
                        ct = 4 * cg + c4
                        for r in range(4):
                            q = 4 * ct + r
                            if r < 3:
                                o = p[32 * r:32 * r + 32, 128 * c4 + 32 * r:128 * c4 + 32 * r + 32]
                                B.mm(o, BTB[0][:, q, :], GB[0][:, q, :], start=True, stop=False, tp=(0, 32 * r))
                                B.mm(o, BTB[1][:, q, :], GB[1][:, q, :], start=False, stop=True, tp=(0, 32 * r))
                            else:
                                o = p[64:128, 128 * c4 + 96:128 * c4 + 128]
                                B.mm(o, BTP[0][:, q, :], GB[0][:, q, :], start=True, stop=False, tp=(0, 64))
                                B.mm(o, BTP[1][:, q, :], GB[1][:, q, :], start=False, stop=True, tp=(0, 64))
                    kcopies.append((cg, p))
                cmul(FF[nxt][0], FF[nxt][1], FF[cur][0], FF[cur][1], ARb, AIb)
            if m < T:
                cmul(GG[nxt][0], GG[nxt][1], Gc[0], Gc[1], ARb, AIb)
                for (cg, p) in kcopies:
                    for r in range(4):
                        pr0 = 32 * r if r < 3 else 64
                        B.cp(KT[pr0:128 if r == 3 else pr0 + 32, 4 * cg:4 * cg + 4, m, 32 * r:32 * r + 32],
                             p[pr0:128 if r == 3 else pr0 + 32, :].rearrange("p (c x) -> p c x", x=128)[:, :, 32 * r:32 * r + 32], eng="dve")
        B.dma(scrK, KT.rearrange("p a b c -> p (a b c)"))

    wglv = w_glu.rearrange("(k p) n -> p k n", p=128)

    def ssm_layer(bi, blk, nt, G):
        import os
        SK = os.environ.get('SSM_SKIP', '')
        subs = subs_of(nt)
        o_ed = G.alloc(32768)
        o_k = G.alloc(16384)
        o_xs = G.alloc(NSLOT * 64 * 4)
        o_xsb = G.alloc(145 * 64 * 2)
        o_wgl = G.alloc(NCT * 2048 * 2)
        ET = V(o_ed, [T, 2, 8, 128], BF16)
        DTl = V(o_ed, [T, 32, 2, 32], BF16)
        KT = V(o_k, [NCT, T, 128], BF16)
        XS = V(o_xs, [NSLOT, 2, 32])
        XSB = V(o_xsb, [2, 32, 145], BF16)
        WGL = V(o_wgl, [NCT, 2048], BF16)
        build_tables()
        if 'S' not in SK:
            B.dma(ET.rearrange("p a b c d -> p (a b c d)"), scrE)
            B.dma(KT.rearrange("p a b c -> p (a b c)"), scrK)
        rmsnorm(nt, 1, o_xs)
        nchp = 128
        cpi = 0
        if 'E' in SK:
            B.memset(XS[:, 1:129, :, :], 0.0, eng="dve")
            B.memset(XS[:, 129:145, :, :], 0.0, eng="dve")
        NCHM = NTMAX // T
        nchk = nt // T
        HMs = [V(o_xsb, [4, T, NCHM], BF16), V(o_xsb + 4 * NTMAX * 2, [4, T, NCHM], BF16)]
        for q in (range(NPAIR) if 'E' not in SK else []):
            r = q % 4
            ct = q // 4
            HM = HMs[ct % 2]
            if r == 0:
                for r_ in range(4):
                    hsrc = H[:, ct, 0:nt].rearrange("p (n k) -> p k n", k=T)
                    if r_ != 3:
                        B.ts(HM[:, r_, :, 0:nchk], hsrc, MASK[:, r_:r_ + 1], None, ALU.mult)
                    else:
                        B.act(HM[:, r_, :, 0:nchk], hsrc, AF.Copy, scale=MASK[:, r_:r_ + 1])
            for part in range(2):
                p = B.ps()
                for kap in range(T):
                    B.mm(p[:, 0:nchk], ET[:, kap, part, ct, :], HM[:, r, kap, 0:nchk],
                         start=(kap == 0), stop=(kap == T - 1))
                eng = "act" if cpi % 2 == 0 else "dve"
                cpi += 1
                B.cp(XS[:, 1:1 + nchk, part, q], p[:, 0:nchk], eng=eng)
        o_scr = o_wgl
        if bi == 0:
            B.memset(XS[:, 0, :, :], 0.0, eng="dve")
            S16 = V(o_scr, [4096])
            if 'I' in SK:
                B.memset(XS[:, 145:161, :, :], 0.0, eng="dve")
            for part, src in (enumerate((sre, sim)) if 'I' not in SK else []):
                B.dma(S16[0:16, :], src)
                p = B.ps()
                for q in range(NPAIR):
                    B.tr(p[:, 16 * q:16 * q + 16], S16[0:16, 128 * q:128 * q + 128], IDF[0:16, 0:16])
                B.cp(XS[:, 145:161, part, :], p[:, :].rearrange("p (q s) -> p s q", s=16), eng="dve")
        else:
            B.cp(XS[:, 0, :, :], SCARRY, eng="dve")
        L1 = V(o_scr + 16384, [16, 2, 32]); L2 = V(o_scr + 16384 + 4096, [16, 2, 32])

        def swp(ap3):
            return bass.AP(ap3.tensor, ap3.offset + 32, [list(ap3.ap[0]), [-32, 2], [1, 32]])

        def bcm(t3, m):
            return bass.AP(t3.tensor, t3.offset, [list(t3.ap[0]), [0, m], [32, 2], [1, 32]])

        def bch(t3, h, m):
            return bass.AP(t3.tensor, t3.offset + 32 * h, [list(t3.ap[0]), [0, m], [1, 32]])

        import os as _os
        USE_SWAP = _os.environ.get('NO_SWAP', '') == ''

        def cmac(dst, s_full, s_h0, s_h1, c_full, s0, s1, m, ts_full=None):
            L1v = L1[:, 0:m, :, :]
            L2v = L2[:, 0:m, :, :]
            B.tt(L1v, c_full, s_full, ALU.mult)
            if USE_SWAP:
                pat = [list(x) for x in s_full.ap]
                assert pat[2] == [32, 2] and pat[3] == [1, 32], pat
                s_sw = bass.AP(s_full.tensor, s_full.offset + 32, [pat[0], pat[1], [-32, 2], [1, 32]])
                B.tt(L2v, ts_full, s_sw, ALU.mult)
            else:
                B.tt(L2v[:, :, 0, :], s0, s_h1, ALU.mult)
                B.tt(L2v[:, :, 1, :], s1, s_h0, ALU.mult)
            B.tt(L1v, L1v, L2v, ALU.add)
            B.tt(dst, dst, L1v, ALU.add)

        if 'L' not in SK:
            PC = V(o_scr, [16, 2, 32]); PS = V(o_scr + 4096, [16, 2, 32])
            B.dma(PC.rearrange("p a b c -> p (a b c)"), scrP[:, 0:1024])
            B.dma(PS.rearrange("p a b c -> p (a b c)"), scrP[:, 1024:2048])
            for t in range(1, 16):
                s = XS[:, t:t + 113:16, :, :]
                d = XS[:, t + 1:t + 114:16, :, :]
                cmac(d, s, s[:, :, 0, :], s[:, :, 1, :], bcm(ACt, 8), bch(ASt, 0, 8), bch(ASt, 1, 8), 8, ts_full=bcm(ASt, 8))
            def stepB(j):
                s = XS[:, 16 * j:16 * j + 1, :, :]
                d = XS[:, 16 * j + 16:16 * j + 17, :, :]
                cmac(d, s, s[:, :, 0, :], s[:, :, 1, :], PC[:, 15:16, :, :], PS[:, 15:16, 0, :], PS[:, 15:16, 1, :], 1, ts_full=PS[:, 15:16, :, :])

            def stepC(j):
                c3 = XS[:, 16 * j, :, :]
                d = XS[:, 16 * j + 1:16 * j + 16, :, :]
                cmac(d, bcm(c3, 15), bch(c3, 0, 15), bch(c3, 1, 15), PC[:, 0:15, :, :], PS[:, 0:15, 0, :], PS[:, 0:15, 1, :], 15, ts_full=PS[:, 0:15, :, :])
                B.cp(XSB[:, :, :, 16 * j:16 * j + 16], XS[:, 16 * j:16 * j + 16, :, :].rearrange("p n a q -> p a q n"), eng="act")

            for j in range(3):
                stepB(j)
            for j in range(4):
                stepC(j)
            for j in range(3, 8):
                stepB(j)
            for j in range(4, 8):
                stepC(j)
        if bi == 0 and 'L' not in SK:
            cur = XS[:, 145:161, :, :]
            acb = bass.AP(ACt.tensor, ACt.offset, [list(ACt.ap[0]), [0, 16], [32, 2], [1, 32]])
            asb = bass.AP(ASt.tensor, ASt.offset, [list(ASt.ap[0]), [0, 16], [32, 2], [1, 32]])
            B.tt(L1, acb, cur, ALU.mult)
            as0 = bass.AP(ASt.tensor, ASt.offset, [list(ASt.ap[0]), [0, 16], [1, 32]])
            as1 = bass.AP(ASt.tensor, ASt.offset + 32, [list(ASt.ap[0]), [0, 16], [1, 32]])
            B.tt(L2[:, :, 0, :], as0, cur[:, :, 1, :], ALU.mult)
            B.tt(L2[:, :, 1, :], as1, cur[:, :, 0, :], ALU.mult)
            B.tt(L1, L1, L2, ALU.add)
            B.tt(XS[:, 129:145, :, :], XS[:, 129:145, :, :], L1, ALU.add)
        if 'L' not in SK:
            if bi == 0:
                B.cp(XSB[:, :, :, 129:145], XS[:, 145:161, :, :].rearrange("p n a q -> p a q n"), eng="act")
        else:
            B.cp(XSB[:, :, :, 0:145], XS[:, 0:145, :, :].rearrange("p n a q -> p a q n"), eng="act")
        if 'S' not in SK:
            B.dma(DTl.rearrange("p a b c d -> p (a b c d)"), scrD)
        for s in range(4):
            B.dma(WGL[:, :, s * 512:(s + 1) * 512], wglv[:, :, s * 512:(s + 1) * 512], eng="pool")
        TA_ = V(o_xs, [512]); TB_ = V(o_xs + 2048, [512]); SIGs = [V(o_xs + 4096, [512]), V(o_xs + 6144, [512])]
        for (c0, n) in subs:
            nch = n // T
            slot0 = (c0 // T) if c0 < 1024 else 129
            for ct in range(NCT):
                p = B.ps()
                pv = p[:, 0:n].rearrange("p (c t) -> p c t", t=T)
                hv = H[:, ct, c0:c0 + n].rearrange("p (c t) -> p c t", t=T)
                for j in (range(T) if 'K' not in SK else [0]):
                    B.mm(pv[:, :, j:T], KT[:, ct, j, :], hv[:, :, 0:T - j], start=(j == 0), stop=False)
                for r in (range(4) if 'D' not in SK else []):
                    q = 4 * ct + r
                    for tau in range(T):
                        for part in range(2):
                            last = (r == 3 and tau == T - 1 and part == 1)
                            B.mm(p[32 * r:32 * r + 32, tau:n:T], DTl[:, tau, q, part, :], XSB[:, part, q, slot0:slot0 + nch],
                                 start=False, stop=last, tp=(0, 32 * r))
                TAc = TA_ if ct % 2 == 0 else TB_
                B.stt(TAc[:, 0:n], X[:, ct, c0:c0 + n], vec(1, ct), RS[:, c0:c0 + n], ALU.mult, ALU.mult)
                B.stt(TAc[:, 0:n], TAc[:, 0:n], vec(10, ct), p[:, 0:n], ALU.mult, ALU.add)
                B.act(H[:, ct, c0:c0 + n], TAc[:, 0:n], AF.Gelu_apprx_tanh)
        OUTS = V(o_xs + 8192, [16, 128])
        if bi == 0:
            B.cp(SCARRY, XS[:, 128, :, :], eng="dve")
            for part, dst in (enumerate((sssr, sssi)) if 'O' not in SK else []):
                for sg in range(4):
                    p = B.ps()
                    for s4 in range(4):
                        s = 4 * sg + s4
                        B.tr(p[0:32, 128 * s4:128 * s4 + 128], XS[:, 129 + s, part, :], IDF)
                    B.cp(OUTS[0:32, 4 * sg:4 * sg + 4, :], p[0:32, :].rearrange("p (s c) -> p s c", c=128), eng="dve")
                for s in range(NSEQ_S):
                    B.dma(dst[s:s + 1, :].rearrange("o (q c) -> (o q) c", c=128), OUTS[0:32, s, :])
        else:
            for part, dst in (enumerate((sspr, sspi)) if 'O' not in SK else []):
                p = B.ps()
                B.tr(p[0:32, 0:128], XS[:, 128, part, :], IDF)
                B.cp(OUTS[0:32, part, :], p[0:32, 0:128], eng="dve")
                B.dma(dst.rearrange("(q t) p -> q (t p)", t=2), OUTS[0:32, part, :])
        cnt = 0
        tn = tail_norm_fn("ssm", o_xs + 16384)
        for (c0, n) in subs_tok(nt):
          for ct in range(NCT):
            if True:
                pa = B.ps()
                pb = B.ps()
                for k in range(NCT):
                    B.mm(pa[:, 0:n], WGL[:, k, ct * 128:(ct + 1) * 128], H[:, k, c0:c0 + n], start=(k == 0), stop=(k == NCT - 1))
                for k in range(NCT):
                    B.mm(pb[:, 0:n], WGL[:, k, 1024 + ct * 128:1024 + (ct + 1) * 128], H[:, k, c0:c0 + n], start=(k == 0), stop=(k == NCT - 1))
                SIG = SIGs[cnt % 2]
                cnt += 1
                B.act(SIG[:, 0:n], pb[:, 0:n], AF.Sigmoid, bias=vec(12, ct), scale=1.0)
                B.stt(SIG[:, 0:n], pa[:, 0:n], vec(11, ct), SIG[:, 0:n], ALU.add, ALU.mult)
                B.tt(X[:, ct, c0:c0 + n], X[:, ct, c0:c0 + n], SIG[:, 0:n], ALU.add)
          if tn is not None:
              tn(c0, n)

    stg0 = [ARENA_BYTES - 49152, ARENA_BYTES - 49152 + 4096]
    load_x(xp[0:1024, :], 0, 1024, stg0)
    load_x(xs, 1024, 128, stg0)
    if "ssm" in stages or "ssmsetup" in stages:
        for r_ in range(4):
            B.P.add("dve", (lambda r_: (lambda e: e.reduce_sum(MASK[:, r_:r_ + 1], IDF[:, 32 * r_:32 * r_ + 32], mybir.AxisListType.X)))(r_),
                    [IDF[:, 32 * r_:32 * r_ + 32]], [MASK[:, r_:r_ + 1]])
        ssm_setup()

    B.V = V
    blocks = [dict(p_off=0, npr=1024, samples=True, nt=1152), dict(p_off=1024, npr=1024, samples=False, nt=1024)]
    for bi, blk in enumerate(blocks):
        nt = blk["nt"]
        G = Arena(arena_h, GEN0, ARENA_BYTES)
        stg = [G.alloc(D * 4), G.alloc(D * 4)]
        if bi > 0:
            load_x(xp[blk["p_off"]:blk["p_off"] + blk["npr"], :], 0, blk["npr"], stg)
        for li, stg_name in enumerate(("conv", "ffn0", "ssm", "ffn1")):
            if stg_name not in stages:
                continue
            G = Arena(arena_h, GEN0, ARENA_BYTES)
            if stg_name.startswith("ffn"):
                ffn(int(stg_name[3]), nt, G)
            elif stg_name == "conv":
                conv_layer(bi, blk, nt, G)
            else:
                ssm_layer(bi, blk, nt, G)
        G = Arena(arena_h, GEN0, ARENA_BYTES)
        stg = [G.alloc(D * 4), G.alloc(D * 4)]
        o_sq = G.alloc(NCT * 512 * 2)
        o_yo = G.alloc(NCT * 512 * 4)
        if final_norm:
            SQ = V(o_sq, [NCT, 512], BF16)
            YO = V(o_yo, [NCT, 512])
            for (c0, n) in subs_of(nt):
                for ct in range(NCT):
                    B.act(SQ[:, ct, 0:n], X[:, ct, c0:c0 + n], AF.Square)
                p = B.ps()
                for ct in range(NCT):
                    B.mm(p[:, 0:n], ONE, SQ[:, ct, 0:n], start=(ct == 0), stop=(ct == NCT - 1))
                B.act(RS[:, c0:c0 + n], p[:, 0:n], AF.Sqrt, bias=float(D * EPS), scale=1.0)
                B.recip(RS[:, c0:c0 + n], RS[:, c0:c0 + n])
                for ct in range(NCT):
                    B.stt(YO[:, ct, 0:n], X[:, ct, c0:c0 + n], vec(4, ct), RS[:, c0:c0 + n], ALU.mult, ALU.mult)
                if c0 < 1024:
                    store_tokens(YO, 0, n, yp[blk["p_off"] + c0: blk["p_off"] + c0 + n, :], stg)
                else:
                    store_tokens(YO, 0, n, ys, stg)
    B.P.emit(nc)
    st.close()
    return nc


_NC_CACHE = {}
_BUILD_KW = {}


def kernel(**inp):
    f32 = np.float32
    n = 8
    if "nc" not in _NC_CACHE:
        _NC_CACHE["nc"] = build(**_BUILD_KW)
    nc = _NC_CACHE["nc"]
    c = lambda a: np.ascontiguousarray(np.asarray(a, dtype=f32))
    shared = {
        "ident": np.eye(128, dtype=f32),
        "norm_mix": c(inp["norm_mix"]), "norm_ffn": c(inp["norm_ffn"]), "norm_final": c(inp["norm_final"]).reshape(1, D),
        "w_pw1": c(inp["conv_w_pw1"][0]), "b_pw1": c(inp["conv_b_pw1"]).reshape(2, D), "w_dw": c(inp["conv_w_dw"][0]),
        "b_dw": c(inp["conv_b_dw"]).reshape(1, D), "ln_g": c(inp["conv_ln_g"]).reshape(1, D),
        "ln_b": c(inp["conv_ln_b"]).reshape(1, D), "w_pw2": c(inp["conv_w_pw2"][0]),
        "lam_re": c(inp["ssm_lam_re"][0]), "lam_im": c(inp["ssm_lam_im"][0]), "log_dt": c(inp["ssm_log_dt"]).reshape(NG, 1),
        "b_re": c(inp["ssm_b_re"]).reshape(NG * 64, 16), "b_im": c(inp["ssm_b_im"]).reshape(NG * 64, 16),
        "c_re": c(inp["ssm_c_re"]).reshape(NG * 16, 64), "c_im": c(inp["ssm_c_im"]).reshape(NG * 16, 64),
        "ssm_d": c(inp["ssm_d"]).reshape(1, D), "w_glu": c(inp["ssm_w_glu"][0]), "b_glu": c(inp["ssm_b_glu"]).reshape(2, D),
        "wg": c(inp["ffn_w_gate"]), "wu": c(inp["ffn_w_up"]), "wd": c(inp["ffn_w_down"]),
    }
    xpr = c(inp["x_prompt"]); xsa = c(inp["x_sample"]); ccv = c(inp["cache_conv"])
    s_re = c(inp["state_ssm_re"]); s_im = c(inp["state_ssm_im"])
    in_maps = []
    for i in range(n):
        m = dict(shared)
        m["xp"] = xpr[i]
        m["xs"] = xsa[16 * i:16 * i + 16].reshape(128, D)
        m["cc"] = ccv[0, 16 * i:16 * i + 16].reshape(480, D)
        m["sre"] = s_re[0, 16 * i:16 * i + 16].reshape(16, 4096)
        m["sim"] = s_im[0, 16 * i:16 * i + 16].reshape(16, 4096)
        in_maps.append(m)
    res = run_bass_kernel_spmd(nc, in_maps, core_ids=list(range(n)))
    R = res.results
    y_prompt = np.stack([R[i]["yp"] for i in range(n)]).astype(f32)
    y_sample = np.concatenate([R[i]["ys"].reshape(16, 8, D) for i in range(n)]).astype(f32)
    conv_p = np.stack([R[i]["convp"] for i in range(n)])[None].astype(f32)
    conv_s = np.concatenate([R[i]["convs"].reshape(16, 30, D) for i in range(n)])[None].astype(f32)
    sp_re = np.stack([R[i]["sspr"] for i in range(n)])[None].astype(f32)
    sp_im = np.stack([R[i]["sspi"] for i in range(n)])[None].astype(f32)
    ss_re = np.concatenate([R[i]["sssr"].reshape(16, 64, 64) for i in range(n)])[None].astype(f32)
    ss_im = np.concatenate([R[i]["sssi"].reshape(16, 64, 64) for i in range(n)])[None].astype(f32)
    return (y_prompt, y_sample, conv_p, conv_s, sp_re, sp_im, ss_re, ss_im)
```

```python
import math
import numpy as np
import concourse.bass as bass
import concourse.mybir as mybir
from concourse.bass_utils import run_bass_kernel_spmd

F32 = mybir.dt.float32
BF16 = mybir.dt.bfloat16
I32 = mybir.dt.int32
AF = mybir.ActivationFunctionType
ALU = mybir.AluOpType

D = 1024
NCT = 8
DFF = 2816
NFC = 22
SEQ = 2048
NSEQ_S = 16
TS = 8
KW = 31
EPS = 1e-6
T = 8
NG = 64
NPAIR = 32

_ESZ = {F32: 4, BF16: 2, I32: 4}


def _esz(dt):
    return _ESZ[dt]


class Op:
    __slots__ = ("eng", "fn", "reads", "writes", "dma", "deps", "sig", "sigval", "dsem", "dval", "prewait")

    def __init__(self, eng, fn, reads, writes, dma):
        self.eng = eng
        self.fn = fn
        self.reads = reads
        self.writes = writes
        self.dma = dma
        self.deps = set()
        self.sig = False
        self.sigval = 0
        self.dsem = None
        self.dval = 0
        self.prewait = None


def _acc(ap):
    t = ap.tensor
    name = t.name
    pat = ap.ap
    esz = _esz(ap.dtype)
    off = ap.offset
    sp = str(ap.space)
    if sp not in ("SB", "PSUM"):
        lo = off
        hi = off
        for (s, c) in pat:
            if s >= 0:
                hi += (c - 1) * s
            else:
                lo += (c - 1) * s
        return (name, 0, 1, lo * esz, (hi + 1) * esz)
    ps = pat[0][0]
    pc = pat[0][1]
    if ps == 0:
        p0 = 0
        f0 = off
        pc = 128
    else:
        p0 = off // ps
        f0 = off % ps
    lo = f0
    hi = f0
    for (s, c) in pat[1:]:
        if s >= 0:
            hi += (c - 1) * s
        else:
            lo += (c - 1) * s
    if sp == "PSUM":
        return (name, (p0 // 32) * 32, ((p0 + pc + 31) // 32) * 32, 0, 1 << 20)
    return (name, p0, p0 + pc, lo * esz, (hi + 1) * esz)


def _acc_multi(ap):
    base = _acc(ap)
    sp = str(ap.space)
    if sp != "SB":
        return [base]
    pat = ap.ap
    if len(pat) < 3 or pat[0][0] == 0:
        return [base]
    esz = _esz(ap.dtype)
    f0 = ap.offset % pat[0][0]
    dims = sorted([(s, c) for (s, c) in pat[1:] if c > 1], key=lambda d: -abs(d[0]))
    name, p0, p1, _, _ = base

    def extent(ds):
        lo = hi = 0
        for (s, c) in ds:
            if s >= 0:
                hi += (c - 1) * s
            else:
                lo += (c - 1) * s
        return lo, hi

    out = []

    def rec(ds, off, budget):
        if ds:
            (s, c) = ds[0]
            lo, hi = extent(ds[1:])
            inner = hi - lo + 1
            if c <= budget and abs(s) > inner:
                for i in range(c):
                    rec(ds[1:], off + i * s, budget // c)
                return
        lo, hi = extent(ds)
        out.append((name, p0, p1, (off + lo) * esz, (off + hi + 1) * esz))

    rec(dims, f0, 64)
    return out


class Prog:
    ENGS = ("pe", "act", "dve", "pool", "sp")

    def __init__(self):
        self.ops = []
        self.recs = {}

    def _track(self, idx, op):
        for (aps, is_w0) in ((op.reads, False), (op.writes, True)):
            for (ap, name, p0, p1, b0, b1) in [(ap_,) + iv for ap_ in aps for iv in _acc_multi(ap_)]:
                ap_space = str(ap.space)
                is_w = is_w0 or (ap_space == "PSUM")
                lst = self.recs.setdefault(name, [])
                keep = []
                for r in lst:
                    ov = not (r[1] <= p0 or p1 <= r[0] or r[3] <= b0 or b1 <= r[2])
                    if ov and (is_w or r[5]):
                        if r[4] != idx:
                            op.deps.add(r[4])
                        if is_w and r[0] >= p0 and r[1] <= p1 and r[2] >= b0 and r[3] <= b1 and r[4] != idx:
                            continue
                    keep.append(r)
                if not is_w and not op.dma:
                    keep = [r for r in keep if not ((not r[5]) and r[0] == p0 and r[1] == p1 and r[2] == b0
                                                    and r[3] == b1 and (not self.ops[r[4]].dma)
                                                    and self.ops[r[4]].eng == op.eng)]
                keep.append([p0, p1, b0, b1, idx, is_w])
                self.recs[name] = keep

    def add(self, eng, fn, reads, writes, dma=False):
        op = Op(eng, fn, list(reads), list(writes), dma)
        idx = len(self.ops)
        self.ops.append(op)
        self._track(idx, op)
        return op

    def emit(self, nc, nsem_dma=12):
        ops = self.ops
        for op in ops:
            nd = set()
            for d in op.deps:
                p = ops[d]
                if (not p.dma) and p.eng == "pe" and op.eng == "pe" and not op.dma:
                    continue
                nd.add(d)
            latest = {}
            nd2 = set()
            for d in nd:
                p = ops[d]
                if p.dma:
                    nd2.add(d)
                else:
                    if p.eng not in latest or d > latest[p.eng]:
                        latest[p.eng] = d
            nd2.update(latest.values())
            op.deps = nd2
            for d in nd2:
                if not ops[d].dma:
                    ops[d].sig = True
        cnt = {e: 0 for e in self.ENGS}
        ndma = 0
        dma_last = {}
        npool_dma = 0
        for op in ops:
            if op.dma and op.eng == "pool":
                op.dsem = ("p", npool_dma)
                op.dval = 16
                npool_dma += 1
            elif op.dma:
                s = ndma % nsem_dma
                op.dsem = s
                op.dval = 16 * (ndma // nsem_dma + 1)
                if s in dma_last:
                    op.prewait = dma_last[s]
                dma_last[s] = (s, op.dval)
                ndma += 1
            elif op.sig:
                cnt[op.eng] += 1
                op.sigval = cnt[op.eng]
        SEMCAP = 1000
        nsem_e = {e: max(1, (cnt[e] + SEMCAP - 1) // SEMCAP) for e in self.ENGS}
        print("sem counts", cnt, "ndma", ndma, "npool_dma", npool_dma, "nops", len(ops))
        import contextlib
        with contextlib.ExitStack() as st:
            esem = {e: [st.enter_context(nc.semaphore("s_%s%d" % (e, i))) for i in range(nsem_e[e])] for e in self.ENGS}
            dsem = {i: st.enter_context(nc.semaphore("d%d" % i)) for i in range(nsem_dma)}
            for i in range(npool_dma):
                dsem[("p", i)] = st.enter_context(nc.semaphore("q%d" % i))
            block = st.enter_context(nc.Block())
            final_d = dict(dma_last)

            def body(ename):
                def run(e):
                    waited = {}

                    def wait(sem, key, val):
                        if waited.get(key, 0) >= val:
                            return
                        waited[key] = val
                        e.wait_ge(sem, val)

                    for op in ops:
                        if op.eng != ename:
                            continue
                        for d in sorted(op.deps):
                            p = ops[d]
                            if p.dma:
                                wait(dsem[p.dsem], ("d", p.dsem), p.dval)
                            else:
                                si_ = (p.sigval - 1) // SEMCAP
                                wait(esem[p.eng][si_], ("e", p.eng, si_), (p.sigval - 1) % SEMCAP + 1)
                        if op.dma and op.prewait is not None:
                            wait(dsem[op.prewait[0]], ("d", op.prewait[0]), op.prewait[1])
                        ins = op.fn(e)
                        if op.dma:
                            ins.then_inc(dsem[op.dsem], 16)
                        elif op.sig:
                            ins.then_inc(esem[op.eng][(op.sigval - 1) // SEMCAP], 1)
                    if ename == "sp":
                        for s, (si, v) in final_d.items():
                            wait(dsem[si], ("d", si), v)
                        for i in range(npool_dma):
                            wait(dsem[("p", i)], ("d", ("p", i)), 16)
                        for en in self.ENGS:
                            if en != "sp" and cnt[en] > 0:
                                si_ = (cnt[en] - 1) // SEMCAP
                                wait(esem[en][si_], ("e", en, si_), (cnt[en] - 1) % SEMCAP + 1)
                return run

            block.tensor(body("pe"))
            block.scalar(body("act"))
            block.vector(body("dve"))
            block.gpsimd(body("pool"))
            block.sync(body("sp"))


class Builder:
    def __init__(self, nc, stages):
        self.nc = nc
        self.P = Prog()
        self.stages = stages
        self.psi = 0

    def mm(self, out, lhsT, rhs, start=True, stop=True, tp=None):
        kw = {}
        if tp is not None:
            kw["tile_position"] = tp
        self.P.add("pe", lambda e: e.matmul(out, lhsT, rhs, start=start, stop=stop, **kw), [lhsT, rhs], [out])

    def tr(self, out, in_, ident, tp=None):
        kw = {}
        if tp is not None:
            kw["tile_position"] = tp
        self.P.add("pe", lambda e: e.transpose(out, in_, ident, **kw), [in_, ident], [out])

    def act(self, out, in_, func, bias=None, scale=None, eng="act"):
        kw = {}
        rd = [in_]
        if bias is not None:
            kw["bias"] = bias
            if not isinstance(bias, (int, float)):
                rd.append(bias)
        if scale is not None:
            kw["scale"] = scale
            if not isinstance(scale, (int, float)):
                rd.append(scale)
        self.P.add("act", lambda e: e.activation(out, in_, func, **kw), rd, [out])

    def tt(self, out, a, b, op, eng="dve"):
        self.P.add(eng, lambda e: e.tensor_tensor(out, a, b, op), [a, b], [out])

    def ts(self, out, a, s1, s2, op0, op1=None, eng="dve"):
        rd = [a]
        for s in (s1, s2):
            if s is not None and not isinstance(s, (int, float)):
                rd.append(s)
        if op1 is None:
            self.P.add(eng, lambda e: e.tensor_scalar(out, a, s1, None, op0), rd, [out])
        else:
            self.P.add(eng, lambda e: e.tensor_scalar(out, a, s1, s2, op0, op1), rd, [out])

    def stt(self, out, a, s, b, op0, op1, eng="dve"):
        rd = [a, b]
        if not isinstance(s, (int, float)):
            rd.append(s)
        self.P.add(eng, lambda e: e.scalar_tensor_tensor(out, a, s, b, op0, op1), rd, [out])

    def cp(self, out, in_, eng="dve"):
        if eng == "act":
            self.P.add("act", lambda e: e.activation(out, in_, AF.Copy), [in_], [out])
        else:
            self.P.add(eng, lambda e: e.tensor_copy(out, in_), [in_], [out])

    def memset(self, ap, v, eng="pool"):
        self.P.add(eng, lambda e: e.memset(ap, v), [], [ap])

    def recip(self, out, in_):
        self.P.add("dve", lambda e: e.reciprocal(out, in_), [in_], [out])

    def dma(self, out, in_, eng="sp"):
        self.P.add(eng, lambda e: e.dma_start(out=out, in_=in_), [in_], [out], dma=True)

    def ps(self):
        p = self.psum[self.psi % 8]
        self.psi += 1
        return p


class Arena:
    def __init__(self, handle, base, limit):
        self.h = handle
        self.off = base
        self.limit = limit

    def alloc(self, nbytes):
        nbytes = (nbytes + 63) // 64 * 64
        o = self.off
        self.off += nbytes
        assert self.off <= self.limit, (self.off, self.limit)
        return o


ARENA_BYTES = 206848
NTMAX = 1152
NV = 44


def build(stages=("conv", "ffn0", "ssm", "ffn1"), final_norm=True):
    nc = bass.Bass("TRN2", target_bir_lowering=False)
    B = Builder(nc, stages)

    def din(name, shape, dt=F32):
        return nc.dram_tensor(name, shape, dt, kind="ExternalInput").ap()

    def dout(name, shape, dt=F32):
        return nc.dram_tensor(name, shape, dt, kind="ExternalOutput").ap()

    xp = din("xp", [SEQ, D]); xs = din("xs", [128, D]); cc = din("cc", [480, D])
    sre = din("sre", [NSEQ_S, 4096]); sim = din("sim", [NSEQ_S, 4096])
    ident_d = din("ident", [128, 128])
    norm_mix = din("norm_mix", [2, D]); norm_ffn = din("norm_ffn", [2, D]); norm_final = din("norm_final", [1, D])
    w_pw1 = din("w_pw1", [D, 2 * D]); b_pw1 = din("b_pw1", [2, D]); w_dw = din("w_dw", [KW, D])
    b_dw = din("b_dw", [1, D]); ln_g = din("ln_g", [1, D]); ln_b = din("ln_b", [1, D]); w_pw2 = din("w_pw2", [D, D])
    lam_re = din("lam_re", [NG, 64]); lam_im = din("lam_im", [NG, 64]); log_dt = din("log_dt", [NG, 1])
    b_re = din("b_re", [NG * 64, 16]); b_im = din("b_im", [NG * 64, 16])
    c_re = din("c_re", [NG * 16, 64]); c_im = din("c_im", [NG * 16, 64])
    ssm_d = din("ssm_d", [1, D]); w_glu = din("w_glu", [D, 2 * D]); b_glu = din("b_glu", [2, D])
    wg = din("wg", [2, D, DFF]); wu = din("wu", [2, D, DFF]); wd = din("wd", [2, DFF, D])

    yp = dout("yp", [SEQ, D]); ys = dout("ys", [128, D])
    convp = dout("convp", [KW - 1, D]); convs = dout("convs", [480, D])
    sspr = dout("sspr", [NG, 64]); sspi = dout("sspi", [NG, 64])
    sssr = dout("sssr", [NSEQ_S, 4096]); sssi = dout("sssi", [NSEQ_S, 4096])

    import contextlib
    st = contextlib.ExitStack()
    arena_h = st.enter_context(nc.sbuf_tensor("arena", [128, ARENA_BYTES // 4], F32))
    B.psum = [st.enter_context(nc.psum_tensor("ps%d" % i, [128, 512], F32)) for i in range(8)]

    def V(off, shape, dt=F32):
        n = 1
        for s in shape:
            n *= s
        nb = n * _esz(dt)
        assert off % 4 == 0 and nb % 4 == 0
        ap = arena_h[:, off // 4: off // 4 + nb // 4]
        if dt != F32:
            ap = ap.bitcast(dt)
        if len(shape) == 2:
            ap = ap.rearrange("p (a b) -> p a b", b=shape[1])
        elif len(shape) == 3:
            ap = ap.rearrange("p (a b c) -> p a b c", b=shape[1], c=shape[2])
        elif len(shape) == 4:
            ap = ap.rearrange("p (a b c d) -> p a b c d", b=shape[1], c=shape[2], d=shape[3])
        return ap

    A = Arena(arena_h, 0, ARENA_BYTES)
    o_x = A.alloc(NCT * NTMAX * 4)
    o_h = A.alloc(NCT * NTMAX * 2)
    o_rs = A.alloc(NTMAX * 4)
    o_idf = A.alloc(128 * 4)
    o_idb = A.alloc(128 * 2)
    o_one = A.alloc(128 * 2)
    o_vec = A.alloc(NCT * NV * 4)
    o_small = A.alloc(2048)
    GEN0 = A.off
    X = V(o_x, [NCT, NTMAX]); H = V(o_h, [NCT, NTMAX], BF16); RS = V(o_rs, [NTMAX])
    IDF = V(o_idf, [128]); IDB = V(o_idb, [128], BF16); ONE = V(o_one, [128], BF16)
    VEC = V(o_vec, [NCT, NV])

    def vec(v, ct):
        return VEC[:, ct, v:v + 1]

    B.dma(IDF, ident_d)
    B.cp(IDB, IDF, eng="dve")
    B.memset(ONE, 1.0, eng="dve")
    G = Arena(arena_h, GEN0, ARENA_BYTES)
    o_vt = G.alloc(D * 4)
    VT = V(o_vt, [D])
    rows = [(norm_mix, 0, 2), (norm_ffn, 2, 2), (norm_final, 4, 1), (b_pw1, 5, 2), (b_dw, 7, 1), (ln_g, 8, 1),
            (ln_b, 9, 1), (ssm_d, 10, 1), (b_glu, 11, 2), (w_dw, 13, KW)]
    for (src, r0, n) in rows:
        B.dma(VT[r0:r0 + n, :], src)
    for ct in range(NCT):
        p = B.ps()
        B.tr(p[:, 0:NV], VT[0:NV, ct * 128:(ct + 1) * 128], IDF[0:NV, 0:NV])
        B.cp(VEC[:, ct, :], p[:, 0:NV], eng="dve")
    B.ts(VEC[:, :, 0:5], VEC[:, :, 0:5], math.sqrt(D), None, ALU.mult)

    def subs_of(nt):
        out = []
        c = 0
        while c < nt:
            n = min(512, nt - c)
            out.append((c, n))
            c += n
        return out

    def subs_tok(nt):
        if nt == 1152:
            return [(0, 384), (384, 384), (768, 384)]
        return subs_of(nt)

    def load_x(src_rows, c0, ntok, stage_off):
        ntile = ntok // 128
        for ti in range(ntile):
            so = stage_off[ti % 2]
            XIN = V(so, [D])
            B.dma(XIN, src_rows[ti * 128:(ti + 1) * 128, :])
            for half in range(2):
                p = B.ps()
                for j in range(4):
                    ct = half * 4 + j
                    B.tr(p[:, j * 128:(j + 1) * 128], XIN[:, ct * 128:(ct + 1) * 128], IDF)
                B.cp(X[:, half * 4:half * 4 + 4, c0 + ti * 128: c0 + (ti + 1) * 128],
                     p.rearrange("p (a b) -> p a b", b=128), eng="act")

    prenormed = [False]
    STAGE_ORDER = [s_ for s_ in ("conv", "ffn0", "ssm", "ffn1") if s_ in stages]
    NEXT_GIDX = {"ffn0": 2, "ssm": 1, "ffn1": 3}

    def rmsnorm_sub(c0, n, gidx, sq_off):
        SQ = V(sq_off, [NCT, 512], BF16)
        for ct in range(NCT):
            B.act(SQ[:, ct, 0:n], X[:, ct, c0:c0 + n], AF.Square)
        p = B.ps()
        for ct in range(NCT):
            B.mm(p[:, 0:n], ONE, SQ[:, ct, 0:n], start=(ct == 0), stop=(ct == NCT - 1))
        B.act(RS[:, c0:c0 + n], p[:, 0:n], AF.Sqrt, bias=float(D * EPS), scale=1.0)
        B.recip(RS[:, c0:c0 + n], RS[:, c0:c0 + n])
        for ct in range(NCT):
            B.stt(H[:, ct, c0:c0 + n], X[:, ct, c0:c0 + n], vec(gidx, ct), RS[:, c0:c0 + n], ALU.mult, ALU.mult)

    def rmsnorm(nt, gidx, sq_off, dst=None, f32dst=None):
        if prenormed[0]:
            prenormed[0] = False
            return
        for (c0, n) in subs_tok(nt):
            rmsnorm_sub(c0, n, gidx, sq_off)

    def tail_norm_fn(stage_name, sq_off):
        i_ = STAGE_ORDER.index(stage_name)
        if i_ + 1 >= len(STAGE_ORDER):
            return None
        gidx = NEXT_GIDX[STAGE_ORDER[i_ + 1]]
        prenormed[0] = True
        return lambda c0, n: rmsnorm_sub(c0, n, gidx, sq_off)

    def store_tokens(src_fm, c0, ntok, dst_rows, stage_off):
        ntile = (ntok + 127) // 128
        for ti in range(ntile):
            n = min(128, ntok - ti * 128)
            so = stage_off[ti % 2]
            XO = V(so, [D])
            for half in range(2):
                p = B.ps()
                for j in range(4):
                    ct = half * 4 + j
                    B.tr(p[0:n, j * 128:(j + 1) * 128], src_fm[:, ct, c0 + ti * 128: c0 + ti * 128 + n], IDF)
                B.cp(XO[0:n, half * 512:(half + 1) * 512], p[0:n, :], eng="act")
            B.dma(dst_rows[ti * 128: ti * 128 + n, :], XO[0:n, :])

    tables_done = []

    def build_tables():
        if tables_done or "ssm" not in stages:
            return
        tables_done.append(1)
        PCs = V(o_h, [16, 2, 32]); PSs = V(o_h + 4096, [16, 2, 32])
        Lt1 = V(o_h + 8192, [2, 32]); Lt2 = V(o_h + 8192 + 256, [2, 32])
        B.cp(PCs[:, 0, :, :], ACt, eng="dve")
        B.cp(PSs[:, 0, :, :], ASt, eng="dve")
        for k in range(1, 16):
            U = PCs[:, k - 1, :, :]; W = PSs[:, k - 1, :, :]
            B.tt(Lt1, U, ACt, ALU.mult)
            B.tt(Lt2, W, ASt, ALU.mult)
            B.tt(PCs[:, k, :, :], Lt1, Lt2, ALU.subtract)
            B.tt(Lt1, U, ASt, ALU.mult)
            B.tt(Lt2, W, ACt, ALU.mult)
            B.tt(PSs[:, k, :, :], Lt1, Lt2, ALU.add)
        B.dma(scrP[:, 0:1024], PCs.rearrange("p a b c -> p (a b c)"))
        B.dma(scrP[:, 1024:2048], PSs.rearrange("p a b c -> p (a b c)"))

    def ffn(layer, nt, G):
        subs = subs_tok(nt)
        o_act = G.alloc(NFC * nt * 2)
        o_wd = G.alloc(NFC * D * 2)
        o_slab = [G.alloc(2 * NCT * 512 * 2) for _ in range(2)]
        o_sg = [G.alloc(2048) for _ in range(2)]
        ACTB = V(o_act, [NFC, nt], BF16)
        WD = V(o_wd, [NFC, D], BF16)
        rmsnorm(nt, 2 + layer, o_wd)
        wgv = wg[layer].rearrange("(k p) n -> p k n", p=128)
        wuv = wu[layer].rearrange("(k p) n -> p k n", p=128)
        wdv = wd[layer].rearrange("(f p) n -> p f n", p=128)
        slabs = []
        c = 0
        while c < DFF:
            w = min(512, DFF - c)
            slabs.append((c, w))
            c += w
        cnt = 0
        for si, (col0, w) in enumerate(slabs):
            SL = V(o_slab[si % 2], [2, NCT, 512], BF16)
            B.dma(SL[:, 0, :, 0:w], wgv[:, :, col0:col0 + w], eng="pool")
            B.dma(SL[:, 1, :, 0:w], wuv[:, :, col0:col0 + w], eng="pool")
            if si == 1:
                B.dma(WD[:, 0:11, :], wdv[:, 0:11, :], eng="pool")
            if si == 2:
                B.dma(WD[:, 11:22, :], wdv[:, 11:22, :], eng="pool")
            tiles = ([(j, c0, n) for (c0, n) in subs for j in range(w // 128)] if si == 0
                     else [(j, c0, n) for j in range(w // 128) for (c0, n) in subs])
            for (j, c0, n) in tiles:
                fc = col0 // 128 + j
                if True:
                    pg = B.ps()
                    pu = B.ps()
                    for k in range(NCT):
                        B.mm(pg[:, 0:n], SL[:, 0, k, j * 128:(j + 1) * 128], H[:, k, c0:c0 + n], start=(k == 0), stop=(k == NCT - 1))
                    for k in range(NCT):
                        B.mm(pu[:, 0:n], SL[:, 1, k, j * 128:(j + 1) * 128], H[:, k, c0:c0 + n], start=(k == 0), stop=(k == NCT - 1))
                    SG = V(o_sg[cnt % 2], [512])
                    cnt += 1
                    B.act(SG[:, 0:n], pg[:, 0:n], AF.Silu)
                    B.tt(ACTB[:, fc, c0:c0 + n], SG[:, 0:n], pu[:, 0:n], ALU.mult)
        if layer == 0:
            build_tables()
        tn = tail_norm_fn("ffn%d" % layer, o_slab[0])
        for (c0, n) in subs:
            for ct in range(NCT):
                p = B.ps()
                for fc in range(NFC):
                    B.mm(p[:, 0:n], WD[:, fc, ct * 128:(ct + 1) * 128], ACTB[:, fc, c0:c0 + n], start=(fc == 0), stop=(fc == NFC - 1))
                B.tt(X[:, ct, c0:c0 + n], X[:, ct, c0:c0 + n], p[:, 0:n], ALU.add)
            if tn is not None:
                tn(c0, n)

    VCARRY = V(o_small, [NCT, 30], BF16)
    WDB = V(o_small + 1536, [NCT, KW + 1], BF16)[:, :, 0:KW]
    B.cp(WDB, VEC[:, :, 13:13 + KW], eng="dve")
    w1v = w_pw1.rearrange("(k p) n -> p k n", p=128)
    w2v = w_pw2.rearrange("(k p) n -> p k n", p=128)
    VWMAX = 30 + 1024 + NSEQ_S * 38
    SB0 = 30 + 1024

    def conv_layer(bi, blk, nt, G):
        subs = subs_of(nt)
        o_vb = G.alloc(NCT * VWMAX * 2)
        o_w1 = ARENA_BYTES - 49152
        o_w2 = ARENA_BYTES - 16384
        G.limit = o_w1
        o_yf = G.alloc(NCT * 512 * 4)
        o_ybf = G.alloc(NCT * 512 * 2)
        o_ysq = G.alloc(NCT * 512 * 2)
        o_dg = [G.alloc(KW * 128 * 2) for _ in range(2)]
        o_sig = [G.alloc(2048) for _ in range(2)]
        o_stat = [G.alloc(2048) for _ in range(4)]
        o_vo = G.alloc(NCT * 160 * 4)
        VB = V(o_vb, [NCT, VWMAX], BF16)
        W1 = V(o_w1, [NCT, 2048], BF16)
        W2 = V(o_w2, [NCT, 1024], BF16)
        YF = V(o_yf, [NCT, 512])
        YBF = V(o_ybf, [NCT, 512], BF16)
        YSQ = V(o_ysq, [NCT, 512], BF16)
        VO = V(o_vo, [NCT, 160])
        MEAN = V(o_stat[0], [512]); TMP = V(o_stat[1], [512]); RSTD = V(o_stat[2], [512]); MR = V(o_stat[3], [512])
        stg = [o_ybf, o_ybf + 4096]
        for s in (0, 2, 1, 3):
            B.dma(W1[:, :, s * 512:(s + 1) * 512], w1v[:, :, s * 512:(s + 1) * 512], eng="pool")
        for s in range(2):
            B.dma(W2[:, :, s * 512:(s + 1) * 512], w2v[:, :, s * 512:(s + 1) * 512], eng="pool")
        rmsnorm(nt, 0, o_yf)
        if bi == 0:
            B.memset(VB[:, :, 0:30], 0.0, eng="pool")
            for i in range(4):
                XIN = V(stg[i % 2], [D])
                B.dma(XIN[0:120, :], cc[120 * i:120 * i + 120, :])
                for half in range(2):
                    p = B.ps()
                    for j in range(4):
                        ct = half * 4 + j
                        B.tr(p[:, j * 128:j * 128 + 120], XIN[0:120, ct * 128:(ct + 1) * 128], IDF[0:120, 0:120])
                    for j in range(4):
                        ct = half * 4 + j
                        dst = VB[:, ct, SB0 + 152 * i: SB0 + 152 * i + 152].rearrange("p (s k) -> p s k", k=38)[:, :, 0:30]
                        B.cp(dst, p[:, j * 128:j * 128 + 120].rearrange("p (s k) -> p s k", k=30), eng="dve")
        else:
            B.cp(VB[:, :, 0:30], VCARRY, eng="dve")
        cnt = 0
        for ct in range(NCT):
            for (c0, n) in subs:
                pa = B.ps()
                pb = B.ps()
                for k in range(NCT):
                    B.mm(pa[:, 0:n], W1[:, k, ct * 128:(ct + 1) * 128], H[:, k, c0:c0 + n], start=(k == 0), stop=(k == NCT - 1))
                for k in range(NCT):
                    B.mm(pb[:, 0:n], W1[:, k, 1024 + ct * 128:1024 + (ct + 1) * 128], H[:, k, c0:c0 + n], start=(k == 0), stop=(k == NCT - 1))
                SIG = V(o_sig[cnt % 2], [512])
                cnt += 1
                B.act(SIG[:, 0:n], pb[:, 0:n], AF.Sigmoid, bias=vec(6, ct), scale=1.0)
                if c0 < 1024:
                    B.stt(VB[:, ct, 30 + c0:30 + c0 + n], pa[:, 0:n], vec(5, ct), SIG[:, 0:n], ALU.add, ALU.mult)
                    if bi == 1 and c0 + n == 1024:
                        B.stt(VO[:, ct, 0:30], pa[:, n - 30:n], vec(5, ct), SIG[:, n - 30:n], ALU.add, ALU.mult)
                else:
                    dst = VB[:, ct, SB0:SB0 + 608].rearrange("p (s k) -> p s k", k=38)[:, :, 30:38]
                    B.stt(dst, pa[:, 0:n].rearrange("p (s t) -> p s t", t=8), vec(5, ct),
                          SIG[:, 0:n].rearrange("p (s t) -> p s t", t=8), ALU.add, ALU.mult)
                    B.stt(VO[:, ct, 32:160], pa[:, 0:n], vec(5, ct), SIG[:, 0:n], ALU.add, ALU.mult)
        o_dgr = list(o_dg) + [o_w1 + 8192 * i_ for i_ in range(4)]
        NDG = len(o_dgr)
        NAHEAD = NDG - 1

        def gen_dg(i):
            ct_ = i % NCT
            DG_ = V(o_dgr[i % NDG], [KW, 128], BF16)
            idb_b = bass.AP(IDB.tensor, IDB.offset, [list(IDB.ap[0]), [0, KW], [1, 128]])
            wsl = WDB[:, ct_, :]
            w_b = bass.AP(wsl.tensor, wsl.offset, [list(wsl.ap[0]), [1, KW], [0, 128]])
            B.tt(DG_, idb_b, w_b, ALU.mult)

        dgc = 0
        ndg = len(subs) * NCT
        for i_ in range(min(NAHEAD, ndg)):
            gen_dg(i_)
        for (c0, n) in subs:
            for ct in range(NCT):
                DG = V(o_dgr[dgc % NDG], [KW, 128], BF16)
                if dgc + NAHEAD < ndg:
                    gen_dg(dgc + NAHEAD)
                dgc += 1
                p = B.ps()
                for k in range(KW):
                    if c0 < 1024:
                        rhs = VB[:, ct, c0 + k:c0 + k + n]
                        out = p[:, 0:n]
                    else:
                        rhs = VB[:, ct, SB0:SB0 + 608].rearrange("p (s k) -> p s k", k=38)[:, :, k:k + 8]
                        out = p[:, 0:n].rearrange("p (s t) -> p s t", t=8)
                    B.mm(out, DG[:, k, :], rhs, start=(k == 0), stop=(k == KW - 1))
                B.act(YF[:, ct, 0:n], p[:, 0:n], AF.Identity, bias=vec(7, ct), scale=1.0)
                B.act(YSQ[:, ct, 0:n], p[:, 0:n], AF.Square, bias=vec(7, ct), scale=1.0)
                B.act(YBF[:, ct, 0:n], p[:, 0:n], AF.Identity, bias=vec(7, ct), scale=1.0)
            p1 = B.ps()
            p2 = B.ps()
            for ct in range(NCT):
                B.mm(p1[:, 0:n], ONE, YBF[:, ct, 0:n], start=(ct == 0), stop=(ct == NCT - 1))
            for ct in range(NCT):
                B.mm(p2[:, 0:n], ONE, YSQ[:, ct, 0:n], start=(ct == 0), stop=(ct == NCT - 1))
            B.ts(MEAN[:, 0:n], p1[:, 0:n], 1.0 / D, None, ALU.mult)
            B.tt(TMP[:, 0:n], MEAN[:, 0:n], MEAN[:, 0:n], ALU.mult)
            B.stt(TMP[:, 0:n], p2[:, 0:n], 1.0 / D, TMP[:, 0:n], ALU.mult, ALU.subtract)
            B.act(RSTD[:, 0:n], TMP[:, 0:n], AF.Sqrt, bias=float(EPS), scale=1.0)
            B.recip(RSTD[:, 0:n], RSTD[:, 0:n])
            B.tt(MR[:, 0:n], MEAN[:, 0:n], RSTD[:, 0:n], ALU.mult)
            for ct in range(NCT):
                B.tt(YF[:, ct, 0:n], YF[:, ct, 0:n], RSTD[:, 0:n], ALU.mult)
                B.tt(YF[:, ct, 0:n], YF[:, ct, 0:n], MR[:, 0:n], ALU.subtract)
                B.act(H[:, ct, c0:c0 + n], YF[:, ct, 0:n], AF.Silu, bias=vec(9, ct), scale=vec(8, ct))
        tn = tail_norm_fn("conv", o_ysq)
        for (c0, n) in subs_tok(nt):
            for ct in range(NCT):
                p = B.ps()
                for k in range(NCT):
                    B.mm(p[:, 0:n], W2[:, k, ct * 128:(ct + 1) * 128], H[:, k, c0:c0 + n], start=(k == 0), stop=(k == NCT - 1))
                B.tt(X[:, ct, c0:c0 + n], X[:, ct, c0:c0 + n], p[:, 0:n], ALU.add)
            if tn is not None:
                tn(c0, n)
        if bi == 0:
            B.cp(VCARRY, VB[:, :, 1024:1054], eng="dve")
            XO = V(stg[0], [D])
            for half in range(2):
                p = B.ps()
                for j in range(4):
                    ct = half * 4 + j
                    B.tr(p[:, j * 128:(j + 1) * 128], VO[:, ct, 32:160], IDF)
                B.cp(XO[:, half * 512:(half + 1) * 512], p[:, :], eng="act")
            cs3 = convs.rearrange("(s k) d -> s k d", k=30)
            cc3 = cc.rearrange("(s k) d -> s k d", k=30)
            for s in range(NSEQ_S):
                B.dma(cs3[s, 22:30, :], XO[8 * s:8 * s + 8, :])
            B.dma(cs3[:, 0:22, :], cc3[:, 8:30, :])
        else:
            XO = V(stg[0], [D])
            for half in range(2):
                p = B.ps()
                for j in range(4):
                    ct = half * 4 + j
                    B.tr(p[0:30, j * 128:(j + 1) * 128], VO[:, ct, 0:30], IDF)
                B.cp(XO[0:30, half * 512:(half + 1) * 512], p[0:30, :], eng="act")
            B.dma(convp, XO[0:30, :])

    scrE = nc.dram_tensor("scrE", [128, 16384], BF16, kind="Internal").ap()
    scrD = nc.dram_tensor("scrD", [128, 16384], BF16, kind="Internal").ap()
    scrK = nc.dram_tensor("scrK", [128, 8192], BF16, kind="Internal").ap()
    scrP = nc.dram_tensor("scrP", [128, 2048], F32, kind="Internal").ap()
    o_ac = o_small + 512
    ACt = V(o_ac, [2, 32]); ASt = V(o_ac + 256, [2, 32]); SCARRY = V(o_ac + 512, [2, 32])
    MASK = V(o_ac + 768, [4])
    PSTR = ARENA_BYTES // 4
    NSLOT = 161

    def bc_last(ap2, n):
        return bass.AP(ap2.tensor, ap2.offset, [list(ap2.ap[0]), list(ap2.ap[1]), [0, n]])

    def ssm_setup():
        G = Arena(arena_h, GEN0, ARENA_BYTES - 49152 - 8192)
        f = lambda n: G.alloc(n)
        sm = {}
        for nm in ("LR", "LI", "DT", "MAG", "ANG", "KF", "R", "M", "SIN", "COS", "AR", "AI", "NR", "DEN", "CFR", "CFI", "t1", "t2", "PR", "PI"):
            sm[nm] = V(f(128), [32])
        KI = V(f(128), [32], I32)
        T1 = V(f(512), [128])
        T2 = V(f(512), [128])
        BZ = [V(f(2048), [32, 16]) for _ in range(2)]
        CZ = [V(f(2048), [32, 16]) for _ in range(2)]
        TA = V(f(2048), [32, 16]); TB = V(f(2048), [32, 16])
        TA2 = V(f(2048), [32, 16]); TB2 = V(f(2048), [32, 16])
        GG = [[V(f(2048), [32, 16]) for _ in range(2)] for _ in range(2)]
        FF = [[V(f(2048), [32, 16]) for _ in range(2)] for _ in range(2)]
        GB = [V(f(2048), [32, 32], BF16) for _ in range(2)]
        FBm = [V(f(4096), [2, 32, 32], BF16) for _ in range(2)]
        BTB = [V(f(2048), [32, 32], BF16), V(f(2048), [32, 32], BF16)]
        ETS = [V(f(2048), [T, 128], BF16), V(f(2048), [T, 128], BF16)]
        DTS = [V(f(4096), [32, 2, 32], BF16) for _ in range(2)]
        KT = V(o_h, [NCT, T, 128], BF16)

        for z_ in (FBm[0], FBm[1], DTS[0], DTS[1], KT):
            B.P.add("act", (lambda z_: (lambda e: e.memzero(z_)))(z_), [], [z_])

        def tposed(dst, src_ap_rows, is_col=False):
            if is_col:
                B.dma(T2[0:64, 0:1], src_ap_rows)
                B.act(T2[0:64, 0:1], T2[0:64, 0:1], AF.Exp)
                B.cp(T1[0:64, :], T2[0:64, 0:1].to_broadcast([64, 128]), eng="dve")
            else:
                B.dma(T1[0:64, 0:64], src_ap_rows)
                B.dma(T1[0:64, 64:128], src_ap_rows)
            p = B.ps()
            B.tr(p[:, 0:64], T1[0:64, :], IDF[0:64, 0:64])
            for g2 in range(2):
                B.cp(dst[64 * g2:64 * g2 + 64, :], p[64 * g2:64 * g2 + 64, g2:64:2], eng="dve")

        tposed(sm["LR"], lam_re)
        tposed(sm["LI"], lam_im)
        tposed(sm["DT"], log_dt, is_col=True)
        S = sm
        B.tt(S["MAG"], S["LR"], S["DT"], ALU.mult)
        B.act(S["MAG"], S["MAG"], AF.Exp)
        B.tt(S["ANG"], S["LI"], S["DT"], ALU.mult)
        TWO_PI = 2.0 * math.pi

        def reduce_sin(dst, src, shift):
            B.ts(S["R"], src, shift, None, ALU.add)
            B.ts(S["KF"], S["R"], 1.0 / TWO_PI, None, ALU.mult)
            B.cp(KI, S["KF"], eng="dve")
            B.cp(S["KF"], KI, eng="dve")
            B.stt(S["R"], S["KF"], -TWO_PI, S["R"], ALU.mult, ALU.add)
            B.ts(S["M"], S["R"], -math.pi, TWO_PI, ALU.is_lt, ALU.mult)
            B.tt(S["R"], S["R"], S["M"], ALU.add)
            B.ts(S["M"], S["R"], math.pi, -TWO_PI, ALU.is_gt, ALU.mult)
            B.tt(S["R"], S["R"], S["M"], ALU.add)
            B.ts(S["R"], S["R"], math.pi, -math.pi, ALU.min, ALU.max)
            B.act(dst, S["R"], AF.Sin)

        reduce_sin(S["SIN"], S["ANG"], 0.0)
        reduce_sin(S["COS"], S["ANG"], math.pi / 2)
        B.tt(S["AR"], S["MAG"], S["COS"], ALU.mult)
        B.tt(S["AI"], S["MAG"], S["SIN"], ALU.mult)
        B.ts(S["NR"], S["AR"], -1.0, None, ALU.add)
        B.tt(S["DEN"], S["LR"], S["LR"], ALU.mult)
        B.tt(S["t1"], S["LI"], S["LI"], ALU.mult)
        B.tt(S["DEN"], S["DEN"], S["t1"], ALU.add)
        B.recip(S["DEN"], S["DEN"])
        B.tt(S["t1"], S["NR"], S["LR"], ALU.mult)
        B.tt(S["t2"], S["AI"], S["LI"], ALU.mult)
        B.tt(S["t1"], S["t1"], S["t2"], ALU.add)
        B.tt(S["CFR"], S["t1"], S["DEN"], ALU.mult)
        B.tt(S["t1"], S["AI"], S["LR"], ALU.mult)
        B.tt(S["t2"], S["NR"], S["LI"], ALU.mult)
        B.tt(S["t1"], S["t1"], S["t2"], ALU.subtract)
        B.tt(S["CFI"], S["t1"], S["DEN"], ALU.mult)
        B.cp(S["PR"], S["AR"], eng="dve")
        B.cp(S["PI"], S["AI"], eng="dve")
        for _ in range(3):
            B.tt(S["t1"], S["PR"], S["PR"], ALU.mult)
            B.tt(S["t2"], S["PI"], S["PI"], ALU.mult)
            B.tt(S["M"], S["PR"], S["PI"], ALU.mult)
            B.tt(S["PR"], S["t1"], S["t2"], ALU.subtract)
            B.ts(S["PI"], S["M"], 2.0, None, ALU.mult)
        B.cp(ACt[:, 0, :], S["PR"], eng="dve")
        B.cp(ACt[:, 1, :], S["PR"], eng="dve")
        B.ts(ASt[:, 0, :], S["PI"], -1.0, None, ALU.mult)
        B.cp(ASt[:, 1, :], S["PI"], eng="dve")
        for part, src in enumerate((b_re, b_im)):
            sv = src.rearrange("(q t p) c -> t p q c", t=2, p=64)
            for g2 in range(2):
                for q8 in range(4):
                    B.dma(BZ[part][64 * g2:64 * g2 + 64, 8 * q8:8 * q8 + 8, :], sv[g2][:, 8 * q8:8 * q8 + 8, :])
        CST = V(f(8192), [2, 8, 128])
        for part, src in enumerate((c_re, c_im)):
            sv3 = src.rearrange("(rt p) x -> p rt x", p=128)
            B.dma(CST[:, part, :, 0:64], sv3)
            B.dma(CST[:, part, :, 64:128], sv3)
        for part, src in enumerate((c_re, c_im)):
            for rt in range(8):
                p = B.ps()
                B.tr(p[:, 0:128], CST[:, part, rt, :], IDF)
                for g2 in range(2):
                    srcv = p[64 * g2:64 * g2 + 64, 0:128].rearrange("p (a t c) -> p a t c", t=2, c=16)[:, :, g2, :]
                    B.cp(CZ[part][64 * g2:64 * g2 + 64, 4 * rt:4 * rt + 4, :], srcv, eng="dve")
        ARb = bc_last(S["AR"], 16); AIb = bc_last(S["AI"], 16)
        CRb = bc_last(S["CFR"], 16); CIb = bc_last(S["CFI"], 16)

        def cmul(dre, dim, sre_, sim_, br, bi_):
            B.tt(TA, sre_, br, ALU.mult)
            B.tt(TB, sim_, bi_, ALU.mult)
            B.tt(TA2, sre_, bi_, ALU.mult)
            B.tt(TB2, sim_, br, ALU.mult)
            B.tt(dre, TA, TB, ALU.subtract)
            B.tt(dim, TA2, TB2, ALU.add)

        def zcast(dst_zb, src_c, scale=None):
            for g2 in range(2):
                d_ = dst_zb[64 * g2:64 * g2 + 64, :, 16 * g2:16 * g2 + 16]
                s_ = src_c[64 * g2:64 * g2 + 64, :, :]
                if scale is None:
                    B.act(d_, s_, AF.Copy)
                else:
                    B.act(d_, s_, AF.Copy, scale=scale)

        for z_ in (GB[0], GB[1], BTB[0], BTB[1]):
            B.memset(z_, 0.0, eng="dve")
        cmul(FF[0][0], FF[0][1], BZ[0], BZ[1], CRb, CIb)
        zcast(BTB[0], FF[0][0])
        zcast(BTB[1], FF[0][1], scale=-1.0)
        BTP = [V(f(4096), [32, 64], BF16) for _ in range(2)]
        for i_ in range(2):
            B.memset(BTP[i_], 0.0, eng="dve")
            B.cp(BTP[i_][:, :, 32:64], BTB[i_], eng="dve")
        B.cp(GG[0][0], CZ[0], eng="act")
        B.cp(GG[0][1], CZ[1], eng="act")
        IDBq = IDB
        for m in range(T + 1):
            cur = m % 2
            nxt = (m + 1) % 2
            Gc = GG[cur]
            if m < T:
                Fc = FF[cur]
                FBc = FBm[m % 2]
                zcast(FBc[:, 0, :, :], Fc[0])
                zcast(FBc[:, 1, :, :], Fc[1])
                for part in range(2):
                    p = B.ps()
                    pb = p.bitcast(BF16)
                    for qq in range(8):
                        for r in range(4):
                            B.tr(pb[32 * r:32 * r + 32, qq * 128:(qq + 1) * 128], FBc[:, part, 4 * qq + r, :], IDB, tp=(0, 32 * r))
                    ets = ETS[(2 * m + part) % 2]
                    B.cp(ets, pb.rearrange("p (k c) -> p k c", c=128), eng="act")
                    B.dma(scrE.rearrange("p (a b c) -> p a b c", a=T, b=2)[:, T - 1 - m, part, :], ets.rearrange("p a b -> p (a b)"))
                zcast(GB[0], Gc[0])
                zcast(GB[1], Gc[1])
                for ct in range(NCT):
                    if m % 4 == 0:
                        pass
            if m >= 1:
                DTc = DTS[m % 2]
                zcast(DTc[:, :, 0, :], Gc[0])
                zcast(DTc[:, :, 1, :], Gc[1], scale=-1.0)
                B.dma(scrD.rearrange("p (a b) -> p a b", a=T)[:, m - 1, :], DTc.rearrange("p a b c -> p (a b c)"))
            if m < T:
                kcopies = []
                for cg in range(2):
                    p = B.ps()
                    for c4 in range(4):
                        ct = 4 * cg + c4
                        for r in range(4):
                            q = 4 * ct + r
                            if r < 3:
                                o = p[32 * r:32 * r + 32, 128 * c4 + 32 * r:128 * c4 + 32 * r + 32]
                                B.mm(o, BTB[0][:, q, :], GB[0][:, q, :], start=True, stop=False, tp=(0, 32 * r))
                                B.mm(o, BTB[1][:, q, :], GB[1][:, q, :], start=False, stop=True, tp=(0, 32 * r))
                            else:
                                o = p[64:128, 128 * c4 + 96:128 * c4 + 128]
                                B.mm(o, BTP[0][:, q, :], GB[0][:, q, :], start=True, stop=False, tp=(0, 64))
                                B.mm(o, BTP[1][:, q, :], GB[1][:, q, :], start=False, stop=True, tp=(0, 64))
                    kcopies.append((cg, p))
                cmul(FF[nxt][0], FF[nxt][1], FF[cur][0], FF[cur][1], ARb, AIb)
            if m < T:
                cmul(GG[nxt][0], GG[nxt][1], Gc[0], Gc[1], ARb, AIb)
                for (cg, p) in kcopies:
                    for r in range(4):
                        pr0 = 32 * r if r < 3 else 64
                        B.cp(KT[pr0:128 if r == 3 else pr0 + 32, 4 * cg:4 * cg + 4, m, 32 * r:32 * r + 32],
                             p[pr0:128 if r == 3 else pr0 + 32, :].rearrange("p (c x) -> p c x", x=128)[:, :, 32 * r:32 * r + 32], eng="dve")
        B.dma(scrK, KT.rearrange("p a b c -> p (a b c)"))

    wglv = w_glu.rearrange("(k p) n -> p k n", p=128)

    def ssm_layer(bi, blk, nt, G):
        import os
        SK = os.environ.get('SSM_SKIP', '')
        subs = subs_of(nt)
        o_ed = G.alloc(32768)
        o_k = G.alloc(16384)
        o_xs = G.alloc(NSLOT * 64 * 4)
        o_xsb = G.alloc(145 * 64 * 2)
        o_wgl = G.alloc(NCT * 2048 * 2)
        ET = V(o_ed, [T, 2, 8, 128], BF16)
        DTl = V(o_ed, [T, 32, 2, 32], BF16)
        KT = V(o_k, [NCT, T, 128], BF16)
        XS = V(o_xs, [NSLOT, 2, 32])
        XSB = V(o_xsb, [2, 32, 145], BF16)
        WGL = V(o_wgl, [NCT, 2048], BF16)
        build_tables()
        if 'S' not in SK:
            B.dma(ET.rearrange("p a b c d -> p (a b c d)"), scrE)
            B.dma(KT.rearrange("p a b c -> p (a b c)"), scrK)
        rmsnorm(nt, 1, o_xs)
        nchp = 128
        cpi = 0
        if 'E' in SK:
            B.memset(XS[:, 1:129, :, :], 0.0, eng="dve")
            B.memset(XS[:, 129:145, :, :], 0.0, eng="dve")
        NCHM = NTMAX // T
        nchk = nt // T
        HMs = [V(o_xsb, [4, T, NCHM], BF16), V(o_xsb + 4 * NTMAX * 2, [4, T, NCHM], BF16)]
        for q in (range(NPAIR) if 'E' not in SK else []):
            r = q % 4
            ct = q // 4
            HM = HMs[ct % 2]
            if r == 0:
                for r_ in range(4):
                    hsrc = H[:, ct, 0:nt].rearrange("p (n k) -> p k n", k=T)
                    if r_ != 3:
                        B.ts(HM[:, r_, :, 0:nchk], hsrc, MASK[:, r_:r_ + 1], None, ALU.mult)
                    else:
                        B.act(HM[:, r_, :, 0:nchk], hsrc, AF.Copy, scale=MASK[:, r_:r_ + 1])
            for part in range(2):
                p = B.ps()
                for kap in range(T):
                    B.mm(p[:, 0:nchk], ET[:, kap, part, ct, :], HM[:, r, kap, 0:nchk],
                         start=(kap == 0), stop=(kap == T - 1))
                eng = "act" if cpi % 2 == 0 else "dve"
                cpi += 1
                B.cp(XS[:, 1:1 + nchk, part, q], p[:, 0:nchk], eng=eng)
        o_scr = o_wgl
        if bi == 0:
            B.memset(XS[:, 0, :, :], 0.0, eng="dve")
            S16 = V(o_scr, [4096])
            if 'I' in SK:
                B.memset(XS[:, 145:161, :, :], 0.0, eng="dve")
            for part, src in (enumerate((sre, sim)) if 'I' not in SK else []):
                B.dma(S16[0:16, :], src)
                p = B.ps()
                for q in range(NPAIR):
                    B.tr(p[:, 16 * q:16 * q + 16], S16[0:16, 128 * q:128 * q + 128], IDF[0:16, 0:16])
                B.cp(XS[:, 145:161, part, :], p[:, :].rearrange("p (q s) -> p s q", s=16), eng="dve")
        else:
            B.cp(XS[:, 0, :, :], SCARRY, eng="dve")
        L1 = V(o_scr + 16384, [16, 2, 32]); L2 = V(o_scr + 16384 + 4096, [16, 2, 32])

        def swp(ap3):
            return bass.AP(ap3.tensor, ap3.offset + 32, [list(ap3.ap[0]), [-32, 2], [1, 32]])

        def bcm(t3, m):
            return bass.AP(t3.tensor, t3.offset, [list(t3.ap[0]), [0, m], [32, 2], [1, 32]])

        def bch(t3, h, m):
            return bass.AP(t3.tensor, t3.offset + 32 * h, [list(t3.ap[0]), [0, m], [1, 32]])

        import os as _os
        USE_SWAP = _os.environ.get('NO_SWAP', '') == ''

        def cmac(dst, s_full, s_h0, s_h1, c_full, s0, s1, m, ts_full=None):
            L1v = L1[:, 0:m, :, :]
            L2v = L2[:, 0:m, :, :]
            B.tt(L1v, c_full, s_full, ALU.mult)
            if USE_SWAP:
                pat = [list(x) for x in s_full.ap]
                assert pat[2] == [32, 2] and pat[3] == [1, 32], pat
                s_sw = bass.AP(s_full.tensor, s_full.offset + 32, [pat[0], pat[1], [-32, 2], [1, 32]])
                B.tt(L2v, ts_full, s_sw, ALU.mult)
            else:
                B.tt(L2v[:, :, 0, :], s0, s_h1, ALU.mult)
                B.tt(L2v[:, :, 1, :], s1, s_h0, ALU.mult)
            B.tt(L1v, L1v, L2v, ALU.add)
            B.tt(dst, dst, L1v, ALU.add)

        if 'L' not in SK:
            PC = V(o_scr, [16, 2, 32]); PS = V(o_scr + 4096, [16, 2, 32])
            B.dma(PC.rearrange("p a b c -> p (a b c)"), scrP[:, 0:1024])
            B.dma(PS.rearrange("p a b c -> p (a b c)"), scrP[:, 1024:2048])
            for t in range(1, 16):
                s = XS[:, t:t + 113:16, :, :]
                d = XS[:, t + 1:t + 114:16, :, :]
                cmac(d, s, s[:, :, 0, :], s[:, :, 1, :], bcm(ACt, 8), bch(ASt, 0, 8), bch(ASt, 1, 8), 8, ts_full=bcm(ASt, 8))
            def stepB(j):
                s = XS[:, 16 * j:16 * j + 1, :, :]
                d = XS[:, 16 * j + 16:16 * j + 17, :, :]
                cmac(d, s, s[:, :, 0, :], s[:, :, 1, :], PC[:, 15:16, :, :], PS[:, 15:16, 0, :], PS[:, 15:16, 1, :], 1, ts_full=PS[:, 15:16, :, :])

            def stepC(j):
                c3 = XS[:, 16 * j, :, :]
                d = XS[:, 16 * j + 1:16 * j + 16, :, :]
                cmac(d, bcm(c3, 15), bch(c3, 0, 15), bch(c3, 1, 15), PC[:, 0:15, :, :], PS[:, 0:15, 0, :], PS[:, 0:15, 1, :], 15, ts_full=PS[:, 0:15, :, :])
                B.cp(XSB[:, :, :, 16 * j:16 * j + 16], XS[:, 16 * j:16 * j + 16, :, :].rearrange("p n a q -> p a q n"), eng="act")

            for j in range(3):
                stepB(j)
            for j in range(4):
                stepC(j)
            for j in range(3, 8):
                stepB(j)
            for j in range(4, 8):
                stepC(j)
        if bi == 0 and 'L' not in SK:
            cur = XS[:, 145:161, :, :]
            acb = bass.AP(ACt.tensor, ACt.offset, [list(ACt.ap[0]), [0, 16], [32, 2], [1, 32]])
            asb = bass.AP(ASt.tensor, ASt.offset, [list(ASt.ap[0]), [0, 16], [32, 2], [1, 32]])
            B.tt(L1, acb, cur, ALU.mult)
            as0 = bass.AP(ASt.tensor, ASt.offset, [list(ASt.ap[0]), [0, 16], [1, 32]])
            as1 = bass.AP(ASt.tensor, ASt.offset + 32, [list(ASt.ap[0]), [0, 16], [1, 32]])
            B.tt(L2[:, :, 0, :], as0, cur[:, :, 1, :], ALU.mult)
            B.tt(L2[:, :, 1, :], as1, cur[:, :, 0, :], ALU.mult)
            B.tt(L1, L1, L2, ALU.add)
            B.tt(XS[:, 129:145, :, :], XS[:, 129:145, :, :], L1, ALU.add)
        if 'L' not in SK:
            if bi == 0:
                B.cp(XSB[:, :, :, 129:145], XS[:, 145:161, :, :].rearrange("p n a q -> p a q n"), eng="act")
        else:
            B.cp(XSB[:, :, :, 0:145], XS[:, 0:145, :, :].rearrange("p n a q -> p a q n"), eng="act")
        if 'S' not in SK:
            B.dma(DTl.rearrange("p a b c d -> p (a b c d)"), scrD)
        for s in range(4):
            B.dma(WGL[:, :, s * 512:(s + 1) * 512], wglv[:, :, s * 512:(s + 1) * 512], eng="pool")
        TA_ = V(o_xs, [512]); TB_ = V(o_xs + 2048, [512]); SIGs = [V(o_xs + 4096, [512]), V(o_xs + 6144, [512])]
        for (c0, n) in subs:
            nch = n // T
            slot0 = (c0 // T) if c0 < 1024 else 129
            for ct in range(NCT):
                p = B.ps()
                pv = p[:, 0:n].rearrange("p (c t) -> p c t", t=T)
                hv = H[:, ct, c0:c0 + n].rearrange("p (c t) -> p c t", t=T)
                for j in (range(T) if 'K' not in SK else [0]):
                    B.mm(pv[:, :, j:T], KT[:, ct, j, :], hv[:, :, 0:T - j], start=(j == 0), stop=False)
                for r in (range(4) if 'D' not in SK else []):
                    q = 4 * ct + r
                    for tau in range(T):
                        for part in range(2):
                            last = (r == 3 and tau == T - 1 and part == 1)
                            B.mm(p[32 * r:32 * r + 32, tau:n:T], DTl[:, tau, q, part, :], XSB[:, part, q, slot0:slot0 + nch],
                                 start=False, stop=last, tp=(0, 32 * r))
                TAc = TA_ if ct % 2 == 0 else TB_
                B.stt(TAc[:, 0:n], X[:, ct, c0:c0 + n], vec(1, ct), RS[:, c0:c0 + n], ALU.mult, ALU.mult)
                B.stt(TAc[:, 0:n], TAc[:, 0:n], vec(10, ct), p[:, 0:n], ALU.mult, ALU.add)
                B.act(H[:, ct, c0:c0 + n], TAc[:, 0:n], AF.Gelu_apprx_tanh)
        OUTS = V(o_xs + 8192, [16, 128])
        if bi == 0:
            B.cp(SCARRY, XS[:, 128, :, :], eng="dve")
            for part, dst in (enumerate((sssr, sssi)) if 'O' not in SK else []):
                for sg in range(4):
                    p = B.ps()
                    for s4 in range(4):
                        s = 4 * sg + s4
                        B.tr(p[0:32, 128 * s4:128 * s4 + 128], XS[:, 129 + s, part, :], IDF)
                    B.cp(OUTS[0:32, 4 * sg:4 * sg + 4, :], p[0:32, :].rearrange("p (s c) -> p s c", c=128), eng="dve")
                for s in range(NSEQ_S):
                    B.dma(dst[s:s + 1, :].rearrange("o (q c) -> (o q) c", c=128), OUTS[0:32, s, :])
        else:
            for part, dst in (enumerate((sspr, sspi)) if 'O' not in SK else []):
                p = B.ps()
                B.tr(p[0:32, 0:128], XS[:, 128, part, :], IDF)
                B.cp(OUTS[0:32, part, :], p[0:32, 0:128], eng="dve")
                B.dma(dst.rearrange("(q t) p -> q (t p)", t=2), OUTS[0:32, part, :])
        cnt = 0
        tn = tail_norm_fn("ssm", o_xs + 16384)
        for (c0, n) in subs_tok(nt):
          for ct in range(NCT):
            if True:
                pa = B.ps()
                pb = B.ps()
                for k in range(NCT):
                    B.mm(pa[:, 0:n], WGL[:, k, ct * 128:(ct + 1) * 128], H[:, k, c0:c0 + n], start=(k == 0), stop=(k == NCT - 1))
                for k in range(NCT):
                    B.mm(pb[:, 0:n], WGL[:, k, 1024 + ct * 128:1024 + (ct + 1) * 128], H[:, k, c0:c0 + n], start=(k == 0), stop=(k == NCT - 1))
                SIG = SIGs[cnt % 2]
                cnt += 1
                B.act(SIG[:, 0:n], pb[:, 0:n], AF.Sigmoid, bias=vec(12, ct), scale=1.0)
                B.stt(SIG[:, 0:n], pa[:, 0:n], vec(11, ct), SIG[:, 0:n], ALU.add, ALU.mult)
                B.tt(X[:, ct, c0:c0 + n], X[:, ct, c0:c0 + n], SIG[:, 0:n], ALU.add)
          if tn is not None:
              tn(c0, n)

    stg0 = [ARENA_BYTES - 49152 - 8192, ARENA_BYTES - 49152 - 4096]
    load_x(xp[0:1024, :], 0, 1024, stg0)
    load_x(xs, 1024, 128, stg0)
    if "ssm" in stages or "ssmsetup" in stages:
        for r_ in range(4):
            B.P.add("dve", (lambda r_: (lambda e: e.reduce_sum(MASK[:, r_:r_ + 1], IDF[:, 32 * r_:32 * r_ + 32], mybir.AxisListType.X)))(r_),
                    [IDF[:, 32 * r_:32 * r_ + 32]], [MASK[:, r_:r_ + 1]])
        ssm_setup()

    B.V = V
    blocks = [dict(p_off=0, npr=1024, samples=True, nt=1152), dict(p_off=1024, npr=1024, samples=False, nt=1024)]
    for bi, blk in enumerate(blocks):
        nt = blk["nt"]
        G = Arena(arena_h, GEN0, ARENA_BYTES)
        stg = [G.alloc(D * 4), G.alloc(D * 4)]
        if bi > 0:
            load_x(xp[blk["p_off"]:blk["p_off"] + blk["npr"], :], 0, blk["npr"], stg)
        for li, stg_name in enumerate(("conv", "ffn0", "ssm", "ffn1")):
            if stg_name not in stages:
                continue
            G = Arena(arena_h, GEN0, ARENA_BYTES)
            if stg_name.startswith("ffn"):
                ffn(int(stg_name[3]), nt, G)
            elif stg_name == "conv":
                conv_layer(bi, blk, nt, G)
            else:
                ssm_layer(bi, blk, nt, G)
        G = Arena(arena_h, GEN0, ARENA_BYTES)
        stg = [G.alloc(D * 4), G.alloc(D * 4)]
        o_sq = G.alloc(NCT * 512 * 2)
        o_yo = G.alloc(NCT * 512 * 4)
        if final_norm:
            SQ = V(o_sq, [NCT, 512], BF16)
            YO = V(o_yo, [NCT, 512])
            for (c0, n) in subs_of(nt):
                for ct in range(NCT):
                    B.act(SQ[:, ct, 0:n], X[:, ct, c0:c0 + n], AF.Square)
                p = B.ps()
                for ct in range(NCT):
                    B.mm(p[:, 0:n], ONE, SQ[:, ct, 0:n], start=(ct == 0), stop=(ct == NCT - 1))
                B.act(RS[:, c0:c0 + n], p[:, 0:n], AF.Sqrt, bias=float(D * EPS), scale=1.0)
                B.recip(RS[:, c0:c0 + n], RS[:, c0:c0 + n])
                for ct in range(NCT):
                    B.stt(YO[:, ct, 0:n], X[:, ct, c0:c0 + n], vec(4, ct), RS[:, c0:c0 + n], ALU.mult, ALU.mult)
                if c0 < 1024:
                    store_tokens(YO, 0, n, yp[blk["p_off"] + c0: blk["p_off"] + c0 + n, :], stg)
                else:
                    store_tokens(YO, 0, n, ys, stg)
    B.P.emit(nc)
    st.close()
    return nc


_NC_CACHE = {}
_BUILD_KW = {}


def kernel(**inp):
    f32 = np.float32
    n = 8
    if "nc" not in _NC_CACHE:
        _NC_CACHE["nc"] = build(**_BUILD_KW)
    nc = _NC_CACHE["nc"]
    c = lambda a: np.ascontiguousarray(np.asarray(a, dtype=f32))
    shared = {
        "ident": np.eye(128, dtype=f32),
        "norm_mix": c(inp["norm_mix"]), "norm_ffn": c(inp["norm_ffn"]), "norm_final": c(inp["norm_final"]).reshape(1, D),
        "w_pw1": c(inp["conv_w_pw1"][0]), "b_pw1": c(inp["conv_b_pw1"]).reshape(2, D), "w_dw": c(inp["conv_w_dw"][0]),
        "b_dw": c(inp["conv_b_dw"]).reshape(1, D), "ln_g": c(inp["conv_ln_g"]).reshape(1, D),
        "ln_b": c(inp["conv_ln_b"]).reshape(1, D), "w_pw2": c(inp["conv_w_pw2"][0]),
        "lam_re": c(inp["ssm_lam_re"][0]), "lam_im": c(inp["ssm_lam_im"][0]), "log_dt": c(inp["ssm_log_dt"]).reshape(NG, 1),
        "b_re": c(inp["ssm_b_re"]).reshape(NG * 64, 16), "b_im": c(inp["ssm_b_im"]).reshape(NG * 64, 16),
        "c_re": c(inp["ssm_c_re"]).reshape(NG * 16, 64), "c_im": c(inp["ssm_c_im"]).reshape(NG * 16, 64),
        "ssm_d": c(inp["ssm_d"]).reshape(1, D), "w_glu": c(inp["ssm_w_glu"][0]), "b_glu": c(inp["ssm_b_glu"]).reshape(2, D),
        "wg": c(inp["ffn_w_gate"]), "wu": c(inp["ffn_w_up"]), "wd": c(inp["ffn_w_down"]),
    }
    xpr = c(inp["x_prompt"]); xsa = c(inp["x_sample"]); ccv = c(inp["cache_conv"])
    s_re = c(inp["state_ssm_re"]); s_im = c(inp["state_ssm_im"])
    in_maps = []
    for i in range(n):
        m = dict(shared)
        m["xp"] = xpr[i]
        m["xs"] = xsa[16 * i:16 * i + 16].reshape(128, D)
        m["cc"] = ccv[0, 16 * i:16 * i + 16].reshape(480, D)
        m["sre"] = s_re[0, 16 * i:16 * i + 16].reshape(16, 4096)
        m["sim"] = s_im[0, 16 * i:16 * i + 16].reshape(16, 4096)
        in_maps.append(m)
    res = run_bass_kernel_spmd(nc, in_maps, core_ids=list(range(n)))
    R = res.results
    y_prompt = np.stack([R[i]["yp"] for i in range(n)]).astype(f32)
    y_sample = np.concatenate([R[i]["ys"].reshape(16, 8, D) for i in range(n)]).astype(f32)
    conv_p = np.stack([R[i]["convp"] for i in range(n)])[None].astype(f32)
    conv_s = np.concatenate([R[i]["convs"].reshape(16, 30, D) for i in range(n)])[None].astype(f32)
    sp_re = np.stack([R[i]["sspr"] for i in range(n)])[None].astype(f32)
    sp_im = np.stack([R[i]["sspi"] for i in range(n)])[None].astype(f32)
    ss_re = np.concatenate([R[i]["sssr"].reshape(16, 64, 64) for i in range(n)])[None].astype(f32)
    ss_im = np.concatenate([R[i]["sssi"].reshape(16, 64, 64) for i in range(n)])[None].astype(f32)
    return (y_prompt, y_sample, conv_p, conv_s, sp_re, sp_im, ss_re, ss_im)
```

```python
import math
import numpy as np
import concourse.bass as bass
import concourse.mybir as mybir
from concourse.bass_utils import run_bass_kernel_spmd

F32 = mybir.dt.float32
BF16 = mybir.dt.bfloat16
I32 = mybir.dt.int32
AF = mybir.ActivationFunctionType
ALU = mybir.AluOpType

D = 1024
NCT = 8
DFF = 2816
NFC = 22
SEQ = 2048
NSEQ_S = 16
TS = 8
KW = 31
EPS = 1e-6
T = 8
NG = 64
NPAIR = 32

_ESZ = {F32: 4, BF16: 2, I32: 4}


def _esz(dt):
    return _ESZ[dt]


class Op:
    __slots__ = ("eng", "fn", "reads", "writes", "dma", "deps", "sig", "sigval", "dsem", "dval", "prewait")

    def __init__(self, eng, fn, reads, writes, dma):
        self.eng = eng
        self.fn = fn
        self.reads = reads
        self.writes = writes
        self.dma = dma
        self.deps = set()
        self.sig = False
        self.sigval = 0
        self.dsem = None
        self.dval = 0
        self.prewait = None


def _acc(ap):
    t = ap.tensor
    name = t.name
    pat = ap.ap
    esz = _esz(ap.dtype)
    off = ap.offset
    sp = str(ap.space)
    if sp not in ("SB", "PSUM"):
        lo = off
        hi = off
        for (s, c) in pat:
            if s >= 0:
                hi += (c - 1) * s
            else:
                lo += (c - 1) * s
        return (name, 0, 1, lo * esz, (hi + 1) * esz)
    ps = pat[0][0]
    pc = pat[0][1]
    if ps == 0:
        p0 = 0
        f0 = off
        pc = 128
    else:
        p0 = off // ps
        f0 = off % ps
    lo = f0
    hi = f0
    for (s, c) in pat[1:]:
        if s >= 0:
            hi += (c - 1) * s
        else:
            lo += (c - 1) * s
    if sp == "PSUM":
        return (name, (p0 // 32) * 32, ((p0 + pc + 31) // 32) * 32, 0, 1 << 20)
    return (name, p0, p0 + pc, lo * esz, (hi + 1) * esz)


def _acc_multi(ap):
    base = _acc(ap)
    sp = str(ap.space)
    if sp != "SB":
        return [base]
    pat = ap.ap
    if len(pat) < 3 or pat[0][0] == 0:
        return [base]
    esz = _esz(ap.dtype)
    f0 = ap.offset % pat[0][0]
    dims = sorted([(s, c) for (s, c) in pat[1:] if c > 1], key=lambda d: -abs(d[0]))
    name, p0, p1, _, _ = base

    def extent(ds):
        lo = hi = 0
        for (s, c) in ds:
            if s >= 0:
                hi += (c - 1) * s
            else:
                lo += (c - 1) * s
        return lo, hi

    out = []

    def rec(ds, off, budget):
        if ds:
            (s, c) = ds[0]
            lo, hi = extent(ds[1:])
            inner = hi - lo + 1
            if c <= budget and abs(s) > inner:
                for i in range(c):
                    rec(ds[1:], off + i * s, budget // c)
                return
        lo, hi = extent(ds)
        out.append((name, p0, p1, (off + lo) * esz, (off + hi + 1) * esz))

    rec(dims, f0, 64)
    return out


class Prog:
    ENGS = ("pe", "act", "dve", "pool", "sp")

    def __init__(self):
        self.ops = []
        self.recs = {}

    def _track(self, idx, op):
        for (aps, is_w0) in ((op.reads, False), (op.writes, True)):
            for (ap, name, p0, p1, b0, b1) in [(ap_,) + iv for ap_ in aps for iv in _acc_multi(ap_)]:
                ap_space = str(ap.space)
                is_w = is_w0 or (ap_space == "PSUM")
                lst = self.recs.setdefault(name, [])
                keep = []
                for r in lst:
                    ov = not (r[1] <= p0 or p1 <= r[0] or r[3] <= b0 or b1 <= r[2])
                    if ov and (is_w or r[5]):
                        if r[4] != idx:
                            op.deps.add(r[4])
                        if is_w and r[0] >= p0 and r[1] <= p1 and r[2] >= b0 and r[3] <= b1 and r[4] != idx:
                            continue
                    keep.append(r)
                if not is_w and not op.dma:
                    keep = [r for r in keep if not ((not r[5]) and r[0] == p0 and r[1] == p1 and r[2] == b0
                                                    and r[3] == b1 and (not self.ops[r[4]].dma)
                                                    and self.ops[r[4]].eng == op.eng)]
                keep.append([p0, p1, b0, b1, idx, is_w])
                self.recs[name] = keep

    def add(self, eng, fn, reads, writes, dma=False):
        op = Op(eng, fn, list(reads), list(writes), dma)
        idx = len(self.ops)
        self.ops.append(op)
        self._track(idx, op)
        return op

    def emit(self, nc, nsem_dma=12):
        ops = self.ops
        for op in ops:
            nd = set()
            for d in op.deps:
                p = ops[d]
                if (not p.dma) and p.eng == "pe" and op.eng == "pe" and not op.dma:
                    continue
                nd.add(d)
            latest = {}
            nd2 = set()
            for d in nd:
                p = ops[d]
                if p.dma:
                    nd2.add(d)
                else:
                    if p.eng not in latest or d > latest[p.eng]:
                        latest[p.eng] = d
            nd2.update(latest.values())
            op.deps = nd2
            for d in nd2:
                if not ops[d].dma:
                    ops[d].sig = True
        cnt = {e: 0 for e in self.ENGS}
        ndma = 0
        dma_last = {}
        npool_dma = 0
        for op in ops:
            if op.dma and op.eng == "pool":
                op.dsem = ("p", npool_dma)
                op.dval = 16
                npool_dma += 1
            elif op.dma:
                s = ndma % nsem_dma
                op.dsem = s
                op.dval = 16 * (ndma // nsem_dma + 1)
                if s in dma_last:
                    op.prewait = dma_last[s]
                dma_last[s] = (s, op.dval)
                ndma += 1
            elif op.sig:
                cnt[op.eng] += 1
                op.sigval = cnt[op.eng]
        SEMCAP = 1000
        nsem_e = {e: max(1, (cnt[e] + SEMCAP - 1) // SEMCAP) for e in self.ENGS}
        print("sem counts", cnt, "ndma", ndma, "npool_dma", npool_dma, "nops", len(ops))
        import contextlib
        with contextlib.ExitStack() as st:
            esem = {e: [st.enter_context(nc.semaphore("s_%s%d" % (e, i))) for i in range(nsem_e[e])] for e in self.ENGS}
            dsem = {i: st.enter_context(nc.semaphore("d%d" % i)) for i in range(nsem_dma)}
            for i in range(npool_dma):
                dsem[("p", i)] = st.enter_context(nc.semaphore("q%d" % i))
            block = st.enter_context(nc.Block())
            final_d = dict(dma_last)

            def body(ename):
                def run(e):
                    waited = {}

                    def wait(sem, key, val):
                        if waited.get(key, 0) >= val:
                            return
                        waited[key] = val
                        e.wait_ge(sem, val)

                    for op in ops:
                        if op.eng != ename:
                            continue
                        for d in sorted(op.deps):
                            p = ops[d]
                            if p.dma:
                                wait(dsem[p.dsem], ("d", p.dsem), p.dval)
                            else:
                                si_ = (p.sigval - 1) // SEMCAP
                                wait(esem[p.eng][si_], ("e", p.eng, si_), (p.sigval - 1) % SEMCAP + 1)
                        if op.dma and op.prewait is not None:
                            wait(dsem[op.prewait[0]], ("d", op.prewait[0]), op.prewait[1])
                        ins = op.fn(e)
                        if op.dma:
                            ins.then_inc(dsem[op.dsem], 16)
                        elif op.sig:
                            ins.then_inc(esem[op.eng][(op.sigval - 1) // SEMCAP], 1)
                    if ename == "sp":
                        for s, (si, v) in final_d.items():
                            wait(dsem[si], ("d", si), v)
                        for i in range(npool_dma):
                            wait(dsem[("p", i)], ("d", ("p", i)), 16)
                        for en in self.ENGS:
                            if en != "sp" and cnt[en] > 0:
                                si_ = (cnt[en] - 1) // SEMCAP
                                wait(esem[en][si_], ("e", en, si_), (cnt[en] - 1) % SEMCAP + 1)
                return run

            block.tensor(body("pe"))
            block.scalar(body("act"))
            block.vector(body("dve"))
            block.gpsimd(body("pool"))
            block.sync(body("sp"))


class Builder:
    def __init__(self, nc, stages):
        self.nc = nc
        self.P = Prog()
        self.stages = stages
        self.psi = 0

    def mm(self, out, lhsT, rhs, start=True, stop=True, tp=None):
        kw = {}
        if tp is not None:
            kw["tile_position"] = tp
        self.P.add("pe", lambda e: e.matmul(out, lhsT, rhs, start=start, stop=stop, **kw), [lhsT, rhs], [out])

    def tr(self, out, in_, ident, tp=None):
        kw = {}
        if tp is not None:
            kw["tile_position"] = tp
        self.P.add("pe", lambda e: e.transpose(out, in_, ident, **kw), [in_, ident], [out])

    def act(self, out, in_, func, bias=None, scale=None, eng="act"):
        kw = {}
        rd = [in_]
        if bias is not None:
            kw["bias"] = bias
            if not isinstance(bias, (int, float)):
                rd.append(bias)
        if scale is not None:
            kw["scale"] = scale
            if not isinstance(scale, (int, float)):
                rd.append(scale)
        self.P.add("act", lambda e: e.activation(out, in_, func, **kw), rd, [out])

    def tt(self, out, a, b, op, eng="dve"):
        self.P.add(eng, lambda e: e.tensor_tensor(out, a, b, op), [a, b], [out])

    def ts(self, out, a, s1, s2, op0, op1=None, eng="dve"):
        rd = [a]
        for s in (s1, s2):
            if s is not None and not isinstance(s, (int, float)):
                rd.append(s)
        if op1 is None:
            self.P.add(eng, lambda e: e.tensor_scalar(out, a, s1, None, op0), rd, [out])
        else:
            self.P.add(eng, lambda e: e.tensor_scalar(out, a, s1, s2, op0, op1), rd, [out])

    def stt(self, out, a, s, b, op0, op1, eng="dve"):
        rd = [a, b]
        if not isinstance(s, (int, float)):
            rd.append(s)
        self.P.add(eng, lambda e: e.scalar_tensor_tensor(out, a, s, b, op0, op1), rd, [out])

    def cp(self, out, in_, eng="dve"):
        if eng == "act":
            self.P.add("act", lambda e: e.activation(out, in_, AF.Copy), [in_], [out])
        else:
            self.P.add(eng, lambda e: e.tensor_copy(out, in_), [in_], [out])

    def memset(self, ap, v, eng="pool"):
        self.P.add(eng, lambda e: e.memset(ap, v), [], [ap])

    def recip(self, out, in_):
        self.P.add("dve", lambda e: e.reciprocal(out, in_), [in_], [out])

    def dma(self, out, in_, eng="sp"):
        self.P.add(eng, lambda e: e.dma_start(out=out, in_=in_), [in_], [out], dma=True)

    def ps(self):
        p = self.psum[self.psi % 8]
        self.psi += 1
        return p


class Arena:
    def __init__(self, handle, base, limit):
        self.h = handle
        self.off = base
        self.limit = limit

    def alloc(self, nbytes):
        nbytes = (nbytes + 63) // 64 * 64
        o = self.off
        self.off += nbytes
        assert self.off <= self.limit, (self.off, self.limit)
        return o


ARENA_BYTES = 206848
NTMAX = 1152
NV = 44


def build(stages=("conv", "ffn0", "ssm", "ffn1"), final_norm=True):
    nc = bass.Bass("TRN2", target_bir_lowering=False)
    B = Builder(nc, stages)

    def din(name, shape, dt=F32):
        return nc.dram_tensor(name, shape, dt, kind="ExternalInput").ap()

    def dout(name, shape, dt=F32):
        return nc.dram_tensor(name, shape, dt, kind="ExternalOutput").ap()

    xp = din("xp", [SEQ, D]); xs = din("xs", [128, D]); cc = din("cc", [480, D])
    sre = din("sre", [NSEQ_S, 4096]); sim = din("sim", [NSEQ_S, 4096])
    ident_d = din("ident", [128, 128])
    norm_mix = din("norm_mix", [2, D]); norm_ffn = din("norm_ffn", [2, D]); norm_final = din("norm_final", [1, D])
    w_pw1 = din("w_pw1", [D, 2 * D]); b_pw1 = din("b_pw1", [2, D]); w_dw = din("w_dw", [KW, D])
    b_dw = din("b_dw", [1, D]); ln_g = din("ln_g", [1, D]); ln_b = din("ln_b", [1, D]); w_pw2 = din("w_pw2", [D, D])
    lam_re = din("lam_re", [NG, 64]); lam_im = din("lam_im", [NG, 64]); log_dt = din("log_dt", [NG, 1])
    b_re = din("b_re", [NG * 64, 16]); b_im = din("b_im", [NG * 64, 16])
    c_re = din("c_re", [NG * 16, 64]); c_im = din("c_im", [NG * 16, 64])
    ssm_d = din("ssm_d", [1, D]); w_glu = din("w_glu", [D, 2 * D]); b_glu = din("b_glu", [2, D])
    wg = din("wg", [2, D, DFF]); wu = din("wu", [2, D, DFF]); wd = din("wd", [2, DFF, D])

    yp = dout("yp", [SEQ, D]); ys = dout("ys", [128, D])
    convp = dout("convp", [KW - 1, D]); convs = dout("convs", [480, D])
    sspr = dout("sspr", [NG, 64]); sspi = dout("sspi", [NG, 64])
    sssr = dout("sssr", [NSEQ_S, 4096]); sssi = dout("sssi", [NSEQ_S, 4096])

    import contextlib
    st = contextlib.ExitStack()
    arena_h = st.enter_context(nc.sbuf_tensor("arena", [128, ARENA_BYTES // 4], F32))
    B.psum = [st.enter_context(nc.psum_tensor("ps%d" % i, [128, 512], F32)) for i in range(8)]

    def V(off, shape, dt=F32):
        n = 1
        for s in shape:
            n *= s
        nb = n * _esz(dt)
        assert off % 4 == 0 and nb % 4 == 0
        ap = arena_h[:, off // 4: off // 4 + nb // 4]
        if dt != F32:
            ap = ap.bitcast(dt)
        if len(shape) == 2:
            ap = ap.rearrange("p (a b) -> p a b", b=shape[1])
        elif len(shape) == 3:
            ap = ap.rearrange("p (a b c) -> p a b c", b=shape[1], c=shape[2])
        elif len(shape) == 4:
            ap = ap.rearrange("p (a b c d) -> p a b c d", b=shape[1], c=shape[2], d=shape[3])
        return ap

    A = Arena(arena_h, 0, ARENA_BYTES)
    o_x = A.alloc(NCT * NTMAX * 4)
    o_h = A.alloc(NCT * NTMAX * 2)
    o_rs = A.alloc(NTMAX * 4)
    o_idf = A.alloc(128 * 4)
    o_idb = A.alloc(128 * 2)
    o_one = A.alloc(128 * 2)
    o_vec = A.alloc(NCT * NV * 4)
    o_small = A.alloc(2048)
    GEN0 = A.off
    X = V(o_x, [NCT, NTMAX]); H = V(o_h, [NCT, NTMAX], BF16); RS = V(o_rs, [NTMAX])
    IDF = V(o_idf, [128]); IDB = V(o_idb, [128], BF16); ONE = V(o_one, [128], BF16)
    VEC = V(o_vec, [NCT, NV])

    def vec(v, ct):
        return VEC[:, ct, v:v + 1]

    B.dma(IDF, ident_d)
    B.cp(IDB, IDF, eng="dve")
    B.memset(ONE, 1.0, eng="dve")
    G = Arena(arena_h, GEN0, ARENA_BYTES)
    o_vt = G.alloc(D * 4)
    VT = V(o_vt, [D])
    rows = [(norm_mix, 0, 2), (norm_ffn, 2, 2), (norm_final, 4, 1), (b_pw1, 5, 2), (b_dw, 7, 1), (ln_g, 8, 1),
            (ln_b, 9, 1), (ssm_d, 10, 1), (b_glu, 11, 2), (w_dw, 13, KW)]
    for (src, r0, n) in rows:
        B.dma(VT[r0:r0 + n, :], src)
    for ct in range(NCT):
        p = B.ps()
        B.tr(p[:, 0:NV], VT[0:NV, ct * 128:(ct + 1) * 128], IDF[0:NV, 0:NV])
        B.cp(VEC[:, ct, :], p[:, 0:NV], eng="dve")
    B.ts(VEC[:, :, 0:5], VEC[:, :, 0:5], math.sqrt(D), None, ALU.mult)

    def subs_of(nt):
        out = []
        c = 0
        while c < nt:
            n = min(512, nt - c)
            out.append((c, n))
            c += n
        return out

    def subs_tok(nt):
        if nt == 1152:
            return [(0, 384), (384, 384), (768, 384)]
        return subs_of(nt)

    def load_x(src_rows, c0, ntok, stage_off):
        ntile = ntok // 128
        for ti in range(ntile):
            so = stage_off[ti % 2]
            XIN = V(so, [D])
            B.dma(XIN, src_rows[ti * 128:(ti + 1) * 128, :])
            for half in range(2):
                p = B.ps()
                for j in range(4):
                    ct = half * 4 + j
                    B.tr(p[:, j * 128:(j + 1) * 128], XIN[:, ct * 128:(ct + 1) * 128], IDF)
                B.cp(X[:, half * 4:half * 4 + 4, c0 + ti * 128: c0 + (ti + 1) * 128],
                     p.rearrange("p (a b) -> p a b", b=128), eng="act")

    prenormed = [False]
    STAGE_ORDER = [s_ for s_ in ("conv", "ffn0", "ssm", "ffn1") if s_ in stages]
    NEXT_GIDX = {"ffn0": 2, "ssm": 1, "ffn1": 3}

    def rmsnorm_sub(c0, n, gidx, sq_off):
        SQ = V(sq_off, [NCT, 512], BF16)
        for ct in range(NCT):
            B.act(SQ[:, ct, 0:n], X[:, ct, c0:c0 + n], AF.Square)
        p = B.ps()
        for ct in range(NCT):
            B.mm(p[:, 0:n], ONE, SQ[:, ct, 0:n], start=(ct == 0), stop=(ct == NCT - 1))
        B.act(RS[:, c0:c0 + n], p[:, 0:n], AF.Sqrt, bias=float(D * EPS), scale=1.0)
        B.recip(RS[:, c0:c0 + n], RS[:, c0:c0 + n])
        for ct in range(NCT):
            B.stt(H[:, ct, c0:c0 + n], X[:, ct, c0:c0 + n], vec(gidx, ct), RS[:, c0:c0 + n], ALU.mult, ALU.mult)

    def rmsnorm(nt, gidx, sq_off, dst=None, f32dst=None):
        if prenormed[0]:
            prenormed[0] = False
            return
        for (c0, n) in subs_tok(nt):
            rmsnorm_sub(c0, n, gidx, sq_off)

    def tail_norm_fn(stage_name, sq_off):
        i_ = STAGE_ORDER.index(stage_name)
        if i_ + 1 >= len(STAGE_ORDER):
            return None
        gidx = NEXT_GIDX[STAGE_ORDER[i_ + 1]]
        prenormed[0] = True
        return lambda c0, n: rmsnorm_sub(c0, n, gidx, sq_off)

    def store_tokens(src_fm, c0, ntok, dst_rows, stage_off):
        ntile = (ntok + 127) // 128
        for ti in range(ntile):
            n = min(128, ntok - ti * 128)
            so = stage_off[ti % 2]
            XO = V(so, [D])
            for half in range(2):
                p = B.ps()
                for j in range(4):
                    ct = half * 4 + j
                    B.tr(p[0:n, j * 128:(j + 1) * 128], src_fm[:, ct, c0 + ti * 128: c0 + ti * 128 + n], IDF)
                B.cp(XO[0:n, half * 512:(half + 1) * 512], p[0:n, :], eng="act")
            B.dma(dst_rows[ti * 128: ti * 128 + n, :], XO[0:n, :])

    tables_done = []

    def build_tables():
        if tables_done or "ssm" not in stages:
            return
        tables_done.append(1)
        PCs = V(o_h, [16, 2, 32]); PSs = V(o_h + 4096, [16, 2, 32])
        Lt1 = V(o_h + 8192, [2, 32]); Lt2 = V(o_h + 8192 + 256, [2, 32])
        B.cp(PCs[:, 0, :, :], ACt, eng="dve")
        B.cp(PSs[:, 0, :, :], ASt, eng="dve")
        for k in range(1, 16):
            U = PCs[:, k - 1, :, :]; W = PSs[:, k - 1, :, :]
            B.tt(Lt1, U, ACt, ALU.mult)
            B.tt(Lt2, W, ASt, ALU.mult)
            B.tt(PCs[:, k, :, :], Lt1, Lt2, ALU.subtract)
            B.tt(Lt1, U, ASt, ALU.mult)
            B.tt(Lt2, W, ACt, ALU.mult)
            B.tt(PSs[:, k, :, :], Lt1, Lt2, ALU.add)
        B.dma(scrP[:, 0:1024], PCs.rearrange("p a b c -> p (a b c)"))
        B.dma(scrP[:, 1024:2048], PSs.rearrange("p a b c -> p (a b c)"))

    def ffn(layer, nt, G):
        subs = subs_tok(nt)
        o_act = G.alloc(NFC * nt * 2)
        o_wd = G.alloc(NFC * D * 2)
        o_slab = [G.alloc(2 * NCT * 512 * 2) for _ in range(2)]
        o_sg = [G.alloc(2048) for _ in range(2)]
        ACTB = V(o_act, [NFC, nt], BF16)
        WD = V(o_wd, [NFC, D], BF16)
        rmsnorm(nt, 2 + layer, o_wd)
        wgv = wg[layer].rearrange("(k p) n -> p k n", p=128)
        wuv = wu[layer].rearrange("(k p) n -> p k n", p=128)
        wdv = wd[layer].rearrange("(f p) n -> p f n", p=128)
        slabs = []
        c = 0
        while c < DFF:
            w = min(512, DFF - c)
            slabs.append((c, w))
            c += w
        cnt = 0
        for si, (col0, w) in enumerate(slabs):
            SL = V(o_slab[si % 2], [2, NCT, 512], BF16)
            B.dma(SL[:, 0, :, 0:w], wgv[:, :, col0:col0 + w], eng="pool")
            B.dma(SL[:, 1, :, 0:w], wuv[:, :, col0:col0 + w], eng="pool")
            if si == 1:
                B.dma(WD[:, 0:11, :], wdv[:, 0:11, :], eng="pool")
            if si == 2:
                B.dma(WD[:, 11:22, :], wdv[:, 11:22, :], eng="pool")
            tiles = ([(j, c0, n) for (c0, n) in subs for j in range(w // 128)] if si == 0
                     else [(j, c0, n) for j in range(w // 128) for (c0, n) in subs])
            for (j, c0, n) in tiles:
                fc = col0 // 128 + j
                if True:
                    pg = B.ps()
                    pu = B.ps()
                    for k in range(NCT):
                        B.mm(pg[:, 0:n], SL[:, 0, k, j * 128:(j + 1) * 128], H[:, k, c0:c0 + n], start=(k == 0), stop=(k == NCT - 1))
                    for k in range(NCT):
                        B.mm(pu[:, 0:n], SL[:, 1, k, j * 128:(j + 1) * 128], H[:, k, c0:c0 + n], start=(k == 0), stop=(k == NCT - 1))
                    SG = V(o_sg[cnt % 2], [512])
                    cnt += 1
                    B.act(SG[:, 0:n], pg[:, 0:n], AF.Silu)
                    B.tt(ACTB[:, fc, c0:c0 + n], SG[:, 0:n], pu[:, 0:n], ALU.mult)
        if layer == 0:
            build_tables()
        tn = tail_norm_fn("ffn%d" % layer, o_slab[0])
        for (c0, n) in subs:
            for ct in range(NCT):
                p = B.ps()
                for fc in range(NFC):
                    B.mm(p[:, 0:n], WD[:, fc, ct * 128:(ct + 1) * 128], ACTB[:, fc, c0:c0 + n], start=(fc == 0), stop=(fc == NFC - 1))
                B.tt(X[:, ct, c0:c0 + n], X[:, ct, c0:c0 + n], p[:, 0:n], ALU.add)
            if tn is not None:
                tn(c0, n)

    VCARRY = V(o_small, [NCT, 30], BF16)
    WDB = V(o_small + 1536, [NCT, KW + 1], BF16)[:, :, 0:KW]
    B.cp(WDB, VEC[:, :, 13:13 + KW], eng="dve")
    w1v = w_pw1.rearrange("(k p) n -> p k n", p=128)
    w2v = w_pw2.rearrange("(k p) n -> p k n", p=128)
    VWMAX = 30 + 1024 + NSEQ_S * 38
    SB0 = 30 + 1024

    def conv_layer(bi, blk, nt, G):
        subs = subs_of(nt)
        o_vb = G.alloc(NCT * VWMAX * 2)
        o_w1 = ARENA_BYTES - 49152
        o_w2 = ARENA_BYTES - 16384
        G.limit = o_w1
        o_yf = G.alloc(NCT * 512 * 4)
        o_ybf = G.alloc(NCT * 512 * 2)
        o_ysq = G.alloc(NCT * 512 * 2)
        o_dg = [G.alloc(KW * 128 * 2) for _ in range(2)]
        o_sig = [G.alloc(2048) for _ in range(2)]
        o_stat = [G.alloc(2048) for _ in range(4)]
        o_vo = G.alloc(NCT * 160 * 4)
        VB = V(o_vb, [NCT, VWMAX], BF16)
        W1 = V(o_w1, [NCT, 2048], BF16)
        W2 = V(o_w2, [NCT, 1024], BF16)
        YF = V(o_yf, [NCT, 512])
        YBF = V(o_ybf, [NCT, 512], BF16)
        YSQ = V(o_ysq, [NCT, 512], BF16)
        VO = V(o_vo, [NCT, 160])
        MEAN = V(o_stat[0], [512]); TMP = V(o_stat[1], [512]); RSTD = V(o_stat[2], [512]); MR = V(o_stat[3], [512])
        stg = [o_ybf, o_ybf + 4096]
        for s in (0, 2, 1, 3):
            B.dma(W1[:, :, s * 512:(s + 1) * 512], w1v[:, :, s * 512:(s + 1) * 512], eng="pool")
        for s in range(2):
            B.dma(W2[:, :, s * 512:(s + 1) * 512], w2v[:, :, s * 512:(s + 1) * 512], eng="pool")
        rmsnorm(nt, 0, o_yf)
        if bi == 0:
            B.memset(VB[:, :, 0:30], 0.0, eng="pool")
            for i in range(4):
                XIN = V(stg[i % 2], [D])
                B.dma(XIN[0:120, :], cc[120 * i:120 * i + 120, :])
                for half in range(2):
                    p = B.ps()
                    for j in range(4):
                        ct = half * 4 + j
                        B.tr(p[:, j * 128:j * 128 + 120], XIN[0:120, ct * 128:(ct + 1) * 128], IDF[0:120, 0:120])
                    for j in range(4):
                        ct = half * 4 + j
                        dst = VB[:, ct, SB0 + 152 * i: SB0 + 152 * i + 152].rearrange("p (s k) -> p s k", k=38)[:, :, 0:30]
                        B.cp(dst, p[:, j * 128:j * 128 + 120].rearrange("p (s k) -> p s k", k=30), eng="dve")
        else:
            B.cp(VB[:, :, 0:30], VCARRY, eng="dve")
        cnt = 0
        for (c0, n) in subs:
            for ct in range(NCT):
                pa = B.ps()
                pb = B.ps()
                for k in range(NCT):
                    B.mm(pa[:, 0:n], W1[:, k, ct * 128:(ct + 1) * 128], H[:, k, c0:c0 + n], start=(k == 0), stop=(k == NCT - 1))
                for k in range(NCT):
                    B.mm(pb[:, 0:n], W1[:, k, 1024 + ct * 128:1024 + (ct + 1) * 128], H[:, k, c0:c0 + n], start=(k == 0), stop=(k == NCT - 1))
                SIG = V(o_sig[cnt % 2], [512])
                cnt += 1
                B.act(SIG[:, 0:n], pb[:, 0:n], AF.Sigmoid, bias=vec(6, ct), scale=1.0)
                if c0 < 1024:
                    B.stt(VB[:, ct, 30 + c0:30 + c0 + n], pa[:, 0:n], vec(5, ct), SIG[:, 0:n], ALU.add, ALU.mult)
                    if bi == 1 and c0 + n == 1024:
                        B.stt(VO[:, ct, 0:30], pa[:, n - 30:n], vec(5, ct), SIG[:, n - 30:n], ALU.add, ALU.mult)
                else:
                    dst = VB[:, ct, SB0:SB0 + 608].rearrange("p (s k) -> p s k", k=38)[:, :, 30:38]
                    B.stt(dst, pa[:, 0:n].rearrange("p (s t) -> p s t", t=8), vec(5, ct),
                          SIG[:, 0:n].rearrange("p (s t) -> p s t", t=8), ALU.add, ALU.mult)
                    B.stt(VO[:, ct, 32:160], pa[:, 0:n], vec(5, ct), SIG[:, 0:n], ALU.add, ALU.mult)
        o_dgr = list(o_dg) + [o_w1 + 8192 * i_ for i_ in range(4)]
        NDG = len(o_dgr)
        NAHEAD = NDG - 1

        def gen_dg(i):
            ct_ = i % NCT
            DG_ = V(o_dgr[i % NDG], [KW, 128], BF16)
            idb_b = bass.AP(IDB.tensor, IDB.offset, [list(IDB.ap[0]), [0, KW], [1, 128]])
            wsl = WDB[:, ct_, :]
            w_b = bass.AP(wsl.tensor, wsl.offset, [list(wsl.ap[0]), [1, KW], [0, 128]])
            B.tt(DG_, idb_b, w_b, ALU.mult)

        dgc = 0
        ndg = len(subs) * NCT
        for i_ in range(min(NAHEAD, ndg)):
            gen_dg(i_)
        for (c0, n) in subs:
            for ct in range(NCT):
                DG = V(o_dgr[dgc % NDG], [KW, 128], BF16)
                if dgc + NAHEAD < ndg:
                    gen_dg(dgc + NAHEAD)
                dgc += 1
                p = B.ps()
                for k in range(KW):
                    if c0 < 1024:
                        rhs = VB[:, ct, c0 + k:c0 + k + n]
                        out = p[:, 0:n]
                    else:
                        rhs = VB[:, ct, SB0:SB0 + 608].rearrange("p (s k) -> p s k", k=38)[:, :, k:k + 8]
                        out = p[:, 0:n].rearrange("p (s t) -> p s t", t=8)
                    B.mm(out, DG[:, k, :], rhs, start=(k == 0), stop=(k == KW - 1))
                B.act(YF[:, ct, 0:n], p[:, 0:n], AF.Identity, bias=vec(7, ct), scale=1.0)
                B.act(YSQ[:, ct, 0:n], p[:, 0:n], AF.Square, bias=vec(7, ct), scale=1.0)
                B.act(YBF[:, ct, 0:n], p[:, 0:n], AF.Identity, bias=vec(7, ct), scale=1.0)
            p1 = B.ps()
            p2 = B.ps()
            for ct in range(NCT):
                B.mm(p1[:, 0:n], ONE, YBF[:, ct, 0:n], start=(ct == 0), stop=(ct == NCT - 1))
            for ct in range(NCT):
                B.mm(p2[:, 0:n], ONE, YSQ[:, ct, 0:n], start=(ct == 0), stop=(ct == NCT - 1))
            B.ts(MEAN[:, 0:n], p1[:, 0:n], 1.0 / D, None, ALU.mult)
            B.tt(TMP[:, 0:n], MEAN[:, 0:n], MEAN[:, 0:n], ALU.mult)
            B.stt(TMP[:, 0:n], p2[:, 0:n], 1.0 / D, TMP[:, 0:n], ALU.mult, ALU.subtract)
            B.act(RSTD[:, 0:n], TMP[:, 0:n], AF.Sqrt, bias=float(EPS), scale=1.0)
            B.recip(RSTD[:, 0:n], RSTD[:, 0:n])
            B.tt(MR[:, 0:n], MEAN[:, 0:n], RSTD[:, 0:n], ALU.mult)
            for ct in range(NCT):
                B.tt(YF[:, ct, 0:n], YF[:, ct, 0:n], RSTD[:, 0:n], ALU.mult)
                B.tt(YF[:, ct, 0:n], YF[:, ct, 0:n], MR[:, 0:n], ALU.subtract)
                B.act(H[:, ct, c0:c0 + n], YF[:, ct, 0:n], AF.Silu, bias=vec(9, ct), scale=vec(8, ct))
        tn = tail_norm_fn("conv", o_ysq)
        for (c0, n) in subs_tok(nt):
            for ct in range(NCT):
                p = B.ps()
                for k in range(NCT):
                    B.mm(p[:, 0:n], W2[:, k, ct * 128:(ct + 1) * 128], H[:, k, c0:c0 + n], start=(k == 0), stop=(k == NCT - 1))
                B.tt(X[:, ct, c0:c0 + n], X[:, ct, c0:c0 + n], p[:, 0:n], ALU.add)
            if tn is not None:
                tn(c0, n)
        if bi == 0:
            B.cp(VCARRY, VB[:, :, 1024:1054], eng="dve")
            XO = V(stg[0], [D])
            for half in range(2):
                p = B.ps()
                for j in range(4):
                    ct = half * 4 + j
                    B.tr(p[:, j * 128:(j + 1) * 128], VO[:, ct, 32:160], IDF)
                B.cp(XO[:, half * 512:(half + 1) * 512], p[:, :], eng="act")
            cs3 = convs.rearrange("(s k) d -> s k d", k=30)
            cc3 = cc.rearrange("(s k) d -> s k d", k=30)
            for s in range(NSEQ_S):
                B.dma(cs3[s, 22:30, :], XO[8 * s:8 * s + 8, :])
            B.dma(cs3[:, 0:22, :], cc3[:, 8:30, :])
        else:
            XO = V(stg[0], [D])
            for half in range(2):
                p = B.ps()
                for j in range(4):
                    ct = half * 4 + j
                    B.tr(p[0:30, j * 128:(j + 1) * 128], VO[:, ct, 0:30], IDF)
                B.cp(XO[0:30, half * 512:(half + 1) * 512], p[0:30, :], eng="act")
            B.dma(convp, XO[0:30, :])

    scrE = nc.dram_tensor("scrE", [128, 16384], BF16, kind="Internal").ap()
    scrD = nc.dram_tensor("scrD", [128, 16384], BF16, kind="Internal").ap()
    scrK = nc.dram_tensor("scrK", [128, 8192], BF16, kind="Internal").ap()
    scrP = nc.dram_tensor("scrP", [128, 2048], F32, kind="Internal").ap()
    o_ac = o_small + 512
    ACt = V(o_ac, [2, 32]); ASt = V(o_ac + 256, [2, 32]); SCARRY = V(o_ac + 512, [2, 32])
    MASK = V(o_ac + 768, [4])
    PSTR = ARENA_BYTES // 4
    NSLOT = 161

    def bc_last(ap2, n):
        return bass.AP(ap2.tensor, ap2.offset, [list(ap2.ap[0]), list(ap2.ap[1]), [0, n]])

    def ssm_setup():
        G = Arena(arena_h, GEN0, ARENA_BYTES - 49152 - 8192)
        f = lambda n: G.alloc(n)
        sm = {}
        for nm in ("LR", "LI", "DT", "MAG", "ANG", "KF", "R", "M", "SIN", "COS", "AR", "AI", "NR", "DEN", "CFR", "CFI", "t1", "t2", "PR", "PI"):
            sm[nm] = V(f(128), [32])
        KI = V(f(128), [32], I32)
        T1 = V(f(512), [128])
        T2 = V(f(512), [128])
        BZ = [V(f(2048), [32, 16]) for _ in range(2)]
        CZ = [V(f(2048), [32, 16]) for _ in range(2)]
        TA = V(f(2048), [32, 16]); TB = V(f(2048), [32, 16])
        TA2 = V(f(2048), [32, 16]); TB2 = V(f(2048), [32, 16])
        GG = [[V(f(2048), [32, 16]) for _ in range(2)] for _ in range(2)]
        FF = [[V(f(2048), [32, 16]) for _ in range(2)] for _ in range(2)]
        GB = [V(f(2048), [32, 32], BF16) for _ in range(2)]
        FBm = [V(f(4096), [2, 32, 32], BF16) for _ in range(2)]
        BTB = [V(f(2048), [32, 32], BF16), V(f(2048), [32, 32], BF16)]
        ETS = [V(f(2048), [T, 128], BF16), V(f(2048), [T, 128], BF16)]
        DTS = [V(f(4096), [32, 2, 32], BF16) for _ in range(2)]
        KT = V(o_h, [NCT, T, 128], BF16)

        for z_ in (FBm[0], FBm[1], DTS[0], DTS[1], KT):
            B.P.add("act", (lambda z_: (lambda e: e.memzero(z_)))(z_), [], [z_])

        def tposed(dst, src_ap_rows, is_col=False):
            if is_col:
                B.dma(T2[0:64, 0:1], src_ap_rows)
                B.act(T2[0:64, 0:1], T2[0:64, 0:1], AF.Exp)
                B.cp(T1[0:64, :], T2[0:64, 0:1].to_broadcast([64, 128]), eng="dve")
            else:
                B.dma(T1[0:64, 0:64], src_ap_rows)
                B.dma(T1[0:64, 64:128], src_ap_rows)
            p = B.ps()
            B.tr(p[:, 0:64], T1[0:64, :], IDF[0:64, 0:64])
            for g2 in range(2):
                B.cp(dst[64 * g2:64 * g2 + 64, :], p[64 * g2:64 * g2 + 64, g2:64:2], eng="dve")

        tposed(sm["LR"], lam_re)
        tposed(sm["LI"], lam_im)
        tposed(sm["DT"], log_dt, is_col=True)
        S = sm
        B.tt(S["MAG"], S["LR"], S["DT"], ALU.mult)
        B.act(S["MAG"], S["MAG"], AF.Exp)
        B.tt(S["ANG"], S["LI"], S["DT"], ALU.mult)
        TWO_PI = 2.0 * math.pi

        def reduce_sin(dst, src, shift):
            B.ts(S["R"], src, shift, None, ALU.add)
            B.ts(S["KF"], S["R"], 1.0 / TWO_PI, None, ALU.mult)
            B.cp(KI, S["KF"], eng="dve")
            B.cp(S["KF"], KI, eng="dve")
            B.stt(S["R"], S["KF"], -TWO_PI, S["R"], ALU.mult, ALU.add)
            B.ts(S["M"], S["R"], -math.pi, TWO_PI, ALU.is_lt, ALU.mult)
            B.tt(S["R"], S["R"], S["M"], ALU.add)
            B.ts(S["M"], S["R"], math.pi, -TWO_PI, ALU.is_gt, ALU.mult)
            B.tt(S["R"], S["R"], S["M"], ALU.add)
            B.ts(S["R"], S["R"], math.pi, -math.pi, ALU.min, ALU.max)
            B.act(dst, S["R"], AF.Sin)

        reduce_sin(S["SIN"], S["ANG"], 0.0)
        reduce_sin(S["COS"], S["ANG"], math.pi / 2)
        B.tt(S["AR"], S["MAG"], S["COS"], ALU.mult)
        B.tt(S["AI"], S["MAG"], S["SIN"], ALU.mult)
        B.ts(S["NR"], S["AR"], -1.0, None, ALU.add)
        B.tt(S["DEN"], S["LR"], S["LR"], ALU.mult)
        B.tt(S["t1"], S["LI"], S["LI"], ALU.mult)
        B.tt(S["DEN"], S["DEN"], S["t1"], ALU.add)
        B.recip(S["DEN"], S["DEN"])
        B.tt(S["t1"], S["NR"], S["LR"], ALU.mult)
        B.tt(S["t2"], S["AI"], S["LI"], ALU.mult)
        B.tt(S["t1"], S["t1"], S["t2"], ALU.add)
        B.tt(S["CFR"], S["t1"], S["DEN"], ALU.mult)
        B.tt(S["t1"], S["AI"], S["LR"], ALU.mult)
        B.tt(S["t2"], S["NR"], S["LI"], ALU.mult)
        B.tt(S["t1"], S["t1"], S["t2"], ALU.subtract)
        B.tt(S["CFI"], S["t1"], S["DEN"], ALU.mult)
        B.cp(S["PR"], S["AR"], eng="dve")
        B.cp(S["PI"], S["AI"], eng="dve")
        for _ in range(3):
            B.tt(S["t1"], S["PR"], S["PR"], ALU.mult)
            B.tt(S["t2"], S["PI"], S["PI"], ALU.mult)
            B.tt(S["M"], S["PR"], S["PI"], ALU.mult)
            B.tt(S["PR"], S["t1"], S["t2"], ALU.subtract)
            B.ts(S["PI"], S["M"], 2.0, None, ALU.mult)
        B.cp(ACt[:, 0, :], S["PR"], eng="dve")
        B.cp(ACt[:, 1, :], S["PR"], eng="dve")
        B.ts(ASt[:, 0, :], S["PI"], -1.0, None, ALU.mult)
        B.cp(ASt[:, 1, :], S["PI"], eng="dve")
        for part, src in enumerate((b_re, b_im)):
            sv = src.rearrange("(q t p) c -> t p q c", t=2, p=64)
            for g2 in range(2):
                for q8 in range(4):
                    B.dma(BZ[part][64 * g2:64 * g2 + 64, 8 * q8:8 * q8 + 8, :], sv[g2][:, 8 * q8:8 * q8 + 8, :])
        CST = V(f(8192), [2, 8, 128])
        for part, src in enumerate((c_re, c_im)):
            sv3 = src.rearrange("(rt p) x -> p rt x", p=128)
            B.dma(CST[:, part, :, 0:64], sv3)
            B.dma(CST[:, part, :, 64:128], sv3)
        for part, src in enumerate((c_re, c_im)):
            for rt in range(8):
                p = B.ps()
                B.tr(p[:, 0:128], CST[:, part, rt, :], IDF)
                for g2 in range(2):
                    srcv = p[64 * g2:64 * g2 + 64, 0:128].rearrange("p (a t c) -> p a t c", t=2, c=16)[:, :, g2, :]
                    B.cp(CZ[part][64 * g2:64 * g2 + 64, 4 * rt:4 * rt + 4, :], srcv, eng="dve")
        ARb = bc_last(S["AR"], 16); AIb = bc_last(S["AI"], 16)
        CRb = bc_last(S["CFR"], 16); CIb = bc_last(S["CFI"], 16)

        def cmul(dre, dim, sre_, sim_, br, bi_):
            B.tt(TA, sre_, br, ALU.mult)
            B.tt(TB, sim_, bi_, ALU.mult)
            B.tt(TA2, sre_, bi_, ALU.mult)
            B.tt(TB2, sim_, br, ALU.mult)
            B.tt(dre, TA, TB, ALU.subtract)
            B.tt(dim, TA2, TB2, ALU.add)

        def zcast(dst_zb, src_c, scale=None):
            for g2 in range(2):
                d_ = dst_zb[64 * g2:64 * g2 + 64, :, 16 * g2:16 * g2 + 16]
                s_ = src_c[64 * g2:64 * g2 + 64, :, :]
                if scale is None:
                    B.act(d_, s_, AF.Copy)
                else:
                    B.act(d_, s_, AF.Copy, scale=scale)

        for z_ in (GB[0], GB[1], BTB[0], BTB[1]):
            B.memset(z_, 0.0, eng="dve")
        cmul(FF[0][0], FF[0][1], BZ[0], BZ[1], CRb, CIb)
        zcast(BTB[0], FF[0][0])
        zcast(BTB[1], FF[0][1], scale=-1.0)
        BTP = [V(f(4096), [32, 64], BF16) for _ in range(2)]
        for i_ in range(2):
            B.memset(BTP[i_], 0.0, eng="dve")
            B.cp(BTP[i_][:, :, 32:64], BTB[i_], eng="dve")
        B.cp(GG[0][0], CZ[0], eng="act")
        B.cp(GG[0][1], CZ[1], eng="act")
        IDBq = IDB
        for m in range(T + 1):
            cur = m % 2
            nxt = (m + 1) % 2
            Gc = GG[cur]
            if m < T:
                Fc = FF[cur]
                FBc = FBm[m % 2]
                zcast(FBc[:, 0, :, :], Fc[0])
                zcast(FBc[:, 1, :, :], Fc[1])
                for part in range(2):
                    p = B.ps()
                    pb = p.bitcast(BF16)
                    for qq in range(8):
                        for r in range(4):
                            B.tr(pb[32 * r:32 * r + 32, qq * 128:(qq + 1) * 128], FBc[:, part, 4 * qq + r, :], IDB, tp=(0, 32 * r))
                    ets = ETS[(2 * m + part) % 2]
                    B.cp(ets, pb.rearrange("p (k c) -> p k c", c=128), eng="act")
                    B.dma(scrE.rearrange("p (a b c) -> p a b c", a=T, b=2)[:, T - 1 - m, part, :], ets.rearrange("p a b -> p (a b)"))
                zcast(GB[0], Gc[0])
                zcast(GB[1], Gc[1])
                for ct in range(NCT):
                    if m % 4 == 0:
                        pass
            if m >= 1:
                DTc = DTS[m % 2]
                zcast(DTc[:, :, 0, :], Gc[0])
                zcast(DTc[:, :, 1, :], Gc[1], scale=-1.0)
                B.dma(scrD.rearrange("p (a b) -> p a b", a=T)[:, m - 1, :], DTc.rearrange("p a b c -> p (a b c)"))
            if m < T:
                kcopies = []
                for cg in range(2):
                    p = B.ps()
                    for c4 in range(4):
                        ct = 4 * cg + c4
                        for r in range(4):
                            q = 4 * ct + r
                            if r < 3:
                                o = p[32 * r:32 * r + 32, 128 * c4 + 32 * r:128 * c4 + 32 * r + 32]
                                B.mm(o, BTB[0][:, q, :], GB[0][:, q, :], start=True, stop=False, tp=(0, 32 * r))
                                B.mm(o, BTB[1][:, q, :], GB[1][:, q, :], start=False, stop=True, tp=(0, 32 * r))
                            else:
                                o = p[64:128, 128 * c4 + 96:128 * c4 + 128]
                                B.mm(o, BTP[0][:, q, :], GB[0][:, q, :], start=True, stop=False, tp=(0, 64))
                                B.mm(o, BTP[1][:, q, :], GB[1][:, q, :], start=False, stop=True, tp=(0, 64))
                    kcopies.append((cg, p))
                cmul(FF[nxt][0], FF[nxt][1], FF[cur][0], FF[cur][1], ARb, AIb)
            if m < T:
                cmul(GG[nxt][0], GG[nxt][1], Gc[0], Gc[1], ARb, AIb)
                for (cg, p) in kcopies:
                    for r in range(4):
                        pr0 = 32 * r if r < 3 else 64
                        B.cp(KT[pr0:128 if r == 3 else pr0 + 32, 4 * cg:4 * cg + 4, m, 32 * r:32 * r + 32],
                             p[pr0:128 if r == 3 else pr0 + 32, :].rearrange("p (c x) -> p c x", x=128)[:, :, 32 * r:32 * r + 32], eng="dve")
        B.dma(scrK, KT.rearrange("p a b c -> p (a b c)"))

    wglv = w_glu.rearrange("(k p) n -> p k n", p=128)

    def ssm_layer(bi, blk, nt, G):
        import os
        SK = os.environ.get('SSM_SKIP', '')
        subs = subs_of(nt)
        o_ed = G.alloc(32768)
        o_k = G.alloc(16384)
        o_xs = G.alloc(NSLOT * 64 * 4)
        o_xsb = G.alloc(145 * 64 * 2)
        o_wgl = G.alloc(NCT * 2048 * 2)
        ET = V(o_ed, [T, 2, 8, 128], BF16)
        DTl = V(o_ed, [T, 32, 2, 32], BF16)
        KT = V(o_k, [NCT, T, 128], BF16)
        XS = V(o_xs, [NSLOT, 2, 32])
        XSB = V(o_xsb, [2, 32, 145], BF16)
        WGL = V(o_wgl, [NCT, 2048], BF16)
        build_tables()
        if 'S' not in SK:
            B.dma(ET.rearrange("p a b c d -> p (a b c d)"), scrE)
            B.dma(KT.rearrange("p a b c -> p (a b c)"), scrK)
        rmsnorm(nt, 1, o_xs)
        nchp = 128
        cpi = 0
        if 'E' in SK:
            B.memset(XS[:, 1:129, :, :], 0.0, eng="dve")
            B.memset(XS[:, 129:145, :, :], 0.0, eng="dve")
        NCHM = NTMAX // T
        nchk = nt // T
        HMs = [V(o_xsb, [4, T, NCHM], BF16), V(o_xsb + 4 * NTMAX * 2, [4, T, NCHM], BF16)]
        for q in (range(NPAIR) if 'E' not in SK else []):
            r = q % 4
            ct = q // 4
            HM = HMs[ct % 2]
            if r == 0:
                for r_ in range(4):
                    hsrc = H[:, ct, 0:nt].rearrange("p (n k) -> p k n", k=T)
                    if r_ != 3:
                        B.ts(HM[:, r_, :, 0:nchk], hsrc, MASK[:, r_:r_ + 1], None, ALU.mult)
                    else:
                        B.act(HM[:, r_, :, 0:nchk], hsrc, AF.Copy, scale=MASK[:, r_:r_ + 1])
            for part in range(2):
                p = B.ps()
                for kap in range(T):
                    B.mm(p[:, 0:nchk], ET[:, kap, part, ct, :], HM[:, r, kap, 0:nchk],
                         start=(kap == 0), stop=(kap == T - 1))
                eng = "dve" if cpi % 4 == 3 else "act"
                cpi += 1
                B.cp(XS[:, 1:1 + nchk, part, q], p[:, 0:nchk], eng=eng)
        o_scr = o_wgl
        if bi == 0:
            B.memset(XS[:, 0, :, :], 0.0, eng="dve")
            S16 = V(o_scr, [4096])
            if 'I' in SK:
                B.memset(XS[:, 145:161, :, :], 0.0, eng="dve")
            for part, src in (enumerate((sre, sim)) if 'I' not in SK else []):
                B.dma(S16[0:16, :], src)
                p = B.ps()
                for q in range(NPAIR):
                    B.tr(p[:, 16 * q:16 * q + 16], S16[0:16, 128 * q:128 * q + 128], IDF[0:16, 0:16])
                B.cp(XS[:, 145:161, part, :], p[:, :].rearrange("p (q s) -> p s q", s=16), eng="dve")
        else:
            B.cp(XS[:, 0, :, :], SCARRY, eng="dve")
        L1 = V(o_scr + 16384, [16, 2, 32]); L2 = V(o_scr + 16384 + 4096, [16, 2, 32])

        def swp(ap3):
            return bass.AP(ap3.tensor, ap3.offset + 32, [list(ap3.ap[0]), [-32, 2], [1, 32]])

        def bcm(t3, m):
            return bass.AP(t3.tensor, t3.offset, [list(t3.ap[0]), [0, m], [32, 2], [1, 32]])

        def bch(t3, h, m):
            return bass.AP(t3.tensor, t3.offset + 32 * h, [list(t3.ap[0]), [0, m], [1, 32]])

        import os as _os
        USE_SWAP = _os.environ.get('NO_SWAP', '') == ''

        def cmac(dst, s_full, s_h0, s_h1, c_full, s0, s1, m, ts_full=None):
            L1v = L1[:, 0:m, :, :]
            L2v = L2[:, 0:m, :, :]
            B.tt(L1v, c_full, s_full, ALU.mult)
            if USE_SWAP:
                pat = [list(x) for x in s_full.ap]
                assert pat[2] == [32, 2] and pat[3] == [1, 32], pat
                s_sw = bass.AP(s_full.tensor, s_full.offset + 32, [pat[0], pat[1], [-32, 2], [1, 32]])
                B.tt(L2v, ts_full, s_sw, ALU.mult)
            else:
                B.tt(L2v[:, :, 0, :], s0, s_h1, ALU.mult)
                B.tt(L2v[:, :, 1, :], s1, s_h0, ALU.mult)
            B.tt(L1v, L1v, L2v, ALU.add)
            B.tt(dst, dst, L1v, ALU.add)

        if 'L' not in SK:
            PC = V(o_scr, [16, 2, 32]); PS = V(o_scr + 4096, [16, 2, 32])
            B.dma(PC.rearrange("p a b c -> p (a b c)"), scrP[:, 0:1024])
            B.dma(PS.rearrange("p a b c -> p (a b c)"), scrP[:, 1024:2048])
            for t in range(1, 16):
                s = XS[:, t:t + 113:16, :, :]
                d = XS[:, t + 1:t + 114:16, :, :]
                cmac(d, s, s[:, :, 0, :], s[:, :, 1, :], bcm(ACt, 8), bch(ASt, 0, 8), bch(ASt, 1, 8), 8, ts_full=bcm(ASt, 8))
            def stepB(j):
                s = XS[:, 16 * j:16 * j + 1, :, :]
                d = XS[:, 16 * j + 16:16 * j + 17, :, :]
                cmac(d, s, s[:, :, 0, :], s[:, :, 1, :], PC[:, 15:16, :, :], PS[:, 15:16, 0, :], PS[:, 15:16, 1, :], 1, ts_full=PS[:, 15:16, :, :])

            def stepC(j):
                c3 = XS[:, 16 * j, :, :]
                d = XS[:, 16 * j + 1:16 * j + 16, :, :]
                cmac(d, bcm(c3, 15), bch(c3, 0, 15), bch(c3, 1, 15), PC[:, 0:15, :, :], PS[:, 0:15, 0, :], PS[:, 0:15, 1, :], 15, ts_full=PS[:, 0:15, :, :])
                B.cp(XSB[:, :, :, 16 * j:16 * j + 16], XS[:, 16 * j:16 * j + 16, :, :].rearrange("p n a q -> p a q n"), eng="act")

            for j in range(3):
                stepB(j)
            for j in range(4):
                stepC(j)
            for j in range(3, 8):
                stepB(j)
            for j in range(4, 8):
                stepC(j)
        if bi == 0 and 'L' not in SK:
            cur = XS[:, 145:161, :, :]
            acb = bass.AP(ACt.tensor, ACt.offset, [list(ACt.ap[0]), [0, 16], [32, 2], [1, 32]])
            asb = bass.AP(ASt.tensor, ASt.offset, [list(ASt.ap[0]), [0, 16], [32, 2], [1, 32]])
            B.tt(L1, acb, cur, ALU.mult)
            as0 = bass.AP(ASt.tensor, ASt.offset, [list(ASt.ap[0]), [0, 16], [1, 32]])
            as1 = bass.AP(ASt.tensor, ASt.offset + 32, [list(ASt.ap[0]), [0, 16], [1, 32]])
            B.tt(L2[:, :, 0, :], as0, cur[:, :, 1, :], ALU.mult)
            B.tt(L2[:, :, 1, :], as1, cur[:, :, 0, :], ALU.mult)
            B.tt(L1, L1, L2, ALU.add)
            B.tt(XS[:, 129:145, :, :], XS[:, 129:145, :, :], L1, ALU.add)
        if 'L' not in SK:
            if bi == 0:
                B.cp(XSB[:, :, :, 129:145], XS[:, 145:161, :, :].rearrange("p n a q -> p a q n"), eng="act")
        else:
            B.cp(XSB[:, :, :, 0:145], XS[:, 0:145, :, :].rearrange("p n a q -> p a q n"), eng="act")
        if 'S' not in SK:
            B.dma(DTl.rearrange("p a b c d -> p (a b c d)"), scrD)
        for s in range(4):
            B.dma(WGL[:, :, s * 512:(s + 1) * 512], wglv[:, :, s * 512:(s + 1) * 512], eng="pool")
        TA_ = V(o_xs, [512]); TB_ = V(o_xs + 2048, [512]); SIGs = [V(o_xs + 4096, [512]), V(o_xs + 6144, [512])]
        for (c0, n) in subs:
            nch = n // T
            slot0 = (c0 // T) if c0 < 1024 else 129
            for ct in range(NCT):
                p = B.ps()
                pv = p[:, 0:n].rearrange("p (c t) -> p c t", t=T)
                hv = H[:, ct, c0:c0 + n].rearrange("p (c t) -> p c t", t=T)
                for j in (range(T) if 'K' not in SK else [0]):
                    B.mm(pv[:, :, j:T], KT[:, ct, j, :], hv[:, :, 0:T - j], start=(j == 0), stop=False)
                for r in (range(4) if 'D' not in SK else []):
                    q = 4 * ct + r
                    for tau in range(T):
                        for part in range(2):
                            last = (r == 3 and tau == T - 1 and part == 1)
                            B.mm(p[32 * r:32 * r + 32, tau:n:T], DTl[:, tau, q, part, :], XSB[:, part, q, slot0:slot0 + nch],
                                 start=False, stop=last, tp=(0, 32 * r))
                TAc = TA_ if ct % 2 == 0 else TB_
                B.stt(TAc[:, 0:n], X[:, ct, c0:c0 + n], vec(1, ct), RS[:, c0:c0 + n], ALU.mult, ALU.mult)
                B.stt(TAc[:, 0:n], TAc[:, 0:n], vec(10, ct), p[:, 0:n], ALU.mult, ALU.add)
                B.act(H[:, ct, c0:c0 + n], TAc[:, 0:n], AF.Gelu_apprx_tanh)
        OUTS = V(o_xs + 8192, [16, 128])
        if bi == 0:
            B.cp(SCARRY, XS[:, 128, :, :], eng="dve")
            for part, dst in (enumerate((sssr, sssi)) if 'O' not in SK else []):
                for sg in range(4):
                    p = B.ps()
                    for s4 in range(4):
                        s = 4 * sg + s4
                        B.tr(p[0:32, 128 * s4:128 * s4 + 128], XS[:, 129 + s, part, :], IDF)
                    B.cp(OUTS[0:32, 4 * sg:4 * sg + 4, :], p[0:32, :].rearrange("p (s c) -> p s c", c=128), eng="dve")
                for s in range(NSEQ_S):
                    B.dma(dst[s:s + 1, :].rearrange("o (q c) -> (o q) c", c=128), OUTS[0:32, s, :])
        else:
            for part, dst in (enumerate((sspr, sspi)) if 'O' not in SK else []):
                p = B.ps()
                B.tr(p[0:32, 0:128], XS[:, 128, part, :], IDF)
                B.cp(OUTS[0:32, part, :], p[0:32, 0:128], eng="dve")
                B.dma(dst.rearrange("(q t) p -> q (t p)", t=2), OUTS[0:32, part, :])
        cnt = 0
        tn = tail_norm_fn("ssm", o_xs + 16384)
        for (c0, n) in subs_tok(nt):
          for ct in range(NCT):
            if True:
                pa = B.ps()
                pb = B.ps()
                for k in range(NCT):
                    B.mm(pa[:, 0:n], WGL[:, k, ct * 128:(ct + 1) * 128], H[:, k, c0:c0 + n], start=(k == 0), stop=(k == NCT - 1))
                for k in range(NCT):
                    B.mm(pb[:, 0:n], WGL[:, k, 1024 + ct * 128:1024 + (ct + 1) * 128], H[:, k, c0:c0 + n], start=(k == 0), stop=(k == NCT - 1))
                SIG = SIGs[cnt % 2]
                cnt += 1
                B.act(SIG[:, 0:n], pb[:, 0:n], AF.Sigmoid, bias=vec(12, ct), scale=1.0)
                B.stt(SIG[:, 0:n], pa[:, 0:n], vec(11, ct), SIG[:, 0:n], ALU.add, ALU.mult)
                B.tt(X[:, ct, c0:c0 + n], X[:, ct, c0:c0 + n], SIG[:, 0:n], ALU.add)
          if tn is not None:
              tn(c0, n)

    stg0 = [ARENA_BYTES - 49152 - 8192, ARENA_BYTES - 49152 - 4096]
    load_x(xp[0:1024, :], 0, 1024, stg0)
    load_x(xs, 1024, 128, stg0)
    if "ssm" in stages or "ssmsetup" in stages:
        for r_ in range(4):
            B.P.add("dve", (lambda r_: (lambda e: e.reduce_sum(MASK[:, r_:r_ + 1], IDF[:, 32 * r_:32 * r_ + 32], mybir.AxisListType.X)))(r_),
                    [IDF[:, 32 * r_:32 * r_ + 32]], [MASK[:, r_:r_ + 1]])
        ssm_setup()

    B.V = V
    blocks = [dict(p_off=0, npr=1024, samples=True, nt=1152), dict(p_off=1024, npr=1024, samples=False, nt=1024)]
    for bi, blk in enumerate(blocks):
        nt = blk["nt"]
        G = Arena(arena_h, GEN0, ARENA_BYTES)
        stg = [G.alloc(D * 4), G.alloc(D * 4)]
        if bi > 0:
            load_x(xp[blk["p_off"]:blk["p_off"] + blk["npr"], :], 0, blk["npr"], stg)
        for li, stg_name in enumerate(("conv", "ffn0", "ssm", "ffn1")):
            if stg_name not in stages:
                continue
            G = Arena(arena_h, GEN0, ARENA_BYTES)
            if stg_name.startswith("ffn"):
                ffn(int(stg_name[3]), nt, G)
            elif stg_name == "conv":
                conv_layer(bi, blk, nt, G)
            else:
                ssm_layer(bi, blk, nt, G)
        G = Arena(arena_h, GEN0, ARENA_BYTES)
        stg = [G.alloc(D * 4), G.alloc(D * 4)]
        o_sq = G.alloc(NCT * 512 * 2)
        o_yo = G.alloc(NCT * 512 * 4)
        if final_norm:
            SQ = V(o_sq, [NCT, 512], BF16)
            YO = V(o_yo, [NCT, 512])
            for (c0, n) in subs_of(nt):
                for ct in range(NCT):
                    B.act(SQ[:, ct, 0:n], X[:, ct, c0:c0 + n], AF.Square)
                p = B.ps()
                for ct in range(NCT):
                    B.mm(p[:, 0:n], ONE, SQ[:, ct, 0:n], start=(ct == 0), stop=(ct == NCT - 1))
                B.act(RS[:, c0:c0 + n], p[:, 0:n], AF.Sqrt, bias=float(D * EPS), scale=1.0)
                B.recip(RS[:, c0:c0 + n], RS[:, c0:c0 + n])
                for ct in range(NCT):
                    B.stt(YO[:, ct, 0:n], X[:, ct, c0:c0 + n], vec(4, ct), RS[:, c0:c0 + n], ALU.mult, ALU.mult)
                if c0 < 1024:
                    store_tokens(YO, 0, n, yp[blk["p_off"] + c0: blk["p_off"] + c0 + n, :], stg)
                else:
                    store_tokens(YO, 0, n, ys, stg)
    B.P.emit(nc)
    st.close()
    return nc


_NC_CACHE = {}
_BUILD_KW = {}


def kernel(**inp):
    f32 = np.float32
    n = 8
    if "nc" not in _NC_CACHE:
        _NC_CACHE["nc"] = build(**_BUILD_KW)
    nc = _NC_CACHE["nc"]
    c = lambda a: np.ascontiguousarray(np.asarray(a, dtype=f32))
    shared = {
        "ident": np.eye(128, dtype=f32),
        "norm_mix": c(inp["norm_mix"]), "norm_ffn": c(inp["norm_ffn"]), "norm_final": c(inp["norm_final"]).reshape(1, D),
        "w_pw1": c(inp["conv_w_pw1"][0]), "b_pw1": c(inp["conv_b_pw1"]).reshape(2, D), "w_dw": c(inp["conv_w_dw"][0]),
        "b_dw": c(inp["conv_b_dw"]).reshape(1, D), "ln_g": c(inp["conv_ln_g"]).reshape(1, D),
        "ln_b": c(inp["conv_ln_b"]).reshape(1, D), "w_pw2": c(inp["conv_w_pw2"][0]),
        "lam_re": c(inp["ssm_lam_re"][0]), "lam_im": c(inp["ssm_lam_im"][0]), "log_dt": c(inp["ssm_log_dt"]).reshape(NG, 1),
        "b_re": c(inp["ssm_b_re"]).reshape(NG * 64, 16), "b_im": c(inp["ssm_b_im"]).reshape(NG * 64, 16),
        "c_re": c(inp["ssm_c_re"]).reshape(NG * 16, 64), "c_im": c(inp["ssm_c_im"]).reshape(NG * 16, 64),
        "ssm_d": c(inp["ssm_d"]).reshape(1, D), "w_glu": c(inp["ssm_w_glu"][0]), "b_glu": c(inp["ssm_b_glu"]).reshape(2, D),
        "wg": c(inp["ffn_w_gate"]), "wu": c(inp["ffn_w_up"]), "wd": c(inp["ffn_w_down"]),
    }
    xpr = c(inp["x_prompt"]); xsa = c(inp["x_sample"]); ccv = c(inp["cache_conv"])
    s_re = c(inp["state_ssm_re"]); s_im = c(inp["state_ssm_im"])
    in_maps = []
    for i in range(n):
        m = dict(shared)
        m["xp"] = xpr[i]
        m["xs"] = xsa[16 * i:16 * i + 16].reshape(128, D)
        m["cc"] = ccv[0, 16 * i:16 * i + 16].reshape(480, D)
        m["sre"] = s_re[0, 16 * i:16 * i + 16].reshape(16, 4096)
        m["sim"] = s_im[0, 16 * i:16 * i + 16].reshape(16, 4096)
        in_maps.append(m)
    res = run_bass_kernel_spmd(nc, in_maps, core_ids=list(range(n)))
    R = res.results
    y_prompt = np.stack([R[i]["yp"] for i in range(n)]).astype(f32)
    y_sample = np.concatenate([R[i]["ys"].reshape(16, 8, D) for i in range(n)]).astype(f32)
    conv_p = np.stack([R[i]["convp"] for i in range(n)])[None].astype(f32)
    conv_s = np.concatenate([R[i]["convs"].reshape(16, 30, D) for i in range(n)])[None].astype(f32)
    sp_re = np.stack([R[i]["sspr"] for i in range(n)])[None].astype(f32)
    sp_im = np.stack([R[i]["sspi"] for i in range(n)])[None].astype(f32)
    ss_re = np.concatenate([R[i]["sssr"].reshape(16, 64, 64) for i in range(n)])[None].astype(f32)
    ss_im = np.concatenate([R[i]["sssi"].reshape(16, 64, 64) for i in range(n)])[None].astype(f32)
    return (y_prompt, y_sample, conv_p, conv_s, sp_re, sp_im, ss_re, ss_im)
```
